# Optimizing a Trainium2 kernel written in Bass

```python
import math
import jax, jax.numpy as jnp
from jax import lax
import numpy as np

D_MODEL = 2048
BATCH = 8
SEQ = 2048
DEPTH = 1

GRID_W = 64
CTX_LEN = 256
HEAD_DIM = 128
N_Q_HEADS = 8
N_KV_HEADS = 2
GQA_GROUP = N_Q_HEADS // N_KV_HEADS
ATTN_WIDTH = N_Q_HEADS * HEAD_DIM
KV_WIDTH = N_KV_HEADS * HEAD_DIM
Q_BLOCK = 128
ROPE_THETA = 10000.0
ROPE_AXIS_DIM = HEAD_DIM // 2
ROPE_FREQS = ROPE_AXIS_DIM // 2
F_GROUPS = 4
F_GROUP_DIM = 256
FOURIER_WIDTH = F_GROUPS * F_GROUP_DIM
MIX_WIDTH = ATTN_WIDTH + FOURIER_WIDTH
IN_PROJ_WIDTH = ATTN_WIDTH + 2 * KV_WIDTH + FOURIER_WIDTH
PEER_HEADS = 8
N_KEYS = 128
N_EXPERTS = N_KEYS * N_KEYS
D_KEY = 256
D_KEY_HALF = D_KEY // 2
TOPK_HALF = 16
TOPK = 16
PEER_BLOCK = 128
N_MOD = 6
EPS = 1e-6

kernel_name = "hymba_fnet_peer_dit_layer"


def rms_norm(x, g):
    xf = x.astype(jnp.float32)
    y = xf * lax.rsqrt(jnp.mean(xf * xf, axis=-1, keepdims=True) + EPS)
    return (y * g.astype(jnp.float32)).astype(x.dtype)


def modulate(h, shift, scale):
    return h * (1.0 + scale) + shift


def rope_tables(length):
    rows = length // GRID_W
    row = jnp.broadcast_to(jnp.arange(rows)[:, None], (rows, GRID_W)).reshape(-1)
    col = jnp.broadcast_to(jnp.arange(GRID_W)[None, :], (rows, GRID_W)).reshape(-1)
    inv_freq = ROPE_THETA ** (-jnp.arange(ROPE_FREQS, dtype=jnp.float32) / ROPE_FREQS)
    ang_row = row.astype(jnp.float32)[:, None] * inv_freq
    ang_col = col.astype(jnp.float32)[:, None] * inv_freq
    return ang_row, ang_col


def _rotate(xp, ang):
    c = jnp.cos(ang)[None, :, None, :].astype(xp.dtype)
    s = jnp.sin(ang)[None, :, None, :].astype(xp.dtype)
    x1, x2 = jnp.split(xp, 2, axis=-1)
    return jnp.concatenate([x1 * c - x2 * s, x2 * c + x1 * s], axis=-1)


def axial_rope(x, ang_row, ang_col):
    return jnp.concatenate([_rotate(x[..., :ROPE_AXIS_DIM], ang_row),
                            _rotate(x[..., ROPE_AXIS_DIM:], ang_col)], axis=-1)


def project(h, w_in, g_q, g_k):
    B, L, _ = h.shape
    p = h @ w_in
    q = p[..., :ATTN_WIDTH].reshape(B, L, N_Q_HEADS, HEAD_DIM)
    k = p[..., ATTN_WIDTH:ATTN_WIDTH + KV_WIDTH].reshape(B, L, N_KV_HEADS, HEAD_DIM)
    v = p[..., ATTN_WIDTH + KV_WIDTH:ATTN_WIDTH + 2 * KV_WIDTH].reshape(B, L, N_KV_HEADS, HEAD_DIM)
    f = p[..., ATTN_WIDTH + 2 * KV_WIDTH:]
    return rms_norm(q, g_q), rms_norm(k, g_k), v, f


def attend(q, k, v):
    B, Lq = q.shape[0], q.shape[1]
    nb = Lq // Q_BLOCK
    qb = (q * (HEAD_DIM ** -0.5)).reshape(B, nb, Q_BLOCK, N_KV_HEADS, GQA_GROUP, HEAD_DIM)
    qb = jnp.moveaxis(qb, 1, 0)

    def block(qi):
        s = jnp.einsum('bqgrd,bkgd->bgrqk', qi, k).astype(jnp.float32)
        p = jax.nn.softmax(s, axis=-1).astype(v.dtype)
        return jnp.einsum('bgrqk,bkgd->bqgrd', p, v)

    o = lax.map(block, qb)
    return jnp.moveaxis(o, 0, 1).reshape(B, Lq, ATTN_WIDTH)


def fourier_mix(f, w_f, b_f):
    B, L, _ = f.shape
    fg = f.reshape(B, L, F_GROUPS, F_GROUP_DIM).astype(jnp.float32)
    spec = jnp.fft.fft2(fg, axes=(1, 3), norm="ortho").real.astype(f.dtype)
    y = jnp.einsum('blgc,gcd->blgd', spec, w_f) + b_f.reshape(F_GROUPS, F_GROUP_DIM)
    return y.reshape(B, L, FOURIER_WIDTH)


def peer(h, w_query, sub_keys, u_exp, v_exp):
    B, L, D = h.shape
    hb = h.reshape(-1, PEER_BLOCK, D)

    def block(xb):
        q = (xb @ w_query).reshape(PEER_BLOCK, PEER_HEADS, 2, D_KEY_HALF)
        s = jnp.einsum('thpd,hpkd->thpk', q, sub_keys).astype(jnp.float32)
        s_top, i_top = lax.top_k(s, TOPK_HALF)
        cand = (s_top[:, :, 0, :, None] + s_top[:, :, 1, None, :]).reshape(
            PEER_BLOCK, PEER_HEADS, TOPK_HALF * TOPK_HALF)
        cand_idx = (i_top[:, :, 0, :, None] * N_KEYS + i_top[:, :, 1, None, :]).reshape(
            PEER_BLOCK, PEER_HEADS, TOPK_HALF * TOPK_HALF)
        s_fin, j = lax.top_k(cand, TOPK)
        idx = jnp.take_along_axis(cand_idx, j, axis=-1)
        g = jax.nn.softmax(s_fin, axis=-1)
        u_sel = u_exp[idx]
        a = jax.nn.gelu(jnp.einsum('td,thkd->thk', xb, u_sel), approximate=False)
        w = (g * a.astype(jnp.float32)).astype(xb.dtype)
        v_sel = v_exp[idx]
        return jnp.einsum('thk,thkd->td', w, v_sel)

    return lax.map(block, hb).reshape(B, L, D)


def setup_inputs(seed: int = 0) -> dict:
    key = jax.random.key(seed)
    ks = jax.random.split(key, 20)
    f32 = jnp.float32
    D = D_MODEL

    def nrm(k, shape, scale):
        return jax.random.normal(k, shape, f32) * scale

    return {
        "x": nrm(ks[0], (BATCH, SEQ, D), 1.0),
        "c": nrm(ks[1], (BATCH, D), 1.0),
        "ctx": nrm(ks[2], (BATCH, CTX_LEN, D), 1.0),
        "c_ctx": nrm(ks[3], (D,), 1.0),
        "w_ada": nrm(ks[4], (DEPTH, D, N_MOD * D), 0.5 * D ** -0.5),
        "b_ada": nrm(ks[5], (DEPTH, N_MOD * D), 0.02),
        "g_norm1": 1.0 + nrm(ks[6], (DEPTH, D), 0.02),
        "w_in": nrm(ks[7], (DEPTH, D, IN_PROJ_WIDTH), D ** -0.5),
        "g_q": 1.0 + nrm(ks[8], (DEPTH, HEAD_DIM), 0.02),
        "g_k": 1.0 + nrm(ks[9], (DEPTH, HEAD_DIM), 0.02),
        "w_fourier": nrm(ks[10], (DEPTH, F_GROUPS, F_GROUP_DIM, F_GROUP_DIM), F_GROUP_DIM ** -0.5),
        "b_fourier": nrm(ks[11], (DEPTH, FOURIER_WIDTH), 0.02),
        "w_out": nrm(ks[12], (DEPTH, MIX_WIDTH, D), MIX_WIDTH ** -0.5),
        "g_norm2": 1.0 + nrm(ks[13], (DEPTH, D), 0.02),
        "w_query": nrm(ks[14], (DEPTH, D, PEER_HEADS * D_KEY), D ** -0.5),
        "sub_keys": nrm(ks[15], (DEPTH, PEER_HEADS, 2, N_KEYS, D_KEY_HALF), D_KEY_HALF ** -0.5),
        "u_experts": nrm(ks[16], (DEPTH, N_EXPERTS, D), D ** -0.5),
        "v_experts": nrm(ks[17], (DEPTH, N_EXPERTS, D), 0.5),
        "g_final": 1.0 + nrm(ks[18], (D,), 0.02),
    }


def reference(x, c, ctx, c_ctx, w_ada, b_ada, g_norm1, w_in, g_q, g_k, w_fourier, b_fourier,
              w_out, g_norm2, w_query, sub_keys, u_experts, v_experts, g_final):
    L = x.shape[1]
    ang_row, ang_col = rope_tables(L)
    xl, xc = x, ctx
    for layer in range(DEPTH):
        mod_l = jax.nn.silu(c) @ w_ada[layer] + b_ada[layer]
        mod_c = jax.nn.silu(c_ctx) @ w_ada[layer] + b_ada[layer]
        sh1, sc1, gt1, sh2, sc2, gt2 = [m[:, None, :] for m in jnp.split(mod_l, N_MOD, axis=-1)]
        csh1, csc1, cgt1, csh2, csc2, cgt2 = jnp.split(mod_c, N_MOD, axis=-1)

        hl = modulate(rms_norm(xl, g_norm1[layer]), sh1, sc1)
        hc = modulate(rms_norm(xc, g_norm1[layer]), csh1, csc1)
        ql, kl, vl, fl = project(hl, w_in[layer], g_q[layer], g_k[layer])
        qc, kc, vc, fc = project(hc, w_in[layer], g_q[layer], g_k[layer])
        ql = axial_rope(ql, ang_row, ang_col)
        kl = axial_rope(kl, ang_row, ang_col)
        k_all = jnp.concatenate([kl, kc], axis=1)
        v_all = jnp.concatenate([vl, vc], axis=1)
        attn_l = attend(ql, k_all, v_all)
        four_l = fourier_mix(fl, w_fourier[layer], b_fourier[layer])
        xl = xl + gt1 * (jnp.concatenate([attn_l, four_l], axis=-1) @ w_out[layer])

        h2 = modulate(rms_norm(xl, g_norm2[layer]), sh2, sc2)
        xl = xl + gt2 * peer(h2, w_query[layer], sub_keys[layer], u_experts[layer], v_experts[layer])

        if layer < DEPTH - 1:
            attn_c = attend(qc, kc, vc)
            four_c = fourier_mix(fc, w_fourier[layer], b_fourier[layer])
            xc = xc + cgt1 * (jnp.concatenate([attn_c, four_c], axis=-1) @ w_out[layer])
            h2c = modulate(rms_norm(xc, g_norm2[layer]), csh2, csc2)
            xc = xc + cgt2 * peer(h2c, w_query[layer], sub_keys[layer], u_experts[layer], v_experts[layer])
    return rms_norm(xl, g_final)
```

```python
import math
import numpy as np
import ml_dtypes
from contextlib import ExitStack
import concourse.bass as bass
import concourse.mybir as mybir
from concourse.bass_utils import run_bass_kernel_spmd

F32 = mybir.dt.float32
BF16 = mybir.dt.bfloat16
AF = mybir.ActivationFunctionType
ALU = mybir.AluOpType
AX = mybir.AxisListType

D = 2048
TL = 2048
TC = 256
TA = TL + TC
EPS = 1e-6
NEG = -1e30


class Buf:
    __slots__ = ("name", "w", "r")

    def __init__(self, name=""):
        self.name = name
        self.w = None
        self.r = []


class Tracker:
    COMPUTE = ("pe", "act", "dve", "pool")

    def __init__(self, nc, es, n_dma_sems=10, same_engine_sync=True):
        self.nc = nc
        self.semobj = {}
        self.count = {}
        for e in self.COMPUTE:
            self.semobj["c_" + e] = es.enter_context(nc.semaphore("s_" + e))
            self.count["c_" + e] = 0
        self.dma_pool = {}
        for q in ("sp", "act", "pool"):
            lst = []
            for i in range(n_dma_sems):
                nm = "d_%s_%d" % (q, i)
                self.semobj[nm] = es.enter_context(nc.semaphore(nm))
                self.count[nm] = 0
                lst.append(nm)
            self.dma_pool[q] = [lst, 0]
        self.prog = {e: [] for e in ("pe", "act", "dve", "pool", "sp")}
        self.seen = {e: {} for e in self.prog}
        self.pending = {e: {} for e in self.prog}
        self.same_engine_sync = same_engine_sync
        self.final_tokens = []
        self.n_ops = 0
        self.n_waits = 0

    def _need(self, eng, tok, waits, kind):
        if tok is None:
            return
        sid, val, teng = tok
        if teng == eng and teng in self.COMPUTE:
            if eng == "pe" or not self.same_engine_sync:
                return
        if self.seen[eng].get(sid, 0) >= val:
            return
        if waits.get(sid, 0) < val:
            waits[sid] = val

    def barrier(self):
        for e in self.prog:
            for sid, c in self.count.items():
                if c > 0 and self.seen[e].get(sid, 0) < c:
                    if sid == "c_" + e:
                        continue
                    if self.pending[e].get(sid, 0) < c:
                        self.pending[e][sid] = c

    def op(self, eng, fn, reads=(), writes=(), dma=False, final=False):
        waits = {}
        if self.pending[eng]:
            for sid, v in self.pending[eng].items():
                if self.seen[eng].get(sid, 0) < v:
                    waits[sid] = v
            self.pending[eng] = {}
        for b in reads:
            self._need(eng, b.w, waits, "raw")
        for b in writes:
            self._need(eng, b.w, waits, "waw")
            for t in b.r:
                self._need(eng, t, waits, "war")
        if dma:
            pool = self.dma_pool[eng]
            sid = pool[0][pool[1] % len(pool[0])]
            pool[1] += 1
            prev = self.count[sid]
            if prev > 0 and self.seen[eng].get(sid, 0) < prev and waits.get(sid, 0) < prev:
                waits[sid] = prev
            self.count[sid] = prev + 16
            tok = (sid, prev + 16, "dma_" + eng)
            inc = 16
        else:
            sid = "c_" + eng
            self.count[sid] += 1
            tok = (sid, self.count[sid], eng)
            inc = 1
        for s, v in waits.items():
            self.seen[eng][s] = v
        self.n_ops += 1
        self.n_waits += len(waits)
        self.prog[eng].append((list(waits.items()), fn, sid, inc))
        for b in reads:
            b.r.append(tok)
        for b in writes:
            b.w = tok
            b.r = []
        if final:
            self.final_tokens.append(tok)
        return tok

    def emit(self):
        nc = self.nc
        prog = self.prog
        semobj = self.semobj
        finals = self.final_tokens

        def run(engname, eng):
            for waits, fn, sid, inc in prog[engname]:
                for s, v in waits:
                    eng.wait_ge(semobj[s], v)
                fn(eng).then_inc(semobj[sid], inc)

        with nc.Block() as block:
            @block.tensor
            def _(eng):
                run("pe", eng)

            @block.scalar
            def _(eng):
                run("act", eng)

            @block.vector
            def _(eng):
                run("dve", eng)

            @block.gpsimd
            def _(eng):
                run("pool", eng)

            @block.sync
            def _(eng):
                run("sp", eng)
                for (s, v, _e) in finals:
                    eng.wait_ge(semobj[s], v)


C_CC = 0
C_G1 = 32
C_G2 = 48
C_BADA = 64
C_BF = 160
C_GQ = 168
C_GK = 169
NCOLS = 170


def build(debug=(), phases=9):
    nc = bass.Bass("TRN2", target_bir_lowering=False)

    def din(name, shape, dt=F32):
        return nc.dram_tensor(name, shape, dt, kind="ExternalInput")

    def dscr(name, shape, dt):
        kind = "ExternalOutput" if name in debug else "Internal"
        return nc.dram_tensor(name, shape, dt, kind=kind)

    x_d = din("x", [TL, D]).ap()
    ctx_d = din("ctx", [TC, D]).ap()
    cols_d = din("cols", [128, NCOLS]).ap()
    wada_d = din("w_ada", [D, 6 * D]).ap()
    win_d = din("w_in", [D, 2560]).ap()
    wf_d = din("w_fourier", [4, 256, 256]).ap()
    wout_d = din("w_out", [D, D]).ap()
    wq_d = din("w_query", [D, D]).ap()
    skT_d = din("skT", [16, 128, 128]).ap()
    uT_d = din("uT", [128, 128, D]).ap()
    v_d = din("v_experts", [128 * 128, D]).ap()
    gfin_d = din("g_final", [1, D])
    badar_d = din("bada_row", [2, 6 * D]).ap()
    identf_d = din("identf", [128, 128]).ap()
    rotT_d = din("rotT", [128, 128]).ap()
    iota_d = din("iotaf", [128, 128]).ap()
    cosT_d = din("cosT", [128, TL]).ap()
    sinT_d = din("sinT", [128, TL]).ap()
    ccsc_d = din("ccsc", [256, 512]).ap()
    CL_d = din("CL", [TL, TL], BF16).ap()
    nSL_d = din("nSL", [TL, TL], BF16).ap()
    y_d = nc.dram_tensor("y", [TL, D], F32, kind="ExternalOutput").ap()

    qT_s = dscr("qT_s", [8, 128, TL], BF16).ap()
    kT_s = dscr("kT_s", [2, 128, TA], BF16).ap()
    v_s = dscr("v_s", [128, 18, 256], BF16).ap()
    fT_s = dscr("fT_s", [8, 128, TL], BF16).ap()
    mixT_s = dscr("mixT_s", [16, 128, TL], BF16).ap()
    x1_s = dscr("x1_s", [TL, D], F32).ap()
    h2T_s = dscr("h2T_s", [128, 16, TL], BF16).ap()
    S_s = dscr("S_s", [TL, 8, 2, 128], F32)
    G_s = dscr("G_s", [128, 128, TL], BF16).ap()
    dbg_s = dscr("dbg_s", [128, 4096], F32).ap()

    with ExitStack() as es0:
        T = Tracker(nc, es0)

        def SB(es, name, shape, dt):
            return es.enter_context(nc.sbuf_tensor("sb_" + name, shape, dt))

        def PS(es, name, shape, dt=F32):
            return es.enter_context(nc.psum_tensor("ps_" + name, shape, dt))

        def dma(q, out, in_, reads=(), writes=(), final=False):
            return T.op(q, lambda e: e.dma_start(out=out, in_=in_), reads=reads, writes=writes, dma=True, final=final)

        cols = SB(es0, "cols", [128, NCOLS], F32); b_cols = Buf()
        identf = SB(es0, "identf", [128, 128], F32); b_id = Buf()
        onesf = SB(es0, "onesf", [128, 128], F32); b_ones = Buf()
        onesb = SB(es0, "onesb", [128, 128], BF16); b_onesb = Buf()
        onesm = SB(es0, "onesm", [128, 128], F32); b_onesm = Buf()
        epsc = SB(es0, "epsc", [128, 1], F32); b_eps = Buf()
        modT = SB(es0, "modT", [128, 96, 2], F32); b_mod = Buf()
        dcol = SB(es0, "dcol", [128, 50], F32); b_dcol = Buf()
        Grow = SB(es0, "Grow", [128, 2, D], F32); b_grow = Buf()

        dma("sp", cols[:], cols_d, writes=[b_cols])
        dma("sp", identf[:], identf_d, writes=[b_id])
        T.op("pool", lambda e: e.memset(onesf[:], 1.0), writes=[b_ones])
        T.op("pool", lambda e: e.memset(onesb[:], 1.0), writes=[b_onesb])
        T.op("pool", lambda e: e.memset(onesm[:], 1.0 / 128.0), writes=[b_onesm])
        T.op("pool", lambda e: e.memset(epsc[:], EPS), writes=[b_eps])

        if phases <= 0:
            dma("sp", y_d[0:128, 0:128], identf[:], reads=[b_id], final=True)
            T.emit()
            return nc
        with ExitStack() as es:
            silu_t = SB(es, "silu_t", [128, 32], F32); b_silu = Buf()
            wa = [SB(es, "wa%d" % i, [128, 16, 512], F32) for i in range(2)]
            b_wa = [Buf(), Buf()]
            modrow = SB(es, "modrow", [2, 6 * D], F32); b_mrow = Buf()
            badar = SB(es, "badar", [2, 6 * D], F32); b_badar = Buf()
            PMr = [PS(es, "PMr%d" % i, [2, 512]) for i in range(2)]; b_pmr = [Buf(), Buf()]
            PM = PS(es, "PM", [128, 96, 2]); b_pm = Buf()
            PG = [PS(es, "PGa%d" % i, [128, 512]) for i in range(2)]
            b_pg = [Buf(), Buf()]
            T.op("act", lambda e: e.activation(out=silu_t[:], in_=cols[:, C_CC:C_CC + 32], func=AF.Silu),
                 reads=[b_cols], writes=[b_silu])
            dma("act", badar[:], badar_d, writes=[b_badar])
            wav = wada_d.rearrange("(kc p) n -> p kc n", p=128)
            for nb in range(24):
                w_t = wa[nb % 2]; bw = b_wa[nb % 2]
                pm = PMr[nb % 2]; bpm = b_pmr[nb % 2]
                for hf in range(2):
                    dma("sp" if hf == 0 else "act", w_t[:, hf * 8:(hf + 1) * 8, :],
                        wav[:, hf * 8:(hf + 1) * 8, nb * 512:(nb + 1) * 512], writes=[bw])
                for kc in range(16):
                    T.op("pe", lambda e, w_t=w_t, pm=pm, kc=kc: e.matmul(
                        pm[:], lhsT=silu_t[:, 2 * kc:2 * kc + 2], rhs=w_t[:, kc, :],
                        start=(kc == 0), stop=(kc == 15)), reads=[bw, b_silu], writes=[bpm])
                T.op("dve", lambda e, pm=pm, nb=nb: e.tensor_tensor(
                    out=modrow[:, nb * 512:(nb + 1) * 512], in0=pm[:], in1=badar[:, nb * 512:(nb + 1) * 512], op=ALU.add),
                    reads=[bpm, b_badar], writes=[b_mrow])
            for jg in range(96):
                T.op("pe", lambda e, jg=jg: e.transpose(out=PM[:, jg, :], in_=modrow[0:2, jg * 128:(jg + 1) * 128],
                     identity=identf[0:2, 0:2]), reads=[b_mrow, b_id], writes=[b_pm])
            T.op("dve", lambda e: e.tensor_copy(out=modT[:], in_=PM[:]), reads=[b_pm], writes=[b_mod])
            T.op("dve", lambda e: e.scalar_tensor_tensor(out=dcol[:, 0:16], in0=modT[:, 16:32, 0], scalar=1.0,
                 in1=cols[:, C_G1:C_G1 + 16], op0=ALU.add, op1=ALU.mult), reads=[b_mod, b_cols], writes=[b_dcol])
            T.op("dve", lambda e: e.scalar_tensor_tensor(out=dcol[:, 16:32], in0=modT[:, 16:32, 1], scalar=1.0,
                 in1=cols[:, C_G1:C_G1 + 16], op0=ALU.add, op1=ALU.mult), reads=[b_mod, b_cols], writes=[b_dcol])
            T.op("dve", lambda e: e.scalar_tensor_tensor(out=dcol[:, 32:48], in0=modT[:, 64:80, 0], scalar=1.0,
                 in1=cols[:, C_G2:C_G2 + 16], op0=ALU.add, op1=ALU.mult), reads=[b_mod, b_cols], writes=[b_dcol])
            T.op("dve", lambda e: e.tensor_scalar(out=dcol[:, 48:49], in0=cols[:, C_GQ:C_GQ + 1],
                 scalar1=float(128 ** -0.5), scalar2=None, op0=ALU.mult), reads=[b_cols], writes=[b_dcol])
            T.op("dve", lambda e: e.tensor_copy(out=dcol[:, 49:50], in_=cols[:, C_GK:C_GK + 1]),
                 reads=[b_cols], writes=[b_dcol])
            n = 0
            for gi, base in enumerate((32, 80)):
                for nb in range(4):
                    pg = PG[n % 2]; bpg = b_pg[n % 2]; n += 1
                    c0 = base * 128 + nb * 512
                    T.op("pe", lambda e, pg=pg, c0=c0: e.matmul(
                        pg[:], lhsT=onesf[0:1, :], rhs=modrow[0:1, c0:c0 + 512], start=True, stop=True),
                        reads=[b_ones, b_mrow], writes=[bpg])
                    T.op("act", lambda e, pg=pg, gi=gi, nb=nb: e.activation(
                        out=Grow[:, gi, nb * 512:(nb + 1) * 512], in_=pg[:], func=AF.Copy),
                        reads=[bpg], writes=[b_grow])
        T.barrier()
        if phases <= 1:
            dma("sp", dbg_s[:, 0:192], modT[:].rearrange("p a b -> p (a b)"), reads=[b_mod], final=True)
            dma("sp", dbg_s[:, 192:242], dcol[:], reads=[b_dcol], final=True)
            dma("sp", y_d[0:128, :], Grow[:, 0, :], reads=[b_grow], final=True)
            dma("sp", y_d[128:256, :], Grow[:, 1, :], reads=[b_grow], final=True)
            T.emit()
            return nc

        class WStream:
            def __init__(self, es, pref, wview, cast_eng="pool"):
                self.cast_eng = cast_eng
                self.st = [SB(es, pref + "wst%d" % i, [128, 16, 128], F32) for i in range(2)]
                self.b_st = [Buf(), Buf()]
                self.wb = [SB(es, pref + "wcb%d" % i, [128, 16, 128], BF16) for i in range(2)]
                self.b_wb = [Buf(), Buf()]
                self.n = 0
                self.wview = wview

            def get(self, col0):
                j = self.n % 2; self.n += 1
                st = self.st[j]; bst = self.b_st[j]; wb = self.wb[j]; bwb = self.b_wb[j]
                dma("sp", st[:], self.wview[:, :, col0:col0 + 128], writes=[bst])
                if self.cast_eng == "act":
                    T.op("act", lambda e: e.activation(out=wb[:], in_=st[:], func=AF.Copy), reads=[bst], writes=[bwb])
                else:
                    T.op("pool", lambda e: e.tensor_copy(out=wb[:], in_=st[:]), reads=[bst], writes=[bwb])
                return wb, bwb

        def norm_mod_T(es, pref, ntiles, load_fn, scol_fn, shcol_fn, hT_fn, tok0_fn, after_fn=None):
            xt = [SB(es, pref + "xt%d" % i, [128, D], F32) for i in range(2)]; b_xt = [Buf(), Buf()]
            xn = [SB(es, pref + "xn%d" % i, [128, D], F32) for i in range(2)]; b_xn = [Buf(), Buf()]
            junk = SB(es, pref + "junk", [128, D], BF16); b_junk = Buf()
            st = [SB(es, pref + "st%d" % i, [128, 4], F32) for i in range(2)]; b_st = [Buf(), Buf()]
            PT = [PS(es, pref + "PT%d" % i, [128, 512]) for i in range(2)]; b_pt = [Buf(), Buf()]
            k = 0
            load_fn(0, xt[0], b_xt[0], xn[0], b_xn[0])
            for i in range(ntiles):
                x_t = xt[i % 2]; bx = b_xt[i % 2]; xn_t = xn[i % 2]; bxn = b_xn[i % 2]
                s_t = st[i % 2]; bs = b_st[i % 2]
                T.op("act", lambda e, x_t=x_t, s_t=s_t: e.activation(out=junk[:], in_=x_t[:], func=AF.Square,
                     accum_out=s_t[:, 0:1]), reads=[bx], writes=[b_junk, bs])
                T.op("act", lambda e, s_t=s_t: e.activation(out=s_t[:, 1:2], in_=s_t[:, 0:1], func=AF.Sqrt,
                     scale=1.0 / D, bias=epsc[:, 0:1]), reads=[bs, b_eps], writes=[bs])
                T.op("dve", lambda e, s_t=s_t: e.reciprocal(out=s_t[:, 2:3], in_=s_t[:, 1:2]), reads=[bs], writes=[bs])
                T.op("act", lambda e, x_t=x_t, xn_t=xn_t, s_t=s_t: e.activation(out=xn_t[:], in_=x_t[:], func=AF.Copy,
                     scale=s_t[:, 2:3]), reads=[bx, bs], writes=[bxn])
                if i + 1 < ntiles:
                    load_fn(i + 1, xt[(i + 1) % 2], b_xt[(i + 1) % 2], xn[(i + 1) % 2], b_xn[(i + 1) % 2])
                t0 = tok0_fn(i)
                hT, b_hT = hT_fn(i)
                for g4 in range(4):
                    p_t = PT[k % 2]; bp = b_pt[k % 2]; k += 1
                    for q in range(4):
                        kc = g4 * 4 + q
                        T.op("pe", lambda e, p_t=p_t, q=q, kc=kc, xn_t=xn_t: e.transpose(
                            out=p_t[:, q * 128:(q + 1) * 128], in_=xn_t[:, kc * 128:(kc + 1) * 128], identity=identf[:]),
                            reads=[bxn, b_id], writes=[bp])
                    for q in range(4):
                        kc = g4 * 4 + q
                        T.op("dve", lambda e, p_t=p_t, q=q, kc=kc, t0=t0, i=i, hT=hT: e.tensor_scalar(
                            out=hT[:, kc, t0:t0 + 128], in0=p_t[:, q * 128:(q + 1) * 128],
                            scalar1=scol_fn(i, kc), scalar2=shcol_fn(i, kc), op0=ALU.mult, op1=ALU.add),
                            reads=[bp, b_dcol, b_mod], writes=[b_hT])
                if after_fn is not None:
                    after_fn(i, x_t, bx, xn_t, bxn)

        with ExitStack() as es:
            hT = SB(es, "hT", [128, 16, TA], BF16); b_hT = Buf()
            with ExitStack() as es2:
                def load1(i, x_t, bx, xn_t, bxn):
                    if i < 16:
                        dma("sp", x_t[:], x_d[i * 128:(i + 1) * 128, :], writes=[bx])
                    else:
                        dma("sp", x_t[:], ctx_d[(i - 16) * 128:(i - 15) * 128, :], writes=[bx])
                norm_mod_T(es2, "b", 18, load1,
                           lambda i, kc: dcol[:, kc:kc + 1] if i < 16 else dcol[:, 16 + kc:17 + kc],
                           lambda i, kc: modT[:, kc, 0:1] if i < 16 else modT[:, kc, 1:2],
                           lambda i: (hT, b_hT), lambda i: i * 128)
            T.barrier()
            winv = win_d.rearrange("(kc p) n -> p kc n", p=128)
            WS = WStream(es, "c", winv, cast_eng="act")
            cosT = SB(es, "cosT", [128, TL], F32); b_cos = Buf()
            sinT = SB(es, "sinT", [128, TL], F32); b_sin = Buf()
            rotT = SB(es, "rotT", [128, 128], F32); b_rot = Buf()
            dma("sp", cosT[:], cosT_d, writes=[b_cos])
            dma("sp", sinT[:], sinT_d, writes=[b_sin])
            dma("sp", rotT[:], rotT_d, writes=[b_rot])
            NS = 3
            PQ = [PS(es, "PQ%d" % i, [128, 512]) for i in range(NS)]; b_pq = [Buf() for _ in range(NS)]
            PSSl = [PS(es, "PSS%d" % i, [128, 512]) for i in range(2)]; b_pssl = [Buf(), Buf()]
            PRl = [PS(es, "PR%d" % i, [128, 512]) for i in range(2)]; b_prl = [Buf(), Buf()]
            qg = [SB(es, "qg%d" % i, [128, 512], F32) for i in range(NS)]; b_qg = [Buf() for _ in range(NS)]
            sq = [SB(es, "sq%d" % i, [128, 512], F32) for i in range(NS)]; b_sq = [Buf() for _ in range(NS)]
            rs = [SB(es, "rs%d" % i, [128, 512], F32) for i in range(NS)]; b_rs = [Buf() for _ in range(NS)]
            t1 = [SB(es, "t1%d" % i, [128, 512], F32) for i in range(NS)]; b_t1 = [Buf() for _ in range(NS)]
            t2 = [SB(es, "t2%d" % i, [128, 512], F32) for i in range(NS)]; b_t2 = [Buf() for _ in range(NS)]
            stage = [SB(es, "stage%d" % i, [128, TA], BF16) for i in range(2)]; b_stage = [Buf(), Buf()]
            vsb = SB(es, "vsb", [128, 18, 256], BF16); b_vsb = Buf()
            nblk = 0
            pending = [None]

            def make_chain(j, n, t0, rope, stg, bstg, fc, last, nb_):
                PSS = PSSl[nb_ % 2]; b_pss = b_pssl[nb_ % 2]; PR = PRl[nb_ % 2]; b_pr = b_prl[nb_ % 2]

                def chain():
                    T.op("pe", lambda e: e.matmul(PSS[:, 0:n], lhsT=onesm[:], rhs=sq[j][:, 0:n],
                         start=True, stop=True), reads=[b_onesm, b_sq[j]], writes=[b_pss])
                    if rope:
                        T.op("pe", lambda e: e.matmul(PR[:, 0:n], lhsT=rotT[:], rhs=qg[j][:, 0:n],
                             start=True, stop=True), reads=[b_rot, b_qg[j]], writes=[b_pr])
                    T.op("act", lambda e: e.activation(out=rs[j][:, 0:n], in_=PSS[:, 0:n], func=AF.Ln,
                         bias=epsc[:, 0:1]), reads=[b_pss, b_eps], writes=[b_rs[j]])
                    T.op("act", lambda e: e.activation(out=rs[j][:, 0:n], in_=rs[j][:, 0:n], func=AF.Exp, scale=-0.5),
                         reads=[b_rs[j]], writes=[b_rs[j]])
                    if rope:
                        T.op("pool", lambda e: e.tensor_tensor(out=t1[j][:, 0:n], in0=qg[j][:, 0:n],
                             in1=cosT[:, t0:t0 + n], op=ALU.mult), reads=[b_qg[j], b_cos], writes=[b_t1[j]])
                        T.op("dve", lambda e: e.tensor_tensor(out=t2[j][:, 0:n], in0=PR[:, 0:n],
                             in1=sinT[:, t0:t0 + n], op=ALU.mult), reads=[b_pr, b_sin], writes=[b_t2[j]])
                        T.op("pool", lambda e: e.tensor_tensor(out=t1[j][:, 0:n], in0=t1[j][:, 0:n],
                             in1=t2[j][:, 0:n], op=ALU.add), reads=[b_t1[j], b_t2[j]], writes=[b_t1[j]])
                        T.op("dve", lambda e: e.tensor_tensor(out=stg[:, t0:t0 + n],
                             in0=t1[j][:, 0:n], in1=rs[j][:, 0:n], op=ALU.mult),
                             reads=[b_t1[j], b_rs[j]], writes=[bstg])
                    else:
                        T.op("dve", lambda e: e.tensor_tensor(out=stg[:, t0:t0 + n],
                             in0=qg[j][:, 0:n], in1=rs[j][:, 0:n], op=ALU.mult),
                             reads=[b_qg[j], b_rs[j]], writes=[bstg])
                    if last:
                        if fc < 8:
                            dma("sp", qT_s[fc], stg[:, 0:TL], reads=[bstg])
                        else:
                            dma("sp", kT_s[fc - 8], stg[:, :], reads=[bstg])
                return chain

            for fc in range(10):
                stg = stage[fc % 2]; bstg = b_stage[fc % 2]
                gcol = dcol[:, 48:49] if fc < 8 else dcol[:, 49:50]
                blocks = [(i * 512, 512, True) for i in range(4)]
                if fc >= 8:
                    blocks.append((TL, 256, False))
                win, b_win = WS.get(fc * 128)
                for bi_, (t0, n, rope) in enumerate(blocks):
                    j = nblk % NS; nb_ = nblk; nblk += 1
                    pq = PQ[j]; bpq = b_pq[j]
                    for kc in range(16):
                        T.op("pe", lambda e, pq=pq, kc=kc, win=win, t0=t0, n=n: e.matmul(
                            pq[:, 0:n], lhsT=win[:, kc, :], rhs=hT[:, kc, t0:t0 + n],
                            start=(kc == 0), stop=(kc == 15)), reads=[b_win, b_hT], writes=[bpq])
                    if pending[0] is not None:
                        pending[0]()
                    T.op("act", lambda e, j=j, pq=pq, n=n, gcol=gcol: e.activation(
                        out=qg[j][:, 0:n], in_=pq[:, 0:n], func=AF.Copy, scale=gcol),
                        reads=[bpq, b_dcol], writes=[b_qg[j]])
                    T.op("act", lambda e, j=j, pq=pq, n=n: e.activation(
                        out=sq[j][:, 0:n], in_=pq[:, 0:n], func=AF.Square), reads=[bpq], writes=[b_sq[j]])
                    pending[0] = make_chain(j, n, t0, rope, stg, bstg, fc, bi_ == len(blocks) - 1, nb_)
            pending[0]()
            nv = 0
            for vc in range(2):
                win, b_win = WS.get(1280 + vc * 128)
                for tile in range(18):
                    pv = PQ[nv % NS]; bpv = b_pq[nv % NS]; nv += 1
                    for kc in range(16):
                        T.op("pe", lambda e, pv=pv, kc=kc, tile=tile, win=win: e.matmul(
                            pv[:, 0:128], lhsT=hT[:, kc, tile * 128:(tile + 1) * 128], rhs=win[:, kc, :],
                            start=(kc == 0), stop=(kc == 15)), reads=[b_hT, b_win], writes=[bpv])
                    T.op("act", lambda e, pv=pv, tile=tile, vc=vc: e.activation(
                        out=vsb[:, tile, vc * 128:(vc + 1) * 128], in_=pv[:, 0:128], func=AF.Copy),
                        reads=[bpv], writes=[b_vsb])
            dma("sp", v_s, vsb[:], reads=[b_vsb])
            for fc in range(8):
                stg = stage[fc % 2]; bstg = b_stage[fc % 2]
                win, b_win = WS.get(1536 + fc * 128)
                for blk in range(4):
                    j = nblk % NS; nblk += 1
                    pq = PQ[j]; bpq = b_pq[j]
                    for kc in range(16):
                        T.op("pe", lambda e, pq=pq, kc=kc, win=win, blk=blk: e.matmul(
                            pq[:], lhsT=win[:, kc, :],
                            rhs=hT[:, kc, blk * 512:(blk + 1) * 512], start=(kc == 0), stop=(kc == 15)),
                            reads=[b_win, b_hT], writes=[bpq])
                    T.op("act", lambda e, pq=pq, blk=blk, stg=stg: e.activation(
                        out=stg[:, blk * 512:(blk + 1) * 512], in_=pq[:], func=AF.Copy), reads=[bpq], writes=[bstg])
                dma("sp", fT_s[fc], stg[:, 0:TL], reads=[bstg])
        T.barrier()
        if phases <= 2:
            dma("sp", y_d[0:128, :], Grow[:, 0, :], reads=[b_grow], final=True)
            T.emit()
            return nc

        b_qTs = Buf(); b_kTs = Buf(); b_vs = Buf(); b_fTs = Buf(); b_mixs = Buf()
        with ExitStack() as es:
            kT = SB(es, "kT", [128, TA], BF16); b_kT = Buf()
            vg = SB(es, "vg", [128, 18, 128], BF16); b_vg = Buf()
            qTh = [SB(es, "qTh%d" % i, [128, TL], BF16) for i in range(2)]; b_qTh = [Buf(), Buf()]
            NR = 4
            LA = 2
            pT = [SB(es, "pT%d" % i, [128, 512], BF16) for i in range(NR)]; b_pT = [Buf() for _ in range(NR)]
            ost = [SB(es, "ost%d" % i, [128, TL], BF16) for i in range(2)]; b_ost = [Buf(), Buf()]
            rsum = SB(es, "rsum", [128, 512], F32); b_rsum = Buf()
            PSt = [PS(es, "PSt%d" % i, [128, 512]) for i in range(NR)]; b_pst = [Buf() for _ in range(NR)]
            PO = [PS(es, "PO%d" % i, [128, 512]) for i in range(2)]; b_po = [Buf(), Buf()]
            PZ = [PS(es, "PZ%d" % i, [128, 512]) for i in range(2)]; b_pz = [Buf(), Buf()]
            steps = [(h // 4, h, qb, kt) for h in range(8) for qb in range(4) for kt in range(18)]
            nst = len(steps)
            for idx in range(nst + LA):
                if idx < nst:
                    g, h, qb, kt = steps[idx]
                    q_t = qTh[h % 2]; bq = b_qTh[h % 2]
                    if qb == 0 and kt == 0:
                        if h % 4 == 0:
                            dma("sp", kT[:], kT_s[g], writes=[b_kT])
                        dma("sp", q_t[:], qT_s[h], writes=[bq])
                    s_p = PSt[idx % NR]; bsp = b_pst[idx % NR]; p_t = pT[idx % NR]; bpt = b_pT[idx % NR]
                    T.op("pe", lambda e, s_p=s_p, kt=kt, q_t=q_t, qb=qb: e.matmul(
                        s_p[:], lhsT=kT[:, kt * 128:(kt + 1) * 128], rhs=q_t[:, qb * 512:(qb + 1) * 512],
                        start=True, stop=True), reads=[b_kT, bq], writes=[bsp])
                    T.op("act", lambda e, s_p=s_p, p_t=p_t: e.activation(out=p_t[:], in_=s_p[:], func=AF.Exp),
                         reads=[bsp], writes=[bpt])
                j = idx - LA
                if j >= 0:
                    g, h, qb, kt = steps[j]
                    u = h * 4 + qb
                    po = PO[u % 2]; bpo = b_po[u % 2]; pz = PZ[u % 2]; bpz = b_pz[u % 2]
                    o_t = ost[h % 2]; bo = b_ost[h % 2]
                    pp_t = pT[j % NR]; pbpt = b_pT[j % NR]
                    if h % 4 == 0 and qb == 0 and kt == 0:
                        dma("sp", vg[:], v_s[:, :, g * 128:(g + 1) * 128], writes=[b_vg])
                    T.op("pe", lambda e, po=po, kt=kt, pp_t=pp_t: e.matmul(
                        po[:], lhsT=vg[:, kt, :], rhs=pp_t[:], start=(kt == 0), stop=(kt == 17)),
                        reads=[b_vg, pbpt], writes=[bpo])
                    T.op("pe", lambda e, pz=pz, kt=kt, pp_t=pp_t: e.matmul(
                        pz[:], lhsT=onesb[:], rhs=pp_t[:], start=(kt == 0), stop=(kt == 17)),
                        reads=[b_onesb, pbpt], writes=[bpz])
                    if kt == 17:
                        T.op("dve", lambda e, pz=pz: e.reciprocal(out=rsum[:], in_=pz[:]), reads=[bpz], writes=[b_rsum])
                        T.op("dve", lambda e, po=po, o_t=o_t, qb=qb: e.tensor_tensor(
                            out=o_t[:, qb * 512:(qb + 1) * 512], in0=po[:], in1=rsum[:], op=ALU.mult),
                            reads=[bpo, b_rsum], writes=[bo])
                        if qb == 3:
                            dma("sp", mixT_s[h], o_t[:], reads=[bo])
        T.barrier()

        with ExitStack() as es:
            ccsc = SB(es, "ccsc", [128, 2, 512], F32); b_ccsc = Buf()
            wf = SB(es, "wf", [128, 4, 2, 256], F32); b_wf = Buf()
            AB = SB(es, "AB", [128, 4, 2, 512], BF16); b_AB = Buf()
            UV = SB(es, "UV", [128, 4, 16, 512], BF16); b_UV = Buf()
            fTt = [SB(es, "fTt%d" % i, [128, TL], BF16) for i in range(4)]; b_fTt = [Buf() for _ in range(4)]
            tabC = [SB(es, "tabC%d" % i, [128, 16, 512], BF16) for i in range(1)]; b_tabC = [Buf()]
            tabS = [SB(es, "tabS%d" % i, [128, 16, 512], BF16) for i in range(1)]; b_tabS = [Buf()]
            yst = [SB(es, "yst%d" % i, [128, 512], BF16) for i in range(2)]; b_yst = [Buf() for _ in range(2)]
            PA = [PS(es, "PA%d" % i, [128, 512]) for i in range(2)]; b_pa = [Buf(), Buf()]
            PU = [PS(es, "PU%d" % i, [128, 512]) for i in range(2)]; b_pu = [Buf(), Buf()]
            PY = [PS(es, "PY%d" % i, [128, 512]) for i in range(2)]; b_py = [Buf(), Buf()]
            dma("sp", ccsc[:], ccsc_d.rearrange("(c p) n -> p c n", p=128), writes=[b_ccsc])
            dma("sp", wf[:], wf_d.rearrange("g (c p) n -> p g c n", p=128), writes=[b_wf])
            na = 0
            for g in range(4):
                for mc in range(2):
                    pa = PA[na % 2]; bpa = b_pa[na % 2]; na += 1
                    for cs in range(2):
                        for kc in range(2):
                            T.op("pe", lambda e, pa=pa, cs=cs, kc=kc, mc=mc, g=g: e.matmul(
                                pa[:, cs * 256:(cs + 1) * 256],
                                lhsT=ccsc[:, kc, cs * 256 + mc * 128:cs * 256 + (mc + 1) * 128],
                                rhs=wf[:, g, kc, :], start=(kc == 0), stop=(kc == 1)),
                                reads=[b_ccsc, b_wf], writes=[bpa])
                    T.op("act", lambda e, pa=pa, g=g, mc=mc: e.activation(out=AB[:, g, mc, :], in_=pa[:], func=AF.Copy),
                         reads=[bpa], writes=[b_AB])
            nu = 0
            for g in range(4):
                for cc in range(2):
                    dma("sp", fTt[(g % 2) * 2 + cc][:], fT_s[g * 2 + cc], writes=[b_fTt[(g % 2) * 2 + cc]])
                for tt in range(16):
                    pu = PU[nu % 2]; bpu = b_pu[nu % 2]; nu += 1
                    for cc in range(2):
                        f_t = fTt[(g % 2) * 2 + cc]; bf = b_fTt[(g % 2) * 2 + cc]
                        T.op("pe", lambda e, pu=pu, f_t=f_t, tt=tt, g=g, cc=cc: e.matmul(
                            pu[:], lhsT=f_t[:, tt * 128:(tt + 1) * 128], rhs=AB[:, g, cc, :],
                            start=(cc == 0), stop=(cc == 1)), reads=[bf, b_AB], writes=[bpu])
                    T.op("act", lambda e, pu=pu, g=g, tt=tt: e.activation(out=UV[:, g, tt, :], in_=pu[:], func=AF.Copy),
                         reads=[bpu], writes=[b_UV])
            ny = 0
            CLv = CL_d.rearrange("(n p) k -> p n k", p=128)
            SLv = nSL_d.rearrange("(n p) k -> p n k", p=128)
            for kb in range(4):
                tc_t = tabC[0]; btc = b_tabC[0]; ts_t = tabS[0]; bts = b_tabS[0]
                dma("sp", tc_t[:], CLv[:, :, kb * 512:(kb + 1) * 512], writes=[btc])
                dma("sp", ts_t[:], SLv[:, :, kb * 512:(kb + 1) * 512], writes=[bts])
                for g in range(4):
                    for dc in range(2):
                        py = PY[ny % 2]; bpy = b_py[ny % 2]; ny += 1
                        for n_ in range(16):
                            T.op("pe", lambda e, py=py, g=g, n_=n_, dc=dc, tc_t=tc_t: e.matmul(
                                py[:], lhsT=UV[:, g, n_, dc * 128:(dc + 1) * 128], rhs=tc_t[:, n_, :],
                                start=(n_ == 0), stop=False), reads=[b_UV, btc], writes=[bpy])
                        for n_ in range(16):
                            T.op("pe", lambda e, py=py, g=g, n_=n_, dc=dc, ts_t=ts_t: e.matmul(
                                py[:], lhsT=UV[:, g, n_, 256 + dc * 128:256 + (dc + 1) * 128], rhs=ts_t[:, n_, :],
                                start=False, stop=(n_ == 15)), reads=[b_UV, bts], writes=[bpy])
                        ci = g * 2 + dc
                        ys = yst[ny % 2]; bys = b_yst[ny % 2]
                        T.op("act", lambda e, py=py, ci=ci, ys=ys: e.activation(
                            out=ys[:], in_=py[:], func=AF.Identity,
                            bias=cols[:, C_BF + ci:C_BF + ci + 1]), reads=[bpy, b_cols], writes=[bys])
                        dma("sp", mixT_s[8 + ci][:, kb * 512:(kb + 1) * 512], ys[:], reads=[bys])
        T.barrier()
        if phases <= 3:
            dma("sp", y_d[0:128, :], Grow[:, 0, :], reads=[b_grow], final=True)
            T.emit()
            return nc

        with ExitStack() as es:
            wout = SB(es, "wout", [128, 16, D], BF16); b_woutl = [Buf() for _ in range(4)]
            woutv = wout_d.rearrange("(kc p) n -> p kc n", p=128)
            for nb_ in range(4):
                dma("pool", wout[:, :, nb_ * 512:(nb_ + 1) * 512], woutv[:, :, nb_ * 512:(nb_ + 1) * 512], writes=[b_woutl[nb_]])
            h2blk = [SB(es, "h2blk%d" % i, [128, 16, 512], BF16) for i in range(2)]; b_h2blk = [Buf(), Buf()]
            mixb = [SB(es, "mixb%d" % i, [128, 16, 512], BF16) for i in range(2)]; b_mixb = [Buf(), Buf()]
            PO4 = PS(es, "PO4", [128, D]); b_po4 = [Buf(), Buf()]

            def loadF(i, x_t, bx, xn_t, bxn):
                blk = i // 4
                mb = mixb[blk % 2]; bmb = b_mixb[blk % 2]
                if i % 4 == 0:
                    dma("sp", mb[:], mixT_s[:, :, blk * 512:(blk + 1) * 512].rearrange("c p t -> p c t"), writes=[bmb])
                dma("act", x_t[:], x_d[i * 128:(i + 1) * 128, :], writes=[bx])
                tl = (i % 4) * 128
                for nb in range(4):
                    for kc in range(16):
                        T.op("pe", lambda e, nb=nb, kc=kc, mb=mb, tl=tl: e.matmul(
                            PO4[:, nb * 512:(nb + 1) * 512], lhsT=mb[:, kc, tl:tl + 128],
                            rhs=wout[:, kc, nb * 512:(nb + 1) * 512], start=(kc == 0), stop=(kc == 15)),
                            reads=[bmb, b_woutl[nb]], writes=[b_po4[nb // 2]])
                for hf in range(2):
                    T.op("dve", lambda e, xn_t=xn_t, hf=hf: e.tensor_tensor(out=xn_t[:, hf * 1024:(hf + 1) * 1024],
                         in0=PO4[:, hf * 1024:(hf + 1) * 1024], in1=Grow[:, 0, hf * 1024:(hf + 1) * 1024], op=ALU.mult),
                         reads=[b_po4[hf], b_grow], writes=[bxn])
                T.op("pool", lambda e, x_t=x_t, xn_t=xn_t: e.tensor_tensor(out=x_t[:], in0=x_t[:], in1=xn_t[:], op=ALU.add),
                     reads=[bx, bxn], writes=[bx])
                dma("sp", x1_s[i * 128:(i + 1) * 128, :], x_t[:], reads=[bx])

            def afterF(i, x_t, bx, xn_t, bxn):
                if i % 4 == 3:
                    blk = i // 4
                    dma("sp", h2T_s[:, :, blk * 512:(blk + 1) * 512], h2blk[blk % 2][:], reads=[b_h2blk[blk % 2]])

            norm_mod_T(es, "f", 16, loadF,
                       lambda i, kc: dcol[:, 32 + kc:33 + kc],
                       lambda i, kc: modT[:, 48 + kc, 0:1],
                       lambda i: (h2blk[(i // 4) % 2], b_h2blk[(i // 4) % 2]), lambda i: (i % 4) * 128, afterF)
        T.barrier()
        if phases <= 4:
            dma("sp", y_d[0:128, :], Grow[:, 0, :], reads=[b_grow], final=True)
            T.emit()
            return nc

        es12 = ExitStack()
        acolT_all = SB(es12, "acolT", [128, TL], F32); b_acolT = Buf()
        thrT_all = SB(es12, "thrT", [128, TL], F32); b_thrT = Buf()
        wT_all = SB(es12, "wT", [128, TL], F32); b_wTc = Buf()
        with ExitStack() as es:
            WQ = WStream(es, "q", wq_d.rearrange("(kc p) n -> p kc n", p=128), cast_eng="act")
            skT = SB(es, "skT", [128, 16, 128], BF16); b_skT = Buf()
            dma("pool", skT[:], skT_d.rearrange("c d j -> d c j"), writes=[b_skT])
            h2b = [SB(es, "h2b%d" % i, [128, 16, 512], BF16) for i in range(1)]; b_h2b = [Buf()]
            qTb2 = [SB(es, "qTb%d" % i, [128, 16, 512], BF16) for i in range(2)]; b_qTb2 = [Buf(), Buf()]
            PQ2 = [PS(es, "PQ2%d" % i, [128, 512]) for i in range(2)]; b_pq2 = [Buf(), Buf()]
            PSc = PS(es, "PSc", [128, D]); b_psc = Buf()
            PTc = PS(es, "PTc", [128, 512]); b_ptc = Buf()
            Ssb = [SB(es, "Ssb%d" % i, [128, D], F32) for i in range(2)]; b_Ssb = [Buf(), Buf()]
            Swk = SB(es, "Swk", [128, D], F32); b_Swk = Buf()
            top = SB(es, "top", [128, 16, 16], F32); b_topc = [Buf() for _ in range(16)]
            idxu = SB(es, "idxu", [128, 8, 16], mybir.dt.uint32); b_idxu = [Buf() for _ in range(8)]
            b_Swkc = [Buf() for _ in range(16)]; b_ctoph = [Buf() for _ in range(8)]; b_cand2h = [Buf() for _ in range(8)]
            cand = SB(es, "cand", [128, 8, 256], F32); b_cand = Buf()
            cand2 = SB(es, "cand2", [128, 8, 256], F32); b_cand2 = Buf()
            ctop = SB(es, "ctop", [128, 8, 16], F32); b_ctop = Buf()
            e16 = SB(es, "e16", [128, 8, 16], F32); b_e16 = Buf()
            zz = SB(es, "zz", [128, 8, 4], F32); b_zz = Buf()
            tm2 = [SB(es, "tm%d" % i, [128, 3, 128], F32) for i in range(2)]; b_tm2 = [Buf(), Buf()]
            nq2 = [0]
            S_flat = S_s.ap().rearrange("t h p j -> t (h p j)")
            hb = h2b[0]; bhb = b_h2b[0]

            def qproj(blk):
                qT_t = qTb2[blk % 2]; bqT = b_qTb2[blk % 2]
                dma("sp", hb[:], h2T_s[:, :, blk * 512:(blk + 1) * 512], writes=[bhb])
                for c in range(16):
                    pq = PQ2[nq2[0] % 2]; bpq = b_pq2[nq2[0] % 2]; nq2[0] += 1
                    wq, b_wq = WQ.get(c * 128)
                    for kc in range(16):
                        T.op("pe", lambda e, pq=pq, kc=kc, wq=wq: e.matmul(
                            pq[:], lhsT=wq[:, kc, :], rhs=hb[:, kc, :],
                            start=(kc == 0), stop=(kc == 15)), reads=[b_wq, bhb], writes=[bpq])
                    T.op("act", lambda e, pq=pq, c=c, qT_t=qT_t: e.activation(out=qT_t[:, c, :], in_=pq[:], func=AF.Copy),
                         reads=[bpq], writes=[bqT])

            pend_tr = [None]
            qproj(0)
            for blk in range(4):
                qTb = qTb2[blk % 2]; b_qTb = b_qTb2[blk % 2]
                if blk + 1 < 4:
                    qproj(blk + 1)
                for tl in range(4):
                    ti = blk * 4 + tl
                    S_t = Ssb[ti % 2]; bS = b_Ssb[ti % 2]
                    tm = tm2[ti % 2]; b_tm = b_tm2[ti % 2]
                    for c in range(16):
                        T.op("pe", lambda e, c=c, tl=tl, qTb=qTb: e.matmul(
                            PSc[:, c * 128:(c + 1) * 128], lhsT=qTb[:, c, tl * 128:(tl + 1) * 128], rhs=skT[:, c, :],
                            start=True, stop=True), reads=[b_qTb, b_skT], writes=[b_psc])
                    if pend_tr[0] is not None:
                        pend_tr[0]()
                        pend_tr[0] = None
                    T.op("act", lambda e, S_t=S_t: e.activation(out=S_t[:], in_=PSc[:], func=AF.Copy),
                         reads=[b_psc], writes=[bS])
                    dma("sp", S_flat[ti * 128:(ti + 1) * 128, :], S_t[:], reads=[bS])
                    for c in range(16):
                        sl = slice(c * 128, (c + 1) * 128)
                        T.op("dve", lambda e, c=c, sl=sl, S_t=S_t: e.max(out=top[:, c, 0:8], in_=S_t[:, sl]),
                             reads=[bS], writes=[b_topc[c]])
                    for c in range(16):
                        sl = slice(c * 128, (c + 1) * 128)
                        T.op("dve", lambda e, c=c, sl=sl, S_t=S_t: e.match_replace(
                            out=Swk[:, sl], in_to_replace=top[:, c, 0:8], in_values=S_t[:, sl], imm_value=NEG),
                            reads=[bS, b_topc[c]], writes=[b_Swkc[c]])
                    for c in range(16):
                        sl = slice(c * 128, (c + 1) * 128)
                        T.op("dve", lambda e, c=c, sl=sl: e.max(out=top[:, c, 8:16], in_=Swk[:, sl]),
                             reads=[b_Swkc[c]], writes=[b_topc[c]])
                    for h in range(8):
                        c = 2 * h
                        sl = slice(c * 128, (c + 1) * 128)
                        T.op("dve", lambda e, c=c, h=h, sl=sl, S_t=S_t: e.max_index(out=idxu[:, h, 0:8], in_max=top[:, c, 0:8],
                             in_values=S_t[:, sl]), reads=[bS, b_topc[c]], writes=[b_idxu[h]])
                        T.op("dve", lambda e, c=c, h=h, sl=sl: e.max_index(out=idxu[:, h, 8:16], in_max=top[:, c, 8:16],
                             in_values=Swk[:, sl]), reads=[b_Swkc[c], b_topc[c]], writes=[b_idxu[h]])
                    tv = top[:].rearrange("t (h p) k -> t h p k", p=2)
                    a_bc = tv[:, :, 0, :].unsqueeze(3).broadcast_to([128, 8, 16, 16])
                    b_bc = tv[:, :, 1, :].unsqueeze(2).broadcast_to([128, 8, 16, 16])
                    T.op("dve", lambda e, a_bc=a_bc, b_bc=b_bc: e.tensor_tensor(
                        out=cand[:].rearrange("t h (k l) -> t h k l", l=16), in0=a_bc, in1=b_bc, op=ALU.add),
                        reads=b_topc, writes=[b_cand])
                    for h in range(8):
                        T.op("dve", lambda e, h=h: e.max(out=ctop[:, h, 0:8], in_=cand[:, h, :]),
                             reads=[b_cand], writes=[b_ctoph[h]])
                    for h in range(8):
                        T.op("dve", lambda e, h=h: e.match_replace(out=cand2[:, h, :], in_to_replace=ctop[:, h, 0:8],
                             in_values=cand[:, h, :], imm_value=NEG), reads=[b_cand, b_ctoph[h]], writes=[b_cand2h[h]])
                    for h in range(8):
                        T.op("dve", lambda e, h=h: e.max(out=ctop[:, h, 8:16], in_=cand2[:, h, :]),
                             reads=[b_cand2h[h]], writes=[b_ctoph[h]])
                    b_top = None
                    T.op("dve", lambda e: e.tensor_tensor(out=e16[:], in0=ctop[:], in1=ctop[:, :, 0:1].broadcast_to([128, 8, 16]),
                         op=ALU.subtract), reads=b_ctoph, writes=[b_e16])
                    T.op("act", lambda e: e.activation(out=e16[:], in_=e16[:], func=AF.Exp), reads=[b_e16], writes=[b_e16])
                    T.op("dve", lambda e: e.tensor_reduce(out=zz[:, :, 1], in_=e16[:], op=ALU.add, axis=AX.X),
                         reads=[b_e16], writes=[b_zz])
                    T.op("act", lambda e: e.activation(out=zz[:, :, 2], in_=zz[:, :, 1], func=AF.Ln), reads=[b_zz], writes=[b_zz])
                    T.op("dve", lambda e: e.tensor_tensor(out=zz[:, :, 0], in0=zz[:, :, 2], in1=ctop[:, :, 0], op=ALU.add),
                         reads=[b_zz] + b_ctoph, writes=[b_zz])
                    a_v = tv[:, :, 0, :]
                    T.op("dve", lambda e, tm=tm: e.tensor_copy(out=tm[:, 0, :].rearrange("t (h k) -> t h k", k=16), in_=idxu[:]),
                         reads=b_idxu, writes=[b_tm])
                    T.op("dve", lambda e: e.tensor_tensor(out=cand2[:], in0=cand[:],
                         in1=ctop[:, :, 15:16].broadcast_to([128, 8, 256]), op=ALU.is_ge),
                         reads=[b_cand] + b_ctoph, writes=b_cand2h)
                    T.op("dve", lambda e: e.tensor_scalar(out=cand2[:], in0=cand2[:], scalar1=-1.0, scalar2=NEG,
                         op0=ALU.add, op1=ALU.mult), reads=b_cand2h, writes=b_cand2h)
                    T.op("dve", lambda e, b_bc=b_bc: e.tensor_tensor(out=cand2[:].rearrange("t h (k l) -> t h k l", l=16),
                         in0=cand2[:].rearrange("t h (k l) -> t h k l", l=16), in1=b_bc, op=ALU.add),
                         reads=b_cand2h + b_topc, writes=b_cand2h)
                    T.op("dve", lambda e, tm=tm: e.tensor_reduce(out=tm[:, 1, :].rearrange("t (h k) -> t h k", k=16),
                         in_=cand2[:].rearrange("t h (k l) -> t h k l", l=16), op=ALU.min, axis=AX.X),
                         reads=b_cand2h, writes=[b_tm])
                    T.op("dve", lambda e, a_v=a_v, tm=tm: e.tensor_tensor(out=tm[:, 2, :].rearrange("t (h k) -> t h k", k=16),
                         in0=a_v, in1=zz[:, :, 0:1].broadcast_to([128, 8, 16]), op=ALU.subtract),
                         reads=b_topc + [b_zz], writes=[b_tm])
                    T.op("act", lambda e, tm=tm: e.activation(out=tm[:, 2, :], in_=tm[:, 2, :], func=AF.Exp), reads=[b_tm], writes=[b_tm])
                    def mk_tr(ti, tm, b_tm):
                        def tr():
                            for k3 in range(3):
                                T.op("pe", lambda e, k3=k3: e.transpose(out=PTc[:, k3 * 128:(k3 + 1) * 128], in_=tm[:, k3, :],
                                     identity=identf[:]), reads=[b_tm, b_id], writes=[b_ptc])
                            T.op("act", lambda e: e.activation(out=acolT_all[:, ti * 128:(ti + 1) * 128], in_=PTc[:, 0:128],
                                 func=AF.Copy), reads=[b_ptc], writes=[b_acolT])
                            T.op("act", lambda e: e.activation(out=thrT_all[:, ti * 128:(ti + 1) * 128], in_=PTc[:, 128:256],
                                 func=AF.Copy), reads=[b_ptc], writes=[b_thrT])
                            T.op("act", lambda e: e.activation(out=wT_all[:, ti * 128:(ti + 1) * 128], in_=PTc[:, 256:384],
                                 func=AF.Copy), reads=[b_ptc], writes=[b_wTc])
                        return tr
                    pend_tr[0] = mk_tr(ti, tm, b_tm)
            pend_tr[0]()
        T.barrier()
        if phases <= 5:
            dma("sp", dbg_s[:, 0:2048], acolT_all[:], reads=[b_acolT], final=True)
            dma("sp", dbg_s[:, 2048:4096], wT_all[:], reads=[b_wTc], final=True)
            dma("sp", y_d[0:128, :], thrT_all[:], reads=[b_thrT], final=True)
            T.emit()
            es12.close()
            return nc

        TB = 16
        with ExitStack() as es:
            iot = SB(es, "iot", [128, 128], F32); b_iot = Buf()
            dma("sp", iot[:], iota_d, writes=[b_iot])
            iota3 = SB(es, "iota3", [128, 128, TB], BF16); b_iota3 = Buf()
            idxb = SB(es, "idxb", [128, TL], BF16); b_idxb = Buf()
            T.op("dve", lambda e: e.tensor_copy(out=iota3[:], in_=iot[:].unsqueeze(2).broadcast_to([128, 128, TB])),
                 reads=[b_iot], writes=[b_iota3])
            T.op("act", lambda e: e.activation(out=idxb[:], in_=acolT_all[:], func=AF.Copy), reads=[b_acolT], writes=[b_idxb])
            rep = [SB(es, "rep%d" % i, [128, TB, 128], F32) for i in range(3)]; b_rep = [Buf() for _ in range(3)]
            for i_ in range(3):
                T.op("pool", lambda e, i_=i_: e.memset(rep[i_][:], 0.0), writes=[b_rep[i_]])
            Eb = [SB(es, "Eb%d" % i, [128, TB, 128], BF16) for i in range(2)]
            b_Eb = [[Buf() for _ in range(TB)] for _ in range(2)]
            Mk = [SB(es, "Mk%d" % i, [128, TB, 128], BF16) for i in range(2)]
            b_Mk = [[Buf() for _ in range(TB)] for _ in range(2)]
            X0 = [SB(es, "X0%d" % i, [128, 128, TB], BF16) for i in range(3)]
            b_X0 = [[Buf() for _ in range(TB)] for _ in range(3)]
            Rt = [SB(es, "Rt%d" % i, [128, TB, 128], BF16) for i in range(3)]
            b_Rt = [[Buf() for _ in range(TB)] for _ in range(3)]
            Gbuf = [SB(es, "Gbuf%d" % i, [128, 128, 128], BF16) for i in range(2)]
            b_Gbuf = [[Buf() for _ in range(32)] for _ in range(2)]
            PGt = [PS(es, "PGt%d" % i, [128, 4, 128]) for i in range(8)]; b_pgt = [Buf() for _ in range(8)]
            npg = [0]
            iota_bc = iot[:].unsqueeze(1).broadcast_to([128, TB, 128])
            pend = []

            def make_back(ti, sb_, R_t, bR, x0, bx0, gb, bgb):
                def back():
                    for q4 in range(TB // 4):
                        pg = PGt[npg[0] % 8]; bpg = b_pgt[npg[0] % 8]; npg[0] += 1
                        for u in range(4):
                            tt = q4 * 4 + u
                            T.op("pe", lambda e, pg=pg, u=u, tt=tt: e.matmul(
                                pg[:, u, :], lhsT=R_t[:, tt, :], rhs=x0[:, :, tt], start=True, stop=True),
                                reads=[bR[tt], bx0[tt]], writes=[bpg])
                        tl0 = sb_ * TB + q4 * 4
                        T.op("act", lambda e, pg=pg, tl0=tl0: e.activation(
                            out=gb[:, :, tl0:tl0 + 4], in_=pg[:].rearrange("j u i -> j i u"), func=AF.Copy),
                            reads=[bpg], writes=[bgb[tl0 // 4]])
                    if sb_ == 128 // TB - 1:
                        dma("sp", G_s[:, :, ti * 128:(ti + 1) * 128].rearrange("i j t -> j i t"), gb[:], reads=bgb)
                return back

            for ti in range(16):
                gb = Gbuf[ti % 2]; bgb = b_Gbuf[ti % 2]
                for sb_ in range(128 // TB):
                    bi = ti * (128 // TB) + sb_
                    tok0 = ti * 128 + sb_ * TB
                    r_t = rep[bi % 3]; br = b_rep[bi % 3]
                    E_t = Eb[bi % 2]; bE = b_Eb[bi % 2]; x0 = X0[bi % 3]; bx0 = b_X0[bi % 3]
                    R_t = Rt[bi % 3]; bR = b_Rt[bi % 3]; m_t = Mk[bi % 2]; bM = b_Mk[bi % 2]
                    src = bass.AP(S_s, tok0 * 2048 + 128, [[256, 8], [2048, TB], [1, 128]])
                    dma("sp", r_t[0:128:16, :, :], src, writes=[br])
                    T.op("dve", lambda e, r_t=r_t: e.stream_shuffle(out=r_t[:], in_=r_t[:], mask=[0] * 16 + [16] * 16),
                         reads=[br], writes=[br])
                    T.op("act", lambda e, E_t=E_t, r_t=r_t: e.activation(out=E_t[:], in_=r_t[:], func=AF.Exp),
                         reads=[br], writes=bE)
                    idx_bc = idxb[:, tok0:tok0 + TB].unsqueeze(1).broadcast_to([128, 128, TB])
                    thr_bc = thrT_all[:, tok0:tok0 + TB].unsqueeze(2).broadcast_to([128, TB, 128])
                    T.op("dve", lambda e, x0=x0, idx_bc=idx_bc: e.tensor_tensor(out=x0[:], in0=iota3[:], in1=idx_bc,
                         op=ALU.is_equal), reads=[b_iota3, b_idxb], writes=bx0)
                    T.op("dve", lambda e, m_t=m_t, r_t=r_t, thr_bc=thr_bc: e.tensor_tensor(out=m_t[:], in0=r_t[:], in1=thr_bc,
                         op=ALU.is_ge), reads=[br, b_thrT], writes=bM)
                    w_bc = wT_all[:, tok0:tok0 + TB].unsqueeze(2).broadcast_to([128, TB, 128])
                    T.op("pool", lambda e, m_t=m_t, E_t=E_t: e.tensor_tensor(out=m_t[:], in0=m_t[:], in1=E_t[:],
                         op=ALU.mult), reads=bM + bE, writes=bM)
                    T.op("pool", lambda e, R_t=R_t, m_t=m_t, w_bc=w_bc: e.tensor_tensor(out=R_t[:], in0=m_t[:], in1=w_bc,
                         op=ALU.mult), reads=bM + [b_wTc], writes=bR)
                    pend.append(make_back(ti, sb_, R_t, bR, x0, bx0, gb, bgb))
                    if len(pend) > 2:
                        pend.pop(0)()
            while pend:
                pend.pop(0)()
        T.barrier()
        es12.close()
        if phases <= 6:
            dma("sp", y_d[0:128, :], Grow[:, 0, :], reads=[b_grow], final=True)
            T.emit()
            return nc

        NH = 2
        TH = TL // NH
        NT = TH // 128
        GE = 4
        with ExitStack() as es:
            gfr = SB(es, "gfr", [128, D], F32); b_gfr = Buf()
            dma("sp", gfr[:], gfin_d.ap().broadcast_to([128, D]), writes=[b_gfr])
            acc = SB(es, "acc", [128, NT, D], F32); b_acc = [Buf() for _ in range(NT)]
            h2h = SB(es, "h2h", [128, 16, TH], BF16); b_h2h = Buf()
            Wg = [SB(es, "Wg%d" % i, [128, GE, TH], BF16) for i in range(2)]; b_Wg = [Buf(), Buf()]
            UT = [SB(es, "UT%d" % i, [128, 16, 128], BF16) for i in range(2)]; b_UT = [Buf() for _ in range(2)]
            Vb = [SB(es, "Vb%d" % i, [128, D], BF16) for i in range(GE)]; b_Vb = [Buf() for _ in range(GE)]
            ga = [SB(es, "ga%d" % i, [128, TH], BF16) for i in range(2)]; b_ga = [Buf(), Buf()]
            x1t = [SB(es, "x1t%d" % i, [128, D], F32) for i in range(1)]; b_x1t = [Buf()]
            sth = [SB(es, "sth%d" % i, [128, 4], F32) for i in range(2)]; b_sth = [Buf(), Buf()]
            PAe = [PS(es, "PAe%d" % i, [128, TH]) for i in range(2)]; b_pae = [Buf(), Buf()]
            POe = [PS(es, "POe%d" % i, [128, 1024]) for i in range(2)]; b_poe = [Buf(), Buf()]
            nch = 0; nun = 0; neg = 0
            for half in range(NH):
                tb0 = half * TH
                dma("sp", h2h[:], h2T_s[:, :, tb0:tb0 + TH], writes=[b_h2h])
                for eg in range(128 // GE):
                    w_t = Wg[neg % 2]; bw = b_Wg[neg % 2]; neg += 1
                    dma("sp", w_t[:], G_s[eg * GE:(eg + 1) * GE, :, tb0:tb0 + TH].rearrange("i j t -> j i t"), writes=[bw])
                    for ii in range(GE):
                        i = eg * GE + ii
                        u_t = UT[nch % 2]; bu = b_UT[nch % 2]
                        v_t = Vb[ii]; bv = b_Vb[ii]
                        g_t = ga[nch % 2]; bg = b_ga[nch % 2]
                        pa = PAe[nch % 2]; bpa = b_pae[nch % 2]
                        nch += 1
                        dma("pool", u_t[:], uT_d[i].rearrange("p (kc e) -> p kc e", e=128), writes=[bu])
                        dma("pool", v_t[:], v_d[i * 128:(i + 1) * 128, :], writes=[bv])
                        for tb in range(TH // 512):
                            for kc in range(16):
                                T.op("pe", lambda e, pa=pa, tb=tb, kc=kc, u_t=u_t: e.matmul(
                                    pa[:, tb * 512:(tb + 1) * 512], lhsT=u_t[:, kc, :], rhs=h2h[:, kc, tb * 512:(tb + 1) * 512],
                                    start=(kc == 0), stop=(kc == 15)), reads=[bu, b_h2h], writes=[bpa])
                        T.op("act", lambda e, pa=pa, g_t=g_t: e.activation(out=g_t[:], in_=pa[:], func=AF.Gelu),
                             reads=[bpa], writes=[bg])
                        T.op("dve", lambda e, w_t=w_t, ii=ii, g_t=g_t: e.tensor_tensor(
                            out=w_t[:, ii, :], in0=w_t[:, ii, :], in1=g_t[:], op=ALU.mult), reads=[bw, bg], writes=[bw])
                    for tl in range(NT):
                        for dh in range(2):
                            po = POe[nun % 2]; bpo = b_poe[nun % 2]; nun += 1
                            for ii in range(GE):
                                v_t = Vb[ii]; bv = b_Vb[ii]
                                for nb in range(2):
                                    T.op("pe", lambda e, po=po, nb=nb, ii=ii, tl=tl, dh=dh, w_t=w_t, v_t=v_t: e.matmul(
                                        po[:, nb * 512:(nb + 1) * 512], lhsT=w_t[:, ii, tl * 128:(tl + 1) * 128],
                                        rhs=v_t[:, dh * 1024 + nb * 512:dh * 1024 + (nb + 1) * 512],
                                        start=(ii == 0), stop=(ii == GE - 1)), reads=[bw, bv], writes=[bpo])
                            if eg == 0:
                                T.op("act", lambda e, po=po, tl=tl, dh=dh: e.activation(
                                    out=acc[:, tl, dh * 1024:(dh + 1) * 1024], in_=po[:], func=AF.Copy),
                                    reads=[bpo], writes=[b_acc[tl]])
                            else:
                                T.op("dve", lambda e, po=po, tl=tl, dh=dh: e.tensor_tensor(
                                    out=acc[:, tl, dh * 1024:(dh + 1) * 1024], in0=po[:],
                                    in1=acc[:, tl, dh * 1024:(dh + 1) * 1024], op=ALU.add),
                                    reads=[bpo, b_acc[tl]], writes=[b_acc[tl]])
                junk = Wg[0][:, 0:2, :]; b_junk = b_Wg[0]
                for tl in range(NT):
                    ti = half * NT + tl
                    x1 = x1t[0]; bx1 = b_x1t[0]
                    s_t = sth[tl % 2]; bs = b_sth[tl % 2]
                    o_t = acc[:, tl, :]; bo = b_acc[tl]
                    dma("sp", x1[:], x1_s[ti * 128:(ti + 1) * 128, :], writes=[bx1])
                    T.op("dve", lambda e, o_t=o_t: e.tensor_tensor(out=o_t, in0=o_t, in1=Grow[:, 1, :],
                         op=ALU.mult), reads=[bo, b_grow], writes=[bo])
                    T.op("pool", lambda e, o_t=o_t, x1=x1: e.tensor_tensor(out=o_t, in0=o_t, in1=x1[:], op=ALU.add),
                         reads=[bo, bx1], writes=[bo])
                    T.op("act", lambda e, o_t=o_t, s_t=s_t: e.activation(out=junk, in_=o_t, func=AF.Square,
                         accum_out=s_t[:, 0:1]), reads=[bo], writes=[b_junk, bs])
                    T.op("act", lambda e, s_t=s_t: e.activation(out=s_t[:, 1:2], in_=s_t[:, 0:1], func=AF.Sqrt,
                         scale=1.0 / D, bias=epsc[:, 0:1]), reads=[bs, b_eps], writes=[bs])
                    T.op("dve", lambda e, s_t=s_t: e.reciprocal(out=s_t[:, 2:3], in_=s_t[:, 1:2]), reads=[bs], writes=[bs])
                    T.op("dve", lambda e, o_t=o_t, s_t=s_t: e.scalar_tensor_tensor(out=o_t, in0=o_t, scalar=s_t[:, 2:3],
                         in1=gfr[:], op0=ALU.mult, op1=ALU.mult), reads=[bo, bs, b_gfr], writes=[bo])
                    dma("sp", y_d[ti * 128:(ti + 1) * 128, :], o_t, reads=[bo], final=True)
        T.emit()
        print("tracker: ops=%d waits=%d" % (T.n_ops, T.n_waits))
    return nc


_CONSTS = None


def _consts():
    global _CONSTS
    if _CONSTS is not None:
        return _CONSTS
    identf = np.eye(128, dtype=np.float32)
    R = np.zeros((128, 128), np.float32)
    for blk in (0, 64):
        for m in range(32):
            R[blk + m, blk + m + 32] = -1.0
            R[blk + 32 + m, blk + m] = 1.0
    rotT = np.ascontiguousarray(R.T)
    inv_freq = (10000.0 ** (-np.arange(32, dtype=np.float32) / 32)).astype(np.float32)
    tpos = np.arange(TL)
    row = (tpos // 64).astype(np.float32)
    col = (tpos % 64).astype(np.float32)
    ang_row = row[:, None] * inv_freq[None, :]
    ang_col = col[:, None] * inv_freq[None, :]
    ang = np.concatenate([ang_row, ang_row, ang_col, ang_col], axis=1).astype(np.float32)
    cosT = np.ascontiguousarray(np.cos(ang).T.astype(np.float32))
    sinT = np.ascontiguousarray(np.sin(ang).T.astype(np.float32))
    cc = np.arange(256)
    ph = 2.0 * np.pi * ((cc[:, None] * cc[None, :]) % 256) / 256.0
    sc = 1.0 / math.sqrt(256.0 * TL)
    ccsc = np.concatenate([np.cos(ph) * sc, np.sin(ph) * sc], axis=1).astype(np.float32)
    n = np.arange(TL)
    phl = 2.0 * np.pi * ((n[:, None] * n[None, :]) % TL) / TL
    CL = np.cos(phl).astype(ml_dtypes.bfloat16)
    nSL = (-np.sin(phl)).astype(ml_dtypes.bfloat16)
    iotaf = np.ascontiguousarray(np.broadcast_to(np.arange(128, dtype=np.float32)[None, :], (128, 128)))
    _CONSTS = dict(iotaf=iotaf, identf=identf, rotT=rotT, cosT=cosT, sinT=sinT, ccsc=ccsc, CL=CL, nSL=nSL)
    return _CONSTS


def make_in_maps(x, c, ctx, c_ctx, w_ada, b_ada, g_norm1, w_in, g_q, g_k, w_fourier, b_fourier,
                 w_out, g_norm2, w_query, sub_keys, u_experts, v_experts, g_final):
    f = lambda a: np.ascontiguousarray(np.asarray(a, dtype=np.float32))
    cs = _consts()
    w_ada0 = f(w_ada[0]); w_in0 = f(w_in[0]); w_out0 = f(w_out[0]); wq0 = f(w_query[0]); wf0 = f(w_fourier[0])
    skT = f(np.asarray(sub_keys[0]).reshape(16, 128, 128).transpose(0, 2, 1))
    u = np.asarray(u_experts[0], dtype=np.float32)
    uT = f(u.reshape(128, 128, 16, 128).transpose(0, 3, 2, 1).reshape(128, 128, 2048))
    v0 = f(v_experts[0])
    gfin = f(np.asarray(g_final).reshape(1, D))
    bada_row = f(np.stack([np.asarray(b_ada[0]), np.asarray(b_ada[0])], axis=0))

    def colz(vec):
        return np.asarray(vec, np.float32).reshape(-1, 128).T

    in_maps = []
    for b in range(8):
        cols = np.zeros((128, NCOLS), np.float32)
        cb = colz(c[b]); cx = colz(c_ctx)
        cols[:, C_CC:C_CC + 32:2] = cb
        cols[:, C_CC + 1:C_CC + 32:2] = cx
        cols[:, C_G1:C_G1 + 16] = colz(g_norm1[0])
        cols[:, C_G2:C_G2 + 16] = colz(g_norm2[0])
        cols[:, C_BADA:C_BADA + 96] = colz(b_ada[0])
        cols[:, C_BF:C_BF + 8] = colz(b_fourier[0])
        cols[:, C_GQ] = np.asarray(g_q[0], np.float32)
        cols[:, C_GK] = np.asarray(g_k[0], np.float32)
        m = dict(x=f(x[b]), ctx=f(ctx[b]), cols=cols, w_ada=w_ada0, w_in=w_in0, w_fourier=wf0, w_out=w_out0,
                 w_query=wq0, skT=skT, uT=uT, v_experts=v0, g_final=gfin, bada_row=bada_row)
        m.update(cs)
        in_maps.append(m)
    return in_maps


def kernel(**inputs):
    in_maps = make_in_maps(**inputs)
    nc = build()
    res = run_bass_kernel_spmd(nc, in_maps, core_ids=list(range(8)))
    out = np.stack([np.asarray(r["y"], dtype=np.float32) for r in res.results], axis=0)
    return out
```

```python
import math
import numpy as np
import ml_dtypes
from contextlib import ExitStack
import concourse.bass as bass
import concourse.mybir as mybir
from concourse.bass_utils import run_bass_kernel_spmd

F32 = mybir.dt.float32
BF16 = mybir.dt.bfloat16
AF = mybir.ActivationFunctionType
ALU = mybir.AluOpType
AX = mybir.AxisListType

D = 2048
TL = 2048
TC = 256
TA = TL + TC
EPS = 1e-6
NEG = -1e30


class Buf:
    __slots__ = ("name", "w", "r")

    def __init__(self, name=""):
        self.name = name
        self.w = None
        self.r = []


class Tracker:
    COMPUTE = ("pe", "act", "dve", "pool")

    def __init__(self, nc, es, n_dma_sems=10, same_engine_sync=True):
        self.nc = nc
        self.semobj = {}
        self.count = {}
        for e in self.COMPUTE:
            self.semobj["c_" + e] = es.enter_context(nc.semaphore("s_" + e))
            self.count["c_" + e] = 0
        self.dma_pool = {}
        for q in ("sp", "act", "pool"):
            lst = []
            for i in range(n_dma_sems):
                nm = "d_%s_%d" % (q, i)
                self.semobj[nm] = es.enter_context(nc.semaphore(nm))
                self.count[nm] = 0
                lst.append(nm)
            self.dma_pool[q] = [lst, 0]
        self.prog = {e: [] for e in ("pe", "act", "dve", "pool", "sp")}
        self.seen = {e: {} for e in self.prog}
        self.pending = {e: {} for e in self.prog}
        self.same_engine_sync = same_engine_sync
        self.final_tokens = []
        self.n_ops = 0
        self.n_waits = 0

    def _need(self, eng, tok, waits, kind):
        if tok is None:
            return
        sid, val, teng = tok
        if teng == eng and teng in self.COMPUTE:
            if eng == "pe" or not self.same_engine_sync:
                return
        if self.seen[eng].get(sid, 0) >= val:
            return
        if waits.get(sid, 0) < val:
            waits[sid] = val

    def barrier(self):
        for e in self.prog:
            for sid, c in self.count.items():
                if c > 0 and self.seen[e].get(sid, 0) < c:
                    if sid == "c_" + e:
                        continue
                    if self.pending[e].get(sid, 0) < c:
                        self.pending[e][sid] = c

    def op(self, eng, fn, reads=(), writes=(), dma=False, final=False):
        waits = {}
        if self.pending[eng]:
            for sid, v in self.pending[eng].items():
                if self.seen[eng].get(sid, 0) < v:
                    waits[sid] = v
            self.pending[eng] = {}
        for b in reads:
            self._need(eng, b.w, waits, "raw")
        for b in writes:
            self._need(eng, b.w, waits, "waw")
            for t in b.r:
                self._need(eng, t, waits, "war")
        if dma:
            pool = self.dma_pool[eng]
            sid = pool[0][pool[1] % len(pool[0])]
            pool[1] += 1
            prev = self.count[sid]
            if prev > 0 and self.seen[eng].get(sid, 0) < prev and waits.get(sid, 0) < prev:
                waits[sid] = prev
            self.count[sid] = prev + 16
            tok = (sid, prev + 16, "dma_" + eng)
            inc = 16
        else:
            sid = "c_" + eng
            self.count[sid] += 1
            tok = (sid, self.count[sid], eng)
            inc = 1
        for s, v in waits.items():
            self.seen[eng][s] = v
        self.n_ops += 1
        self.n_waits += len(waits)
        self.prog[eng].append((list(waits.items()), fn, sid, inc))
        for b in reads:
            b.r.append(tok)
        for b in writes:
            b.w = tok
            b.r = []
        if final:
            self.final_tokens.append(tok)
        return tok

    def emit(self):
        nc = self.nc
        prog = self.prog
        semobj = self.semobj
        finals = self.final_tokens

        def run(engname, eng):
            for waits, fn, sid, inc in prog[engname]:
                for s, v in waits:
                    eng.wait_ge(semobj[s], v)
                fn(eng).then_inc(semobj[sid], inc)

        with nc.Block() as block:
            @block.tensor
            def _(eng):
                run("pe", eng)

            @block.scalar
            def _(eng):
                run("act", eng)

            @block.vector
            def _(eng):
                run("dve", eng)

            @block.gpsimd
            def _(eng):
                run("pool", eng)

            @block.sync
            def _(eng):
                run("sp", eng)
                for (s, v, _e) in finals:
                    eng.wait_ge(semobj[s], v)


C_CC = 0
C_G1 = 32
C_G2 = 48
C_BADA = 64
C_BF = 160
C_GQ = 168
C_GK = 169
NCOLS = 170


def build(debug=(), phases=9):
    nc = bass.Bass("TRN2", target_bir_lowering=False)

    def din(name, shape, dt=F32):
        return nc.dram_tensor(name, shape, dt, kind="ExternalInput")

    def dscr(name, shape, dt):
        kind = "ExternalOutput" if name in debug else "Internal"
        return nc.dram_tensor(name, shape, dt, kind=kind)

    x_d = din("x", [TL, D]).ap()
    ctx_d = din("ctx", [TC, D]).ap()
    cols_d = din("cols", [128, NCOLS]).ap()
    wada_d = din("w_ada", [D, 6 * D]).ap()
    win_d = din("w_in", [D, 2560]).ap()
    wf_d = din("w_fourier", [4, 256, 256]).ap()
    wout_d = din("w_out", [D, D]).ap()
    wq_d = din("w_query", [D, D]).ap()
    skT_d = din("skT", [16, 128, 128]).ap()
    uT_d = din("uT", [128, 128, D]).ap()
    v_d = din("v_experts", [128 * 128, D]).ap()
    gfin_d = din("g_final", [1, D])
    badar_d = din("bada_row", [2, 6 * D]).ap()
    identf_d = din("identf", [128, 128]).ap()
    rotT_d = din("rotT", [128, 128]).ap()
    iota_d = din("iotaf", [128, 128]).ap()
    cosT_d = din("cosT", [128, TL]).ap()
    sinT_d = din("sinT", [128, TL]).ap()
    ccsc_d = din("ccsc", [256, 512]).ap()
    CL_d = din("CL", [TL, TL], BF16).ap()
    nSL_d = din("nSL", [TL, TL], BF16).ap()
    y_d = nc.dram_tensor("y", [TL, D], F32, kind="ExternalOutput").ap()

    qT_s = dscr("qT_s", [8, 128, TL], BF16).ap()
    kT_s = dscr("kT_s", [2, 128, TA], BF16).ap()
    v_s = dscr("v_s", [128, 18, 256], BF16).ap()
    fT_s = dscr("fT_s", [8, 128, TL], BF16).ap()
    mixT_s = dscr("mixT_s", [16, 128, TL], BF16).ap()
    x1_s = dscr("x1_s", [TL, D], F32).ap()
    h2T_s = dscr("h2T_s", [128, 16, TL], BF16).ap()
    S_s = dscr("S_s", [TL, 8, 2, 128], F32)
    G_s = dscr("G_s", [128, 128, TL], BF16).ap()
    dbg_s = dscr("dbg_s", [128, 4096], F32).ap()

    with ExitStack() as es0:
        T = Tracker(nc, es0)

        def SB(es, name, shape, dt):
            return es.enter_context(nc.sbuf_tensor("sb_" + name, shape, dt))

        def PS(es, name, shape, dt=F32):
            return es.enter_context(nc.psum_tensor("ps_" + name, shape, dt))

        def dma(q, out, in_, reads=(), writes=(), final=False):
            return T.op(q, lambda e: e.dma_start(out=out, in_=in_), reads=reads, writes=writes, dma=True, final=final)

        cols = SB(es0, "cols", [128, NCOLS], F32); b_cols = Buf()
        identf = SB(es0, "identf", [128, 128], F32); b_id = Buf()
        onesf = SB(es0, "onesf", [128, 128], F32); b_ones = Buf()
        onesb = SB(es0, "onesb", [128, 128], BF16); b_onesb = Buf()
        onesm = SB(es0, "onesm", [128, 128], F32); b_onesm = Buf()
        epsc = SB(es0, "epsc", [128, 1], F32); b_eps = Buf()
        modT = SB(es0, "modT", [128, 96, 2], F32); b_mod = Buf()
        dcol = SB(es0, "dcol", [128, 50], F32); b_dcol = Buf()
        Grow = SB(es0, "Grow", [128, 2, D], F32); b_grow = Buf()

        dma("sp", cols[:], cols_d, writes=[b_cols])
        dma("sp", identf[:], identf_d, writes=[b_id])
        T.op("pool", lambda e: e.memset(onesf[:], 1.0), writes=[b_ones])
        T.op("pool", lambda e: e.memset(onesb[:], 1.0), writes=[b_onesb])
        T.op("pool", lambda e: e.memset(onesm[:], 1.0 / 128.0), writes=[b_onesm])
        T.op("pool", lambda e: e.memset(epsc[:], EPS), writes=[b_eps])

        if phases <= 0:
            dma("sp", y_d[0:128, 0:128], identf[:], reads=[b_id], final=True)
            T.emit()
            return nc
        with ExitStack() as es:
            silu_t = SB(es, "silu_t", [128, 32], F32); b_silu = Buf()
            wa = [SB(es, "wa%d" % i, [128, 16, 512], F32) for i in range(2)]
            b_wa = [Buf(), Buf()]
            modrow = SB(es, "modrow", [2, 6 * D], F32); b_mrow = Buf()
            badar = SB(es, "badar", [2, 6 * D], F32); b_badar = Buf()
            PMr = [PS(es, "PMr%d" % i, [2, 512]) for i in range(2)]; b_pmr = [Buf(), Buf()]
            PM = PS(es, "PM", [128, 96, 2]); b_pm = Buf()
            PG = [PS(es, "PGa%d" % i, [128, 512]) for i in range(2)]
            b_pg = [Buf(), Buf()]
            T.op("act", lambda e: e.activation(out=silu_t[:], in_=cols[:, C_CC:C_CC + 32], func=AF.Silu),
                 reads=[b_cols], writes=[b_silu])
            dma("act", badar[:], badar_d, writes=[b_badar])
            wav = wada_d.rearrange("(kc p) n -> p kc n", p=128)
            for nb in range(24):
                w_t = wa[nb % 2]; bw = b_wa[nb % 2]
                pm = PMr[nb % 2]; bpm = b_pmr[nb % 2]
                for hf in range(2):
                    dma("sp" if hf == 0 else "act", w_t[:, hf * 8:(hf + 1) * 8, :],
                        wav[:, hf * 8:(hf + 1) * 8, nb * 512:(nb + 1) * 512], writes=[bw])
                for kc in range(16):
                    T.op("pe", lambda e, w_t=w_t, pm=pm, kc=kc: e.matmul(
                        pm[:], lhsT=silu_t[:, 2 * kc:2 * kc + 2], rhs=w_t[:, kc, :],
                        start=(kc == 0), stop=(kc == 15)), reads=[bw, b_silu], writes=[bpm])
                T.op("dve", lambda e, pm=pm, nb=nb: e.tensor_tensor(
                    out=modrow[:, nb * 512:(nb + 1) * 512], in0=pm[:], in1=badar[:, nb * 512:(nb + 1) * 512], op=ALU.add),
                    reads=[bpm, b_badar], writes=[b_mrow])
            for jg in range(96):
                T.op("pe", lambda e, jg=jg: e.transpose(out=PM[:, jg, :], in_=modrow[0:2, jg * 128:(jg + 1) * 128],
                     identity=identf[0:2, 0:2]), reads=[b_mrow, b_id], writes=[b_pm])
            T.op("dve", lambda e: e.tensor_copy(out=modT[:], in_=PM[:]), reads=[b_pm], writes=[b_mod])
            T.op("dve", lambda e: e.scalar_tensor_tensor(out=dcol[:, 0:16], in0=modT[:, 16:32, 0], scalar=1.0,
                 in1=cols[:, C_G1:C_G1 + 16], op0=ALU.add, op1=ALU.mult), reads=[b_mod, b_cols], writes=[b_dcol])
            T.op("dve", lambda e: e.scalar_tensor_tensor(out=dcol[:, 16:32], in0=modT[:, 16:32, 1], scalar=1.0,
                 in1=cols[:, C_G1:C_G1 + 16], op0=ALU.add, op1=ALU.mult), reads=[b_mod, b_cols], writes=[b_dcol])
            T.op("dve", lambda e: e.scalar_tensor_tensor(out=dcol[:, 32:48], in0=modT[:, 64:80, 0], scalar=1.0,
                 in1=cols[:, C_G2:C_G2 + 16], op0=ALU.add, op1=ALU.mult), reads=[b_mod, b_cols], writes=[b_dcol])
            T.op("dve", lambda e: e.tensor_scalar(out=dcol[:, 48:49], in0=cols[:, C_GQ:C_GQ + 1],
                 scalar1=float(128 ** -0.5), scalar2=None, op0=ALU.mult), reads=[b_cols], writes=[b_dcol])
            T.op("dve", lambda e: e.tensor_copy(out=dcol[:, 49:50], in_=cols[:, C_GK:C_GK + 1]),
                 reads=[b_cols], writes=[b_dcol])
            n = 0
            for gi, base in enumerate((32, 80)):
                for nb in range(4):
                    pg = PG[n % 2]; bpg = b_pg[n % 2]; n += 1
                    c0 = base * 128 + nb * 512
                    T.op("pe", lambda e, pg=pg, c0=c0: e.matmul(
                        pg[:], lhsT=onesf[0:1, :], rhs=modrow[0:1, c0:c0 + 512], start=True, stop=True),
                        reads=[b_ones, b_mrow], writes=[bpg])
                    T.op("act", lambda e, pg=pg, gi=gi, nb=nb: e.activation(
                        out=Grow[:, gi, nb * 512:(nb + 1) * 512], in_=pg[:], func=AF.Copy),
                        reads=[bpg], writes=[b_grow])
        T.barrier()
        if phases <= 1:
            dma("sp", dbg_s[:, 0:192], modT[:].rearrange("p a b -> p (a b)"), reads=[b_mod], final=True)
            dma("sp", dbg_s[:, 192:242], dcol[:], reads=[b_dcol], final=True)
            dma("sp", y_d[0:128, :], Grow[:, 0, :], reads=[b_grow], final=True)
            dma("sp", y_d[128:256, :], Grow[:, 1, :], reads=[b_grow], final=True)
            T.emit()
            return nc

        class WStream:
            def __init__(self, es, pref, wview, cast_eng="pool"):
                self.cast_eng = cast_eng
                self.st = [SB(es, pref + "wst%d" % i, [128, 16, 128], F32) for i in range(2)]
                self.b_st = [Buf(), Buf()]
                self.wb = [SB(es, pref + "wcb%d" % i, [128, 16, 128], BF16) for i in range(2)]
                self.b_wb = [Buf(), Buf()]
                self.n = 0
                self.wview = wview

            def get(self, col0):
                j = self.n % 2; self.n += 1
                st = self.st[j]; bst = self.b_st[j]; wb = self.wb[j]; bwb = self.b_wb[j]
                dma("sp", st[:], self.wview[:, :, col0:col0 + 128], writes=[bst])
                if self.cast_eng == "act":
                    T.op("act", lambda e: e.activation(out=wb[:], in_=st[:], func=AF.Copy), reads=[bst], writes=[bwb])
                else:
                    T.op("pool", lambda e: e.tensor_copy(out=wb[:], in_=st[:]), reads=[bst], writes=[bwb])
                return wb, bwb

        def norm_mod_T(es, pref, ntiles, load_fn, scol_fn, shcol_fn, hT_fn, tok0_fn, after_fn=None):
            xt = [SB(es, pref + "xt%d" % i, [128, D], F32) for i in range(2)]; b_xt = [Buf(), Buf()]
            xn = [SB(es, pref + "xn%d" % i, [128, D], F32) for i in range(2)]; b_xn = [Buf(), Buf()]
            junk = SB(es, pref + "junk", [128, D], BF16); b_junk = Buf()
            st = [SB(es, pref + "st%d" % i, [128, 4], F32) for i in range(2)]; b_st = [Buf(), Buf()]
            PT = [PS(es, pref + "PT%d" % i, [128, 512]) for i in range(2)]; b_pt = [Buf(), Buf()]
            k = 0
            load_fn(0, xt[0], b_xt[0], xn[0], b_xn[0])
            for i in range(ntiles):
                x_t = xt[i % 2]; bx = b_xt[i % 2]; xn_t = xn[i % 2]; bxn = b_xn[i % 2]
                s_t = st[i % 2]; bs = b_st[i % 2]
                T.op("act", lambda e, x_t=x_t, s_t=s_t: e.activation(out=junk[:], in_=x_t[:], func=AF.Square,
                     accum_out=s_t[:, 0:1]), reads=[bx], writes=[b_junk, bs])
                T.op("act", lambda e, s_t=s_t: e.activation(out=s_t[:, 1:2], in_=s_t[:, 0:1], func=AF.Sqrt,
                     scale=1.0 / D, bias=epsc[:, 0:1]), reads=[bs, b_eps], writes=[bs])
                T.op("dve", lambda e, s_t=s_t: e.reciprocal(out=s_t[:, 2:3], in_=s_t[:, 1:2]), reads=[bs], writes=[bs])
                T.op("act", lambda e, x_t=x_t, xn_t=xn_t, s_t=s_t: e.activation(out=xn_t[:], in_=x_t[:], func=AF.Copy,
                     scale=s_t[:, 2:3]), reads=[bx, bs], writes=[bxn])
                if i + 1 < ntiles:
                    load_fn(i + 1, xt[(i + 1) % 2], b_xt[(i + 1) % 2], xn[(i + 1) % 2], b_xn[(i + 1) % 2])
                t0 = tok0_fn(i)
                hT, b_hT = hT_fn(i)
                for g4 in range(4):
                    p_t = PT[k % 2]; bp = b_pt[k % 2]; k += 1
                    for q in range(4):
                        kc = g4 * 4 + q
                        T.op("pe", lambda e, p_t=p_t, q=q, kc=kc, xn_t=xn_t: e.transpose(
                            out=p_t[:, q * 128:(q + 1) * 128], in_=xn_t[:, kc * 128:(kc + 1) * 128], identity=identf[:]),
                            reads=[bxn, b_id], writes=[bp])
                    for q in range(4):
                        kc = g4 * 4 + q
                        if q % 2 == 0:
                            T.op("dve", lambda e, p_t=p_t, q=q, kc=kc, t0=t0, i=i, hT=hT: e.tensor_scalar(
                                out=hT[:, kc, t0:t0 + 128], in0=p_t[:, q * 128:(q + 1) * 128],
                                scalar1=scol_fn(i, kc), scalar2=shcol_fn(i, kc), op0=ALU.mult, op1=ALU.add),
                                reads=[bp, b_dcol, b_mod], writes=[b_hT])
                        else:
                            T.op("act", lambda e, p_t=p_t, q=q, kc=kc, t0=t0, i=i, hT=hT: e.activation(
                                out=hT[:, kc, t0:t0 + 128], in_=p_t[:, q * 128:(q + 1) * 128], func=AF.Identity,
                                scale=scol_fn(i, kc), bias=shcol_fn(i, kc)),
                                reads=[bp, b_dcol, b_mod], writes=[b_hT])
                if after_fn is not None:
                    after_fn(i, x_t, bx, xn_t, bxn)

        with ExitStack() as es:
            hT = SB(es, "hT", [128, 16, TA], BF16); b_hT = Buf()
            with ExitStack() as es2:
                def load1(i, x_t, bx, xn_t, bxn):
                    if i < 16:
                        dma("sp", x_t[:], x_d[i * 128:(i + 1) * 128, :], writes=[bx])
                    else:
                        dma("sp", x_t[:], ctx_d[(i - 16) * 128:(i - 15) * 128, :], writes=[bx])
                norm_mod_T(es2, "b", 18, load1,
                           lambda i, kc: dcol[:, kc:kc + 1] if i < 16 else dcol[:, 16 + kc:17 + kc],
                           lambda i, kc: modT[:, kc, 0:1] if i < 16 else modT[:, kc, 1:2],
                           lambda i: (hT, b_hT), lambda i: i * 128)
            T.barrier()
            winv = win_d.rearrange("(kc p) n -> p kc n", p=128)
            WS = WStream(es, "c", winv, cast_eng="act")
            cosT = SB(es, "cosT", [128, TL], F32); b_cos = Buf()
            sinT = SB(es, "sinT", [128, TL], F32); b_sin = Buf()
            rotT = SB(es, "rotT", [128, 128], F32); b_rot = Buf()
            dma("sp", cosT[:], cosT_d, writes=[b_cos])
            dma("sp", sinT[:], sinT_d, writes=[b_sin])
            dma("sp", rotT[:], rotT_d, writes=[b_rot])
            NS = 3
            PQ = [PS(es, "PQ%d" % i, [128, 512]) for i in range(NS)]; b_pq = [Buf() for _ in range(NS)]
            PSSl = [PS(es, "PSS%d" % i, [128, 512]) for i in range(2)]; b_pssl = [Buf(), Buf()]
            PRl = [PS(es, "PR%d" % i, [128, 512]) for i in range(2)]; b_prl = [Buf(), Buf()]
            qg = [SB(es, "qg%d" % i, [128, 512], F32) for i in range(NS)]; b_qg = [Buf() for _ in range(NS)]
            sq = [SB(es, "sq%d" % i, [128, 512], F32) for i in range(NS)]; b_sq = [Buf() for _ in range(NS)]
            rs = [SB(es, "rs%d" % i, [128, 512], F32) for i in range(NS)]; b_rs = [Buf() for _ in range(NS)]
            t1 = [SB(es, "t1%d" % i, [128, 512], F32) for i in range(NS)]; b_t1 = [Buf() for _ in range(NS)]
            t2 = [SB(es, "t2%d" % i, [128, 512], F32) for i in range(NS)]; b_t2 = [Buf() for _ in range(NS)]
            stage = [SB(es, "stage%d" % i, [128, TA], BF16) for i in range(2)]; b_stage = [Buf(), Buf()]
            vsb = SB(es, "vsb", [128, 18, 256], BF16); b_vsb = Buf()
            nblk = 0
            pending = [None]

            def make_chain(j, n, t0, rope, stg, bstg, fc, last, nb_):
                PSS = PSSl[nb_ % 2]; b_pss = b_pssl[nb_ % 2]; PR = PRl[nb_ % 2]; b_pr = b_prl[nb_ % 2]

                def chain():
                    T.op("pe", lambda e: e.matmul(PSS[:, 0:n], lhsT=onesm[:], rhs=sq[j][:, 0:n],
                         start=True, stop=True), reads=[b_onesm, b_sq[j]], writes=[b_pss])
                    if rope:
                        T.op("pe", lambda e: e.matmul(PR[:, 0:n], lhsT=rotT[:], rhs=qg[j][:, 0:n],
                             start=True, stop=True), reads=[b_rot, b_qg[j]], writes=[b_pr])
                    T.op("act", lambda e: e.activation(out=rs[j][:, 0:n], in_=PSS[:, 0:n], func=AF.Ln,
                         bias=epsc[:, 0:1]), reads=[b_pss, b_eps], writes=[b_rs[j]])
                    T.op("act", lambda e: e.activation(out=rs[j][:, 0:n], in_=rs[j][:, 0:n], func=AF.Exp, scale=-0.5),
                         reads=[b_rs[j]], writes=[b_rs[j]])
                    if rope:
                        T.op("pool", lambda e: e.tensor_tensor(out=t1[j][:, 0:n], in0=qg[j][:, 0:n],
                             in1=cosT[:, t0:t0 + n], op=ALU.mult), reads=[b_qg[j], b_cos], writes=[b_t1[j]])
                        T.op("dve", lambda e: e.tensor_tensor(out=t2[j][:, 0:n], in0=PR[:, 0:n],
                             in1=sinT[:, t0:t0 + n], op=ALU.mult), reads=[b_pr, b_sin], writes=[b_t2[j]])
                        T.op("pool", lambda e: e.tensor_tensor(out=t1[j][:, 0:n], in0=t1[j][:, 0:n],
                             in1=t2[j][:, 0:n], op=ALU.add), reads=[b_t1[j], b_t2[j]], writes=[b_t1[j]])
                        T.op("dve", lambda e: e.tensor_tensor(out=stg[:, t0:t0 + n],
                             in0=t1[j][:, 0:n], in1=rs[j][:, 0:n], op=ALU.mult),
                             reads=[b_t1[j], b_rs[j]], writes=[bstg])
                    else:
                        T.op("dve", lambda e: e.tensor_tensor(out=stg[:, t0:t0 + n],
                             in0=qg[j][:, 0:n], in1=rs[j][:, 0:n], op=ALU.mult),
                             reads=[b_qg[j], b_rs[j]], writes=[bstg])
                    if last:
                        if fc < 8:
                            dma("sp", qT_s[fc], stg[:, 0:TL], reads=[bstg])
                        else:
                            dma("sp", kT_s[fc - 8], stg[:, :], reads=[bstg])
                return chain

            for fc in range(10):
                stg = stage[fc % 2]; bstg = b_stage[fc % 2]
                gcol = dcol[:, 48:49] if fc < 8 else dcol[:, 49:50]
                blocks = [(i * 512, 512, True) for i in range(4)]
                if fc >= 8:
                    blocks.append((TL, 256, False))
                win, b_win = WS.get(fc * 128)
                for bi_, (t0, n, rope) in enumerate(blocks):
                    j = nblk % NS; nb_ = nblk; nblk += 1
                    pq = PQ[j]; bpq = b_pq[j]
                    for kc in range(16):
                        T.op("pe", lambda e, pq=pq, kc=kc, win=win, t0=t0, n=n: e.matmul(
                            pq[:, 0:n], lhsT=win[:, kc, :], rhs=hT[:, kc, t0:t0 + n],
                            start=(kc == 0), stop=(kc == 15)), reads=[b_win, b_hT], writes=[bpq])
                    if pending[0] is not None:
                        pending[0]()
                    T.op("act", lambda e, j=j, pq=pq, n=n, gcol=gcol: e.activation(
                        out=qg[j][:, 0:n], in_=pq[:, 0:n], func=AF.Copy, scale=gcol),
                        reads=[bpq, b_dcol], writes=[b_qg[j]])
                    T.op("act", lambda e, j=j, pq=pq, n=n: e.activation(
                        out=sq[j][:, 0:n], in_=pq[:, 0:n], func=AF.Square), reads=[bpq], writes=[b_sq[j]])
                    pending[0] = make_chain(j, n, t0, rope, stg, bstg, fc, bi_ == len(blocks) - 1, nb_)
            pending[0]()
            nv = 0
            for vc in range(2):
                win, b_win = WS.get(1280 + vc * 128)
                for tile in range(18):
                    pv = PQ[nv % NS]; bpv = b_pq[nv % NS]; nv += 1
                    for kc in range(16):
                        T.op("pe", lambda e, pv=pv, kc=kc, tile=tile, win=win: e.matmul(
                            pv[:, 0:128], lhsT=hT[:, kc, tile * 128:(tile + 1) * 128], rhs=win[:, kc, :],
                            start=(kc == 0), stop=(kc == 15)), reads=[b_hT, b_win], writes=[bpv])
                    T.op("act", lambda e, pv=pv, tile=tile, vc=vc: e.activation(
                        out=vsb[:, tile, vc * 128:(vc + 1) * 128], in_=pv[:, 0:128], func=AF.Copy),
                        reads=[bpv], writes=[b_vsb])
            dma("sp", v_s, vsb[:], reads=[b_vsb])
            for fc in range(8):
                stg = stage[fc % 2]; bstg = b_stage[fc % 2]
                win, b_win = WS.get(1536 + fc * 128)
                for blk in range(4):
                    j = nblk % NS; nblk += 1
                    pq = PQ[j]; bpq = b_pq[j]
                    for kc in range(16):
                        T.op("pe", lambda e, pq=pq, kc=kc, win=win, blk=blk: e.matmul(
                            pq[:], lhsT=win[:, kc, :],
                            rhs=hT[:, kc, blk * 512:(blk + 1) * 512], start=(kc == 0), stop=(kc == 15)),
                            reads=[b_win, b_hT], writes=[bpq])
                    T.op("act", lambda e, pq=pq, blk=blk, stg=stg: e.activation(
                        out=stg[:, blk * 512:(blk + 1) * 512], in_=pq[:], func=AF.Copy), reads=[bpq], writes=[bstg])
                dma("sp", fT_s[fc], stg[:, 0:TL], reads=[bstg])
        T.barrier()
        if phases <= 2:
            dma("sp", y_d[0:128, :], Grow[:, 0, :], reads=[b_grow], final=True)
            T.emit()
            return nc

        b_qTs = Buf(); b_kTs = Buf(); b_vs = Buf(); b_fTs = Buf(); b_mixs = Buf()
        with ExitStack() as es:
            kT = SB(es, "kT", [128, TA], BF16); b_kT = Buf()
            vg = SB(es, "vg", [128, 18, 128], BF16); b_vg = Buf()
            qTh = [SB(es, "qTh%d" % i, [128, TL], BF16) for i in range(2)]; b_qTh = [Buf(), Buf()]
            NR = 4
            LA = 2
            pT = [SB(es, "pT%d" % i, [128, 512], BF16) for i in range(NR)]; b_pT = [Buf() for _ in range(NR)]
            ost = [SB(es, "ost%d" % i, [128, TL], BF16) for i in range(2)]; b_ost = [Buf(), Buf()]
            rsum = SB(es, "rsum", [128, 512], F32); b_rsum = Buf()
            PSt = [PS(es, "PSt%d" % i, [128, 512]) for i in range(NR)]; b_pst = [Buf() for _ in range(NR)]
            PO = [PS(es, "PO%d" % i, [128, 512]) for i in range(2)]; b_po = [Buf(), Buf()]
            PZ = [PS(es, "PZ%d" % i, [128, 512]) for i in range(2)]; b_pz = [Buf(), Buf()]
            steps = [(h // 4, h, qb, kt) for h in range(8) for qb in range(4) for kt in range(18)]
            nst = len(steps)
            for idx in range(nst + LA):
                if idx < nst:
                    g, h, qb, kt = steps[idx]
                    q_t = qTh[h % 2]; bq = b_qTh[h % 2]
                    if qb == 0 and kt == 0:
                        if h % 4 == 0:
                            dma("sp", kT[:], kT_s[g], writes=[b_kT])
                        dma("sp", q_t[:], qT_s[h], writes=[bq])
                    s_p = PSt[idx % NR]; bsp = b_pst[idx % NR]; p_t = pT[idx % NR]; bpt = b_pT[idx % NR]
                    T.op("pe", lambda e, s_p=s_p, kt=kt, q_t=q_t, qb=qb: e.matmul(
                        s_p[:], lhsT=kT[:, kt * 128:(kt + 1) * 128], rhs=q_t[:, qb * 512:(qb + 1) * 512],
                        start=True, stop=True), reads=[b_kT, bq], writes=[bsp])
                    T.op("act", lambda e, s_p=s_p, p_t=p_t: e.activation(out=p_t[:], in_=s_p[:], func=AF.Exp),
                         reads=[bsp], writes=[bpt])
                j = idx - LA
                if j >= 0:
                    g, h, qb, kt = steps[j]
                    u = h * 4 + qb
                    po = PO[u % 2]; bpo = b_po[u % 2]; pz = PZ[u % 2]; bpz = b_pz[u % 2]
                    o_t = ost[h % 2]; bo = b_ost[h % 2]
                    pp_t = pT[j % NR]; pbpt = b_pT[j % NR]
                    if h % 4 == 0 and qb == 0 and kt == 0:
                        dma("sp", vg[:], v_s[:, :, g * 128:(g + 1) * 128], writes=[b_vg])
                    T.op("pe", lambda e, po=po, kt=kt, pp_t=pp_t: e.matmul(
                        po[:], lhsT=vg[:, kt, :], rhs=pp_t[:], start=(kt == 0), stop=(kt == 17)),
                        reads=[b_vg, pbpt], writes=[bpo])
                    T.op("pe", lambda e, pz=pz, kt=kt, pp_t=pp_t: e.matmul(
                        pz[:], lhsT=onesb[:], rhs=pp_t[:], start=(kt == 0), stop=(kt == 17)),
                        reads=[b_onesb, pbpt], writes=[bpz])
                    if kt == 17:
                        T.op("dve", lambda e, pz=pz: e.reciprocal(out=rsum[:], in_=pz[:]), reads=[bpz], writes=[b_rsum])
                        T.op("dve", lambda e, po=po, o_t=o_t, qb=qb: e.tensor_tensor(
                            out=o_t[:, qb * 512:(qb + 1) * 512], in0=po[:], in1=rsum[:], op=ALU.mult),
                            reads=[bpo, b_rsum], writes=[bo])
                        if qb == 3:
                            dma("sp", mixT_s[h], o_t[:], reads=[bo])
        T.barrier()

        with ExitStack() as es:
            ccsc = SB(es, "ccsc", [128, 2, 512], F32); b_ccsc = Buf()
            wf = SB(es, "wf", [128, 4, 2, 256], F32); b_wf = Buf()
            AB = SB(es, "AB", [128, 4, 2, 512], BF16); b_AB = Buf()
            UV = SB(es, "UV", [128, 4, 16, 512], BF16); b_UV = Buf()
            fTt = [SB(es, "fTt%d" % i, [128, TL], BF16) for i in range(4)]; b_fTt = [Buf() for _ in range(4)]
            tabC = [SB(es, "tabC%d" % i, [128, 16, 512], BF16) for i in range(1)]; b_tabC = [Buf()]
            tabS = [SB(es, "tabS%d" % i, [128, 16, 512], BF16) for i in range(1)]; b_tabS = [Buf()]
            yst = [SB(es, "yst%d" % i, [128, 512], BF16) for i in range(2)]; b_yst = [Buf() for _ in range(2)]
            PA = [PS(es, "PA%d" % i, [128, 512]) for i in range(2)]; b_pa = [Buf(), Buf()]
            PU = [PS(es, "PU%d" % i, [128, 512]) for i in range(2)]; b_pu = [Buf(), Buf()]
            PY = [PS(es, "PY%d" % i, [128, 512]) for i in range(2)]; b_py = [Buf(), Buf()]
            dma("sp", ccsc[:], ccsc_d.rearrange("(c p) n -> p c n", p=128), writes=[b_ccsc])
            dma("sp", wf[:], wf_d.rearrange("g (c p) n -> p g c n", p=128), writes=[b_wf])
            na = 0
            for g in range(4):
                for mc in range(2):
                    pa = PA[na % 2]; bpa = b_pa[na % 2]; na += 1
                    for cs in range(2):
                        for kc in range(2):
                            T.op("pe", lambda e, pa=pa, cs=cs, kc=kc, mc=mc, g=g: e.matmul(
                                pa[:, cs * 256:(cs + 1) * 256],
                                lhsT=ccsc[:, kc, cs * 256 + mc * 128:cs * 256 + (mc + 1) * 128],
                                rhs=wf[:, g, kc, :], start=(kc == 0), stop=(kc == 1)),
                                reads=[b_ccsc, b_wf], writes=[bpa])
                    T.op("act", lambda e, pa=pa, g=g, mc=mc: e.activation(out=AB[:, g, mc, :], in_=pa[:], func=AF.Copy),
                         reads=[bpa], writes=[b_AB])
            nu = 0
            for g in range(4):
                for cc in range(2):
                    dma("sp", fTt[(g % 2) * 2 + cc][:], fT_s[g * 2 + cc], writes=[b_fTt[(g % 2) * 2 + cc]])
                for tt in range(16):
                    pu = PU[nu % 2]; bpu = b_pu[nu % 2]; nu += 1
                    for cc in range(2):
                        f_t = fTt[(g % 2) * 2 + cc]; bf = b_fTt[(g % 2) * 2 + cc]
                        T.op("pe", lambda e, pu=pu, f_t=f_t, tt=tt, g=g, cc=cc: e.matmul(
                            pu[:], lhsT=f_t[:, tt * 128:(tt + 1) * 128], rhs=AB[:, g, cc, :],
                            start=(cc == 0), stop=(cc == 1)), reads=[bf, b_AB], writes=[bpu])
                    T.op("act", lambda e, pu=pu, g=g, tt=tt: e.activation(out=UV[:, g, tt, :], in_=pu[:], func=AF.Copy),
                         reads=[bpu], writes=[b_UV])
            ny = 0
            CLv = CL_d.rearrange("(n p) k -> p n k", p=128)
            SLv = nSL_d.rearrange("(n p) k -> p n k", p=128)
            for kb in range(4):
                tc_t = tabC[0]; btc = b_tabC[0]; ts_t = tabS[0]; bts = b_tabS[0]
                dma("sp", tc_t[:], CLv[:, :, kb * 512:(kb + 1) * 512], writes=[btc])
                dma("sp", ts_t[:], SLv[:, :, kb * 512:(kb + 1) * 512], writes=[bts])
                for g in range(4):
                    for dc in range(2):
                        py = PY[ny % 2]; bpy = b_py[ny % 2]; ny += 1
                        for n_ in range(16):
                            T.op("pe", lambda e, py=py, g=g, n_=n_, dc=dc, tc_t=tc_t: e.matmul(
                                py[:], lhsT=UV[:, g, n_, dc * 128:(dc + 1) * 128], rhs=tc_t[:, n_, :],
                                start=(n_ == 0), stop=False), reads=[b_UV, btc], writes=[bpy])
                        for n_ in range(16):
                            T.op("pe", lambda e, py=py, g=g, n_=n_, dc=dc, ts_t=ts_t: e.matmul(
                                py[:], lhsT=UV[:, g, n_, 256 + dc * 128:256 + (dc + 1) * 128], rhs=ts_t[:, n_, :],
                                start=False, stop=(n_ == 15)), reads=[b_UV, bts], writes=[bpy])
                        ci = g * 2 + dc
                        ys = yst[ny % 2]; bys = b_yst[ny % 2]
                        T.op("act", lambda e, py=py, ci=ci, ys=ys: e.activation(
                            out=ys[:], in_=py[:], func=AF.Identity,
                            bias=cols[:, C_BF + ci:C_BF + ci + 1]), reads=[bpy, b_cols], writes=[bys])
                        dma("sp", mixT_s[8 + ci][:, kb * 512:(kb + 1) * 512], ys[:], reads=[bys])
        T.barrier()
        if phases <= 3:
            dma("sp", y_d[0:128, :], Grow[:, 0, :], reads=[b_grow], final=True)
            T.emit()
            return nc

        with ExitStack() as es:
            wout = SB(es, "wout", [128, 16, D], BF16); b_woutl = [Buf() for _ in range(4)]
            woutv = wout_d.rearrange("(kc p) n -> p kc n", p=128)
            for nb_ in range(4):
                dma("pool", wout[:, :, nb_ * 512:(nb_ + 1) * 512], woutv[:, :, nb_ * 512:(nb_ + 1) * 512], writes=[b_woutl[nb_]])
            h2blk = [SB(es, "h2blk%d" % i, [128, 16, 512], BF16) for i in range(2)]; b_h2blk = [Buf(), Buf()]
            mixb = [SB(es, "mixb%d" % i, [128, 16, 512], BF16) for i in range(2)]; b_mixb = [Buf(), Buf()]
            PO4 = PS(es, "PO4", [128, D]); b_po4 = [Buf(), Buf()]

            def loadF(i, x_t, bx, xn_t, bxn):
                blk = i // 4
                mb = mixb[blk % 2]; bmb = b_mixb[blk % 2]
                if i % 4 == 0:
                    dma("sp", mb[:], mixT_s[:, :, blk * 512:(blk + 1) * 512].rearrange("c p t -> p c t"), writes=[bmb])
                dma("act", x_t[:], x_d[i * 128:(i + 1) * 128, :], writes=[bx])
                tl = (i % 4) * 128
                for nb in range(4):
                    for kc in range(16):
                        T.op("pe", lambda e, nb=nb, kc=kc, mb=mb, tl=tl: e.matmul(
                            PO4[:, nb * 512:(nb + 1) * 512], lhsT=mb[:, kc, tl:tl + 128],
                            rhs=wout[:, kc, nb * 512:(nb + 1) * 512], start=(kc == 0), stop=(kc == 15)),
                            reads=[bmb, b_woutl[nb]], writes=[b_po4[nb // 2]])
                for hf in range(2):
                    T.op("dve", lambda e, xn_t=xn_t, hf=hf: e.tensor_tensor(out=xn_t[:, hf * 1024:(hf + 1) * 1024],
                         in0=PO4[:, hf * 1024:(hf + 1) * 1024], in1=Grow[:, 0, hf * 1024:(hf + 1) * 1024], op=ALU.mult),
                         reads=[b_po4[hf], b_grow], writes=[bxn])
                T.op("pool", lambda e, x_t=x_t, xn_t=xn_t: e.tensor_tensor(out=x_t[:], in0=x_t[:], in1=xn_t[:], op=ALU.add),
                     reads=[bx, bxn], writes=[bx])
                dma("sp", x1_s[i * 128:(i + 1) * 128, :], x_t[:], reads=[bx])

            def afterF(i, x_t, bx, xn_t, bxn):
                if i % 4 == 3:
                    blk = i // 4
                    dma("sp", h2T_s[:, :, blk * 512:(blk + 1) * 512], h2blk[blk % 2][:], reads=[b_h2blk[blk % 2]])

            norm_mod_T(es, "f", 16, loadF,
                       lambda i, kc: dcol[:, 32 + kc:33 + kc],
                       lambda i, kc: modT[:, 48 + kc, 0:1],
                       lambda i: (h2blk[(i // 4) % 2], b_h2blk[(i // 4) % 2]), lambda i: (i % 4) * 128, afterF)
        T.barrier()
        if phases <= 4:
            dma("sp", y_d[0:128, :], Grow[:, 0, :], reads=[b_grow], final=True)
            T.emit()
            return nc

        es12 = ExitStack()
        acolT_all = SB(es12, "acolT", [128, TL], F32); b_acolT = Buf()
        thrT_all = SB(es12, "thrT", [128, TL], F32); b_thrT = Buf()
        wT_all = SB(es12, "wT", [128, TL], F32); b_wTc = Buf()
        with ExitStack() as es:
            WQ = WStream(es, "q", wq_d.rearrange("(kc p) n -> p kc n", p=128), cast_eng="act")
            skT = SB(es, "skT", [128, 16, 128], BF16); b_skT = Buf()
            dma("pool", skT[:], skT_d.rearrange("c d j -> d c j"), writes=[b_skT])
            h2b = [SB(es, "h2b%d" % i, [128, 16, 512], BF16) for i in range(1)]; b_h2b = [Buf()]
            qTb2 = [SB(es, "qTb%d" % i, [128, 16, 512], BF16) for i in range(2)]; b_qTb2 = [Buf(), Buf()]
            PQ2 = [PS(es, "PQ2%d" % i, [128, 512]) for i in range(2)]; b_pq2 = [Buf(), Buf()]
            PSc = PS(es, "PSc", [128, D]); b_psc = Buf()
            PTc = PS(es, "PTc", [128, 512]); b_ptc = Buf()
            Ssb = [SB(es, "Ssb%d" % i, [128, D], F32) for i in range(2)]; b_Ssb = [Buf(), Buf()]
            Swk = SB(es, "Swk", [128, D], F32); b_Swk = Buf()
            top = SB(es, "top", [128, 16, 16], F32); b_topc = [Buf() for _ in range(16)]
            idxu = SB(es, "idxu", [128, 8, 16], mybir.dt.uint32); b_idxu = [Buf() for _ in range(8)]
            b_Swkc = [Buf() for _ in range(16)]; b_ctoph = [Buf() for _ in range(8)]; b_cand2h = [Buf() for _ in range(8)]
            cand = SB(es, "cand", [128, 8, 256], F32); b_cand = Buf()
            cand2 = SB(es, "cand2", [128, 8, 256], F32); b_cand2 = Buf()
            ctop = SB(es, "ctop", [128, 8, 16], F32); b_ctop = Buf()
            e16 = SB(es, "e16", [128, 8, 16], F32); b_e16 = Buf()
            zz = SB(es, "zz", [128, 8, 4], F32); b_zz = Buf()
            tm2 = [SB(es, "tm%d" % i, [128, 3, 128], F32) for i in range(2)]; b_tm2 = [Buf(), Buf()]
            nq2 = [0]
            S_flat = S_s.ap().rearrange("t h p j -> t (h p j)")
            hb = h2b[0]; bhb = b_h2b[0]

            def qproj(blk):
                qT_t = qTb2[blk % 2]; bqT = b_qTb2[blk % 2]
                dma("sp", hb[:], h2T_s[:, :, blk * 512:(blk + 1) * 512], writes=[bhb])
                for c in range(16):
                    pq = PQ2[nq2[0] % 2]; bpq = b_pq2[nq2[0] % 2]; nq2[0] += 1
                    wq, b_wq = WQ.get(c * 128)
                    for kc in range(16):
                        T.op("pe", lambda e, pq=pq, kc=kc, wq=wq: e.matmul(
                            pq[:], lhsT=wq[:, kc, :], rhs=hb[:, kc, :],
                            start=(kc == 0), stop=(kc == 15)), reads=[b_wq, bhb], writes=[bpq])
                    T.op("act", lambda e, pq=pq, c=c, qT_t=qT_t: e.activation(out=qT_t[:, c, :], in_=pq[:], func=AF.Copy),
                         reads=[bpq], writes=[bqT])

            pend_tr = [None]
            qproj(0)
            for blk in range(4):
                qTb = qTb2[blk % 2]; b_qTb = b_qTb2[blk % 2]
                if blk + 1 < 4:
                    qproj(blk + 1)
                for tl in range(4):
                    ti = blk * 4 + tl
                    S_t = Ssb[ti % 2]; bS = b_Ssb[ti % 2]
                    tm = tm2[ti % 2]; b_tm = b_tm2[ti % 2]
                    for c in range(16):
                        T.op("pe", lambda e, c=c, tl=tl, qTb=qTb: e.matmul(
                            PSc[:, c * 128:(c + 1) * 128], lhsT=qTb[:, c, tl * 128:(tl + 1) * 128], rhs=skT[:, c, :],
                            start=True, stop=True), reads=[b_qTb, b_skT], writes=[b_psc])
                    if pend_tr[0] is not None:
                        pend_tr[0]()
                        pend_tr[0] = None
                    T.op("act", lambda e, S_t=S_t: e.activation(out=S_t[:], in_=PSc[:], func=AF.Copy),
                         reads=[b_psc], writes=[bS])
                    dma("sp", S_flat[ti * 128:(ti + 1) * 128, :], S_t[:], reads=[bS])
                    for c in range(16):
                        sl = slice(c * 128, (c + 1) * 128)
                        T.op("dve", lambda e, c=c, sl=sl, S_t=S_t: e.max(out=top[:, c, 0:8], in_=S_t[:, sl]),
                             reads=[bS], writes=[b_topc[c]])
                    for c in range(16):
                        sl = slice(c * 128, (c + 1) * 128)
                        T.op("dve", lambda e, c=c, sl=sl, S_t=S_t: e.match_replace(
                            out=Swk[:, sl], in_to_replace=top[:, c, 0:8], in_values=S_t[:, sl], imm_value=NEG),
                            reads=[bS, b_topc[c]], writes=[b_Swkc[c]])
                    for c in range(16):
                        sl = slice(c * 128, (c + 1) * 128)
                        T.op("dve", lambda e, c=c, sl=sl: e.max(out=top[:, c, 8:16], in_=Swk[:, sl]),
                             reads=[b_Swkc[c]], writes=[b_topc[c]])
                    for h in range(8):
                        c = 2 * h
                        sl = slice(c * 128, (c + 1) * 128)
                        T.op("dve", lambda e, c=c, h=h, sl=sl, S_t=S_t: e.max_index(out=idxu[:, h, 0:8], in_max=top[:, c, 0:8],
                             in_values=S_t[:, sl]), reads=[bS, b_topc[c]], writes=[b_idxu[h]])
                        T.op("dve", lambda e, c=c, h=h, sl=sl: e.max_index(out=idxu[:, h, 8:16], in_max=top[:, c, 8:16],
                             in_values=Swk[:, sl]), reads=[b_Swkc[c], b_topc[c]], writes=[b_idxu[h]])
                    tv = top[:].rearrange("t (h p) k -> t h p k", p=2)
                    a_bc = tv[:, :, 0, :].unsqueeze(3).broadcast_to([128, 8, 16, 16])
                    b_bc = tv[:, :, 1, :].unsqueeze(2).broadcast_to([128, 8, 16, 16])
                    T.op("dve", lambda e, a_bc=a_bc, b_bc=b_bc: e.tensor_tensor(
                        out=cand[:].rearrange("t h (k l) -> t h k l", l=16), in0=a_bc, in1=b_bc, op=ALU.add),
                        reads=b_topc, writes=[b_cand])
                    for h in range(8):
                        T.op("dve", lambda e, h=h: e.max(out=ctop[:, h, 0:8], in_=cand[:, h, :]),
                             reads=[b_cand], writes=[b_ctoph[h]])
                    for h in range(8):
                        T.op("dve", lambda e, h=h: e.match_replace(out=cand2[:, h, :], in_to_replace=ctop[:, h, 0:8],
                             in_values=cand[:, h, :], imm_value=NEG), reads=[b_cand, b_ctoph[h]], writes=[b_cand2h[h]])
                    for h in range(8):
                        T.op("dve", lambda e, h=h: e.max(out=ctop[:, h, 8:16], in_=cand2[:, h, :]),
                             reads=[b_cand2h[h]], writes=[b_ctoph[h]])
                    b_top = None
                    T.op("dve", lambda e: e.tensor_tensor(out=e16[:], in0=ctop[:], in1=ctop[:, :, 0:1].broadcast_to([128, 8, 16]),
                         op=ALU.subtract), reads=b_ctoph, writes=[b_e16])
                    T.op("act", lambda e: e.activation(out=e16[:], in_=e16[:], func=AF.Exp), reads=[b_e16], writes=[b_e16])
                    T.op("dve", lambda e: e.tensor_reduce(out=zz[:, :, 1], in_=e16[:], op=ALU.add, axis=AX.X),
                         reads=[b_e16], writes=[b_zz])
                    T.op("act", lambda e: e.activation(out=zz[:, :, 2], in_=zz[:, :, 1], func=AF.Ln), reads=[b_zz], writes=[b_zz])
                    T.op("dve", lambda e: e.tensor_tensor(out=zz[:, :, 0], in0=zz[:, :, 2], in1=ctop[:, :, 0], op=ALU.add),
                         reads=[b_zz] + b_ctoph, writes=[b_zz])
                    a_v = tv[:, :, 0, :]
                    T.op("dve", lambda e, tm=tm: e.tensor_copy(out=tm[:, 0, :].rearrange("t (h k) -> t h k", k=16), in_=idxu[:]),
                         reads=b_idxu, writes=[b_tm])
                    T.op("dve", lambda e: e.tensor_tensor(out=cand2[:], in0=cand[:],
                         in1=ctop[:, :, 15:16].broadcast_to([128, 8, 256]), op=ALU.is_ge),
                         reads=[b_cand] + b_ctoph, writes=b_cand2h)
                    T.op("dve", lambda e: e.tensor_scalar(out=cand2[:], in0=cand2[:], scalar1=-1.0, scalar2=NEG,
                         op0=ALU.add, op1=ALU.mult), reads=b_cand2h, writes=b_cand2h)
                    T.op("dve", lambda e, b_bc=b_bc: e.tensor_tensor(out=cand2[:].rearrange("t h (k l) -> t h k l", l=16),
                         in0=cand2[:].rearrange("t h (k l) -> t h k l", l=16), in1=b_bc, op=ALU.add),
                         reads=b_cand2h + b_topc, writes=b_cand2h)
                    T.op("dve", lambda e, tm=tm: e.tensor_reduce(out=tm[:, 1, :].rearrange("t (h k) -> t h k", k=16),
                         in_=cand2[:].rearrange("t h (k l) -> t h k l", l=16), op=ALU.min, axis=AX.X),
                         reads=b_cand2h, writes=[b_tm])
                    T.op("dve", lambda e, a_v=a_v, tm=tm: e.tensor_tensor(out=tm[:, 2, :].rearrange("t (h k) -> t h k", k=16),
                         in0=a_v, in1=zz[:, :, 0:1].broadcast_to([128, 8, 16]), op=ALU.subtract),
                         reads=b_topc + [b_zz], writes=[b_tm])
                    T.op("act", lambda e, tm=tm: e.activation(out=tm[:, 2, :], in_=tm[:, 2, :], func=AF.Exp), reads=[b_tm], writes=[b_tm])
                    def mk_tr(ti, tm, b_tm):
                        def tr():
                            for k3 in range(3):
                                T.op("pe", lambda e, k3=k3: e.transpose(out=PTc[:, k3 * 128:(k3 + 1) * 128], in_=tm[:, k3, :],
                                     identity=identf[:]), reads=[b_tm, b_id], writes=[b_ptc])
                            T.op("act", lambda e: e.activation(out=acolT_all[:, ti * 128:(ti + 1) * 128], in_=PTc[:, 0:128],
                                 func=AF.Copy), reads=[b_ptc], writes=[b_acolT])
                            T.op("act", lambda e: e.activation(out=thrT_all[:, ti * 128:(ti + 1) * 128], in_=PTc[:, 128:256],
                                 func=AF.Copy), reads=[b_ptc], writes=[b_thrT])
                            T.op("act", lambda e: e.activation(out=wT_all[:, ti * 128:(ti + 1) * 128], in_=PTc[:, 256:384],
                                 func=AF.Copy), reads=[b_ptc], writes=[b_wTc])
                        return tr
                    pend_tr[0] = mk_tr(ti, tm, b_tm)
            pend_tr[0]()
        T.barrier()
        if phases <= 5:
            dma("sp", dbg_s[:, 0:2048], acolT_all[:], reads=[b_acolT], final=True)
            dma("sp", dbg_s[:, 2048:4096], wT_all[:], reads=[b_wTc], final=True)
            dma("sp", y_d[0:128, :], thrT_all[:], reads=[b_thrT], final=True)
            T.emit()
            es12.close()
            return nc

        TB = 16
        with ExitStack() as es:
            iot = SB(es, "iot", [128, 128], F32); b_iot = Buf()
            dma("sp", iot[:], iota_d, writes=[b_iot])
            iota3 = SB(es, "iota3", [128, 128, TB], BF16); b_iota3 = Buf()
            idxb = SB(es, "idxb", [128, TL], BF16); b_idxb = Buf()
            T.op("dve", lambda e: e.tensor_copy(out=iota3[:], in_=iot[:].unsqueeze(2).broadcast_to([128, 128, TB])),
                 reads=[b_iot], writes=[b_iota3])
            T.op("act", lambda e: e.activation(out=idxb[:], in_=acolT_all[:], func=AF.Copy), reads=[b_acolT], writes=[b_idxb])
            rep = [SB(es, "rep%d" % i, [128, TB, 128], F32) for i in range(3)]; b_rep = [Buf() for _ in range(3)]
            for i_ in range(3):
                T.op("pool", lambda e, i_=i_: e.memset(rep[i_][:], 0.0), writes=[b_rep[i_]])
            Eb = [SB(es, "Eb%d" % i, [128, TB, 128], BF16) for i in range(2)]
            b_Eb = [[Buf() for _ in range(TB)] for _ in range(2)]
            Mk = [SB(es, "Mk%d" % i, [128, TB, 128], BF16) for i in range(2)]
            b_Mk = [[Buf() for _ in range(TB)] for _ in range(2)]
            X0 = [SB(es, "X0%d" % i, [128, 128, TB], BF16) for i in range(3)]
            b_X0 = [[Buf() for _ in range(TB)] for _ in range(3)]
            Rt = [SB(es, "Rt%d" % i, [128, TB, 128], BF16) for i in range(3)]
            b_Rt = [[Buf() for _ in range(TB)] for _ in range(3)]
            Gbuf = [SB(es, "Gbuf%d" % i, [128, 128, 128], BF16) for i in range(2)]
            b_Gbuf = [[Buf() for _ in range(32)] for _ in range(2)]
            PGt = [PS(es, "PGt%d" % i, [128, 4, 128]) for i in range(8)]; b_pgt = [Buf() for _ in range(8)]
            npg = [0]
            iota_bc = iot[:].unsqueeze(1).broadcast_to([128, TB, 128])
            pend = []

            def make_back(ti, sb_, R_t, bR, x0, bx0, gb, bgb):
                def back():
                    for q4 in range(TB // 4):
                        pg = PGt[npg[0] % 8]; bpg = b_pgt[npg[0] % 8]; npg[0] += 1
                        for u in range(4):
                            tt = q4 * 4 + u
                            T.op("pe", lambda e, pg=pg, u=u, tt=tt: e.matmul(
                                pg[:, u, :], lhsT=R_t[:, tt, :], rhs=x0[:, :, tt], start=True, stop=True),
                                reads=[bR[tt], bx0[tt]], writes=[bpg])
                        tl0 = sb_ * TB + q4 * 4
                        T.op("act", lambda e, pg=pg, tl0=tl0: e.activation(
                            out=gb[:, :, tl0:tl0 + 4], in_=pg[:].rearrange("j u i -> j i u"), func=AF.Copy),
                            reads=[bpg], writes=[bgb[tl0 // 4]])
                    if sb_ == 128 // TB - 1:
                        dma("sp", G_s[:, :, ti * 128:(ti + 1) * 128].rearrange("i j t -> j i t"), gb[:], reads=bgb)
                return back

            for ti in range(16):
                gb = Gbuf[ti % 2]; bgb = b_Gbuf[ti % 2]
                for sb_ in range(128 // TB):
                    bi = ti * (128 // TB) + sb_
                    tok0 = ti * 128 + sb_ * TB
                    r_t = rep[bi % 3]; br = b_rep[bi % 3]
                    E_t = Eb[bi % 2]; bE = b_Eb[bi % 2]; x0 = X0[bi % 3]; bx0 = b_X0[bi % 3]
                    R_t = Rt[bi % 3]; bR = b_Rt[bi % 3]; m_t = Mk[bi % 2]; bM = b_Mk[bi % 2]
                    src = bass.AP(S_s, tok0 * 2048 + 128, [[256, 8], [2048, TB], [1, 128]])
                    dma("sp", r_t[0:128:16, :, :], src, writes=[br])
                    T.op("dve", lambda e, r_t=r_t: e.stream_shuffle(out=r_t[:], in_=r_t[:], mask=[0] * 16 + [16] * 16),
                         reads=[br], writes=[br])
                    T.op("act", lambda e, E_t=E_t, r_t=r_t: e.activation(out=E_t[:], in_=r_t[:], func=AF.Exp),
                         reads=[br], writes=bE)
                    idx_bc = idxb[:, tok0:tok0 + TB].unsqueeze(1).broadcast_to([128, 128, TB])
                    thr_bc = thrT_all[:, tok0:tok0 + TB].unsqueeze(2).broadcast_to([128, TB, 128])
                    T.op("dve", lambda e, x0=x0, idx_bc=idx_bc: e.tensor_tensor(out=x0[:], in0=iota3[:], in1=idx_bc,
                         op=ALU.is_equal), reads=[b_iota3, b_idxb], writes=bx0)
                    T.op("dve", lambda e, m_t=m_t, r_t=r_t, thr_bc=thr_bc: e.tensor_tensor(out=m_t[:], in0=r_t[:], in1=thr_bc,
                         op=ALU.is_ge), reads=[br, b_thrT], writes=bM)
                    w_bc = wT_all[:, tok0:tok0 + TB].unsqueeze(2).broadcast_to([128, TB, 128])
                    T.op("pool", lambda e, m_t=m_t, E_t=E_t: e.tensor_tensor(out=m_t[:], in0=m_t[:], in1=E_t[:],
                         op=ALU.mult), reads=bM + bE, writes=bM)
                    T.op("pool", lambda e, R_t=R_t, m_t=m_t, w_bc=w_bc: e.tensor_tensor(out=R_t[:], in0=m_t[:], in1=w_bc,
                         op=ALU.mult), reads=bM + [b_wTc], writes=bR)
                    pend.append(make_back(ti, sb_, R_t, bR, x0, bx0, gb, bgb))
                    if len(pend) > 2:
                        pend.pop(0)()
            while pend:
                pend.pop(0)()
        T.barrier()
        es12.close()
        if phases <= 6:
            dma("sp", y_d[0:128, :], Grow[:, 0, :], reads=[b_grow], final=True)
            T.emit()
            return nc

        NH = 2
        TH = TL // NH
        NT = TH // 128
        GE = 4
        with ExitStack() as es:
            gfr = SB(es, "gfr", [128, D], F32); b_gfr = Buf()
            dma("sp", gfr[:], gfin_d.ap().broadcast_to([128, D]), writes=[b_gfr])
            acc = SB(es, "acc", [128, NT, D], F32); b_acc = [Buf() for _ in range(NT)]
            h2h = SB(es, "h2h", [128, 16, TH], BF16); b_h2h = Buf()
            Wg = [SB(es, "Wg%d" % i, [128, GE, TH], BF16) for i in range(2)]; b_Wg = [Buf(), Buf()]
            UT = [SB(es, "UT%d" % i, [128, 16, 128], BF16) for i in range(2)]; b_UT = [Buf() for _ in range(2)]
            Vb = [SB(es, "Vb%d" % i, [128, D], BF16) for i in range(GE)]; b_Vb = [Buf() for _ in range(GE)]
            ga = [SB(es, "ga%d" % i, [128, TH], BF16) for i in range(2)]; b_ga = [Buf(), Buf()]
            x1t = [SB(es, "x1t%d" % i, [128, D], F32) for i in range(1)]; b_x1t = [Buf()]
            sth = [SB(es, "sth%d" % i, [128, 4], F32) for i in range(2)]; b_sth = [Buf(), Buf()]
            PAe = [PS(es, "PAe%d" % i, [128, TH]) for i in range(2)]; b_pae = [Buf(), Buf()]
            POe = [PS(es, "POe%d" % i, [128, 1024]) for i in range(2)]; b_poe = [Buf(), Buf()]
            nch = 0; nun = 0; neg = 0
            for half in range(NH):
                tb0 = half * TH
                dma("sp", h2h[:], h2T_s[:, :, tb0:tb0 + TH], writes=[b_h2h])
                for eg in range(128 // GE):
                    w_t = Wg[neg % 2]; bw = b_Wg[neg % 2]; neg += 1
                    dma("sp", w_t[:], G_s[eg * GE:(eg + 1) * GE, :, tb0:tb0 + TH].rearrange("i j t -> j i t"), writes=[bw])
                    for ii in range(GE):
                        i = eg * GE + ii
                        u_t = UT[nch % 2]; bu = b_UT[nch % 2]
                        v_t = Vb[ii]; bv = b_Vb[ii]
                        g_t = ga[nch % 2]; bg = b_ga[nch % 2]
                        pa = PAe[nch % 2]; bpa = b_pae[nch % 2]
                        nch += 1
                        dma("pool", u_t[:], uT_d[i].rearrange("p (kc e) -> p kc e", e=128), writes=[bu])
                        dma("pool", v_t[:], v_d[i * 128:(i + 1) * 128, :], writes=[bv])
                        for tb in range(TH // 512):
                            for kc in range(16):
                                T.op("pe", lambda e, pa=pa, tb=tb, kc=kc, u_t=u_t: e.matmul(
                                    pa[:, tb * 512:(tb + 1) * 512], lhsT=u_t[:, kc, :], rhs=h2h[:, kc, tb * 512:(tb + 1) * 512],
                                    start=(kc == 0), stop=(kc == 15)), reads=[bu, b_h2h], writes=[bpa])
                        T.op("act", lambda e, pa=pa, g_t=g_t: e.activation(out=g_t[:], in_=pa[:], func=AF.Gelu),
                             reads=[bpa], writes=[bg])
                        T.op("dve", lambda e, w_t=w_t, ii=ii, g_t=g_t: e.tensor_tensor(
                            out=w_t[:, ii, :], in0=w_t[:, ii, :], in1=g_t[:], op=ALU.mult), reads=[bw, bg], writes=[bw])
                    for tl in range(NT):
                        for dh in range(2):
                            po = POe[nun % 2]; bpo = b_poe[nun % 2]; nun += 1
                            for ii in range(GE):
                                v_t = Vb[ii]; bv = b_Vb[ii]
                                for nb in range(2):
                                    T.op("pe", lambda e, po=po, nb=nb, ii=ii, tl=tl, dh=dh, w_t=w_t, v_t=v_t: e.matmul(
                                        po[:, nb * 512:(nb + 1) * 512], lhsT=w_t[:, ii, tl * 128:(tl + 1) * 128],
                                        rhs=v_t[:, dh * 1024 + nb * 512:dh * 1024 + (nb + 1) * 512],
                                        start=(ii == 0), stop=(ii == GE - 1)), reads=[bw, bv], writes=[bpo])
                            if eg == 0:
                                T.op("act", lambda e, po=po, tl=tl, dh=dh: e.activation(
                                    out=acc[:, tl, dh * 1024:(dh + 1) * 1024], in_=po[:], func=AF.Copy),
                                    reads=[bpo], writes=[b_acc[tl]])
                            else:
                                T.op("dve", lambda e, po=po, tl=tl, dh=dh: e.tensor_tensor(
                                    out=acc[:, tl, dh * 1024:(dh + 1) * 1024], in0=po[:],
                                    in1=acc[:, tl, dh * 1024:(dh + 1) * 1024], op=ALU.add),
                                    reads=[bpo, b_acc[tl]], writes=[b_acc[tl]])
                junk = Wg[0][:, 0:2, :]; b_junk = b_Wg[0]
                for tl in range(NT):
                    ti = half * NT + tl
                    x1 = x1t[0]; bx1 = b_x1t[0]
                    s_t = sth[tl % 2]; bs = b_sth[tl % 2]
                    o_t = acc[:, tl, :]; bo = b_acc[tl]
                    dma("sp", x1[:], x1_s[ti * 128:(ti + 1) * 128, :], writes=[bx1])
                    T.op("dve", lambda e, o_t=o_t: e.tensor_tensor(out=o_t, in0=o_t, in1=Grow[:, 1, :],
                         op=ALU.mult), reads=[bo, b_grow], writes=[bo])
                    T.op("pool", lambda e, o_t=o_t, x1=x1: e.tensor_tensor(out=o_t, in0=o_t, in1=x1[:], op=ALU.add),
                         reads=[bo, bx1], writes=[bo])
                    T.op("act", lambda e, o_t=o_t, s_t=s_t: e.activation(out=junk, in_=o_t, func=AF.Square,
                         accum_out=s_t[:, 0:1]), reads=[bo], writes=[b_junk, bs])
                    T.op("act", lambda e, s_t=s_t: e.activation(out=s_t[:, 1:2], in_=s_t[:, 0:1], func=AF.Sqrt,
                         scale=1.0 / D, bias=epsc[:, 0:1]), reads=[bs, b_eps], writes=[bs])
                    T.op("dve", lambda e, s_t=s_t: e.reciprocal(out=s_t[:, 2:3], in_=s_t[:, 1:2]), reads=[bs], writes=[bs])
                    T.op("dve", lambda e, o_t=o_t, s_t=s_t: e.scalar_tensor_tensor(out=o_t, in0=o_t, scalar=s_t[:, 2:3],
                         in1=gfr[:], op0=ALU.mult, op1=ALU.mult), reads=[bo, bs, b_gfr], writes=[bo])
                    dma("sp", y_d[ti * 128:(ti + 1) * 128, :], o_t, reads=[bo], final=True)
        T.emit()
        print("tracker: ops=%d waits=%d" % (T.n_ops, T.n_waits))
    return nc


_CONSTS = None


def _consts():
    global _CONSTS
    if _CONSTS is not None:
        return _CONSTS
    identf = np.eye(128, dtype=np.float32)
    R = np.zeros((128, 128), np.float32)
    for blk in (0, 64):
        for m in range(32):
            R[blk + m, blk + m + 32] = -1.0
            R[blk + 32 + m, blk + m] = 1.0
    rotT = np.ascontiguousarray(R.T)
    inv_freq = (10000.0 ** (-np.arange(32, dtype=np.float32) / 32)).astype(np.float32)
    tpos = np.arange(TL)
    row = (tpos // 64).astype(np.float32)
    col = (tpos % 64).astype(np.float32)
    ang_row = row[:, None] * inv_freq[None, :]
    ang_col = col[:, None] * inv_freq[None, :]
    ang = np.concatenate([ang_row, ang_row, ang_col, ang_col], axis=1).astype(np.float32)
    cosT = np.ascontiguousarray(np.cos(ang).T.astype(np.float32))
    sinT = np.ascontiguousarray(np.sin(ang).T.astype(np.float32))
    cc = np.arange(256)
    ph = 2.0 * np.pi * ((cc[:, None] * cc[None, :]) % 256) / 256.0
    sc = 1.0 / math.sqrt(256.0 * TL)
    ccsc = np.concatenate([np.cos(ph) * sc, np.sin(ph) * sc], axis=1).astype(np.float32)
    n = np.arange(TL)
    phl = 2.0 * np.pi * ((n[:, None] * n[None, :]) % TL) / TL
    CL = np.cos(phl).astype(ml_dtypes.bfloat16)
    nSL = (-np.sin(phl)).astype(ml_dtypes.bfloat16)
    iotaf = np.ascontiguousarray(np.broadcast_to(np.arange(128, dtype=np.float32)[None, :], (128, 128)))
    _CONSTS = dict(iotaf=iotaf, identf=identf, rotT=rotT, cosT=cosT, sinT=sinT, ccsc=ccsc, CL=CL, nSL=nSL)
    return _CONSTS


def make_in_maps(x, c, ctx, c_ctx, w_ada, b_ada, g_norm1, w_in, g_q, g_k, w_fourier, b_fourier,
                 w_out, g_norm2, w_query, sub_keys, u_experts, v_experts, g_final):
    f = lambda a: np.ascontiguousarray(np.asarray(a, dtype=np.float32))
    cs = _consts()
    w_ada0 = f(w_ada[0]); w_in0 = f(w_in[0]); w_out0 = f(w_out[0]); wq0 = f(w_query[0]); wf0 = f(w_fourier[0])
    skT = f(np.asarray(sub_keys[0]).reshape(16, 128, 128).transpose(0, 2, 1))
    u = np.asarray(u_experts[0], dtype=np.float32)
    uT = f(u.reshape(128, 128, 16, 128).transpose(0, 3, 2, 1).reshape(128, 128, 2048))
    v0 = f(v_experts[0])
    gfin = f(np.asarray(g_final).reshape(1, D))
    bada_row = f(np.stack([np.asarray(b_ada[0]), np.asarray(b_ada[0])], axis=0))

    def colz(vec):
        return np.asarray(vec, np.float32).reshape(-1, 128).T

    in_maps = []
    for b in range(8):
        cols = np.zeros((128, NCOLS), np.float32)
        cb = colz(c[b]); cx = colz(c_ctx)
        cols[:, C_CC:C_CC + 32:2] = cb
        cols[:, C_CC + 1:C_CC + 32:2] = cx
        cols[:, C_G1:C_G1 + 16] = colz(g_norm1[0])
        cols[:, C_G2:C_G2 + 16] = colz(g_norm2[0])
        cols[:, C_BADA:C_BADA + 96] = colz(b_ada[0])
        cols[:, C_BF:C_BF + 8] = colz(b_fourier[0])
        cols[:, C_GQ] = np.asarray(g_q[0], np.float32)
        cols[:, C_GK] = np.asarray(g_k[0], np.float32)
        m = dict(x=f(x[b]), ctx=f(ctx[b]), cols=cols, w_ada=w_ada0, w_in=w_in0, w_fourier=wf0, w_out=w_out0,
                 w_query=wq0, skT=skT, uT=uT, v_experts=v0, g_final=gfin, bada_row=bada_row)
        m.update(cs)
        in_maps.append(m)
    return in_maps


def kernel(**inputs):
    in_maps = make_in_maps(**inputs)
    nc = build()
    res = run_bass_kernel_spmd(nc, in_maps, core_ids=list(range(8)))
    out = np.stack([np.asarray(r["y"], dtype=np.float32) for r in res.results], axis=0)
    return out
```

```python
import math
import numpy as np
import ml_dtypes
from contextlib import ExitStack
import concourse.bass as bass
import concourse.mybir as mybir
from concourse.bass_utils import run_bass_kernel_spmd

F32 = mybir.dt.float32
BF16 = mybir.dt.bfloat16
AF = mybir.ActivationFunctionType
ALU = mybir.AluOpType
AX = mybir.AxisListType

D = 2048
TL = 2048
TC = 256
TA = TL + TC
EPS = 1e-6
NEG = -1e30


class Buf:
    __slots__ = ("name", "w", "r")

    def __init__(self, name=""):
        self.name = name
        self.w = None
        self.r = []


class Tracker:
    COMPUTE = ("pe", "act", "dve", "pool")

    def __init__(self, nc, es, n_dma_sems=10, same_engine_sync=True):
        self.nc = nc
        self.semobj = {}
        self.count = {}
        for e in self.COMPUTE:
            self.semobj["c_" + e] = es.enter_context(nc.semaphore("s_" + e))
            self.count["c_" + e] = 0
        self.dma_pool = {}
        for q in ("sp", "act", "pool"):
            lst = []
            for i in range(n_dma_sems):
                nm = "d_%s_%d" % (q, i)
                self.semobj[nm] = es.enter_context(nc.semaphore(nm))
                self.count[nm] = 0
                lst.append(nm)
            self.dma_pool[q] = [lst, 0]
        self.prog = {e: [] for e in ("pe", "act", "dve", "pool", "sp")}
        self.seen = {e: {} for e in self.prog}
        self.pending = {e: {} for e in self.prog}
        self.same_engine_sync = same_engine_sync
        self.final_tokens = []
        self.n_ops = 0
        self.n_waits = 0

    def _need(self, eng, tok, waits, kind):
        if tok is None:
            return
        sid, val, teng = tok
        if teng == eng and teng in self.COMPUTE:
            if eng == "pe" or not self.same_engine_sync:
                return
        if self.seen[eng].get(sid, 0) >= val:
            return
        if waits.get(sid, 0) < val:
            waits[sid] = val

    def barrier(self):
        for e in self.prog:
            for sid, c in self.count.items():
                if c > 0 and self.seen[e].get(sid, 0) < c:
                    if sid == "c_" + e:
                        continue
                    if self.pending[e].get(sid, 0) < c:
                        self.pending[e][sid] = c

    def op(self, eng, fn, reads=(), writes=(), dma=False, final=False):
        waits = {}
        if self.pending[eng]:
            for sid, v in self.pending[eng].items():
                if self.seen[eng].get(sid, 0) < v:
                    waits[sid] = v
            self.pending[eng] = {}
        for b in reads:
            self._need(eng, b.w, waits, "raw")
        for b in writes:
            self._need(eng, b.w, waits, "waw")
            for t in b.r:
                self._need(eng, t, waits, "war")
        if dma:
            pool = self.dma_pool[eng]
            sid = pool[0][pool[1] % len(pool[0])]
            pool[1] += 1
            prev = self.count[sid]
            if prev > 0 and self.seen[eng].get(sid, 0) < prev and waits.get(sid, 0) < prev:
                waits[sid] = prev
            self.count[sid] = prev + 16
            tok = (sid, prev + 16, "dma_" + eng)
            inc = 16
        else:
            sid = "c_" + eng
            self.count[sid] += 1
            tok = (sid, self.count[sid], eng)
            inc = 1
        for s, v in waits.items():
            self.seen[eng][s] = v
        self.n_ops += 1
        self.n_waits += len(waits)
        self.prog[eng].append((list(waits.items()), fn, sid, inc))
        for b in reads:
            b.r.append(tok)
        for b in writes:
            b.w = tok
            b.r = []
        if final:
            self.final_tokens.append(tok)
        return tok

    def emit(self):
        nc = self.nc
        prog = self.prog
        semobj = self.semobj
        finals = self.final_tokens

        def run(engname, eng):
            for waits, fn, sid, inc in prog[engname]:
                for s, v in waits:
                    eng.wait_ge(semobj[s], v)
                fn(eng).then_inc(semobj[sid], inc)

        with nc.Block() as block:
            @block.tensor
            def _(eng):
                run("pe", eng)

            @block.scalar
            def _(eng):
                run("act", eng)

            @block.vector
            def _(eng):
                run("dve", eng)

            @block.gpsimd
            def _(eng):
                run("pool", eng)

            @block.sync
            def _(eng):
                run("sp", eng)
                for (s, v, _e) in finals:
                    eng.wait_ge(semobj[s], v)


C_CC = 0
C_G1 = 32
C_G2 = 48
C_BADA = 64
C_BF = 160
C_GQ = 168
C_GK = 169
NCOLS = 170


def build(debug=(), phases=9):
    nc = bass.Bass("TRN2", target_bir_lowering=False)

    def din(name, shape, dt=F32):
        return nc.dram_tensor(name, shape, dt, kind="ExternalInput")

    def dscr(name, shape, dt):
        kind = "ExternalOutput" if name in debug else "Internal"
        return nc.dram_tensor(name, shape, dt, kind=kind)

    x_d = din("x", [TL, D]).ap()
    ctx_d = din("ctx", [TC, D]).ap()
    cols_d = din("cols", [128, NCOLS]).ap()
    wada_d = din("w_ada", [D, 6 * D]).ap()
    win_d = din("w_in", [D, 2560]).ap()
    wf_d = din("w_fourier", [4, 256, 256]).ap()
    wout_d = din("w_out", [D, D]).ap()
    wq_d = din("w_query", [D, D]).ap()
    skT_d = din("skT", [16, 128, 128]).ap()
    uT_d = din("uT", [128, 128, D]).ap()
    v_d = din("v_experts", [128 * 128, D]).ap()
    gfin_d = din("g_final", [1, D])
    badar_d = din("bada_row", [2, 6 * D]).ap()
    identf_d = din("identf", [128, 128]).ap()
    rotT_d = din("rotT", [128, 128]).ap()
    iota_d = din("iotaf", [128, 128]).ap()
    cosT_d = din("cosT", [128, TL]).ap()
    sinT_d = din("sinT", [128, TL]).ap()
    ccsc_d = din("ccsc", [256, 512]).ap()
    CL_d = din("CL", [TL, TL], BF16).ap()
    nSL_d = din("nSL", [TL, TL], BF16).ap()
    y_d = nc.dram_tensor("y", [TL, D], F32, kind="ExternalOutput").ap()

    qT_s = dscr("qT_s", [8, 128, TL], BF16).ap()
    kT_s = dscr("kT_s", [2, 128, TA], BF16).ap()
    v_s = dscr("v_s", [128, 18, 256], BF16).ap()
    fT_s = dscr("fT_s", [8, 128, TL], BF16).ap()
    mixT_s = dscr("mixT_s", [16, 128, TL], BF16).ap()
    x1_s = dscr("x1_s", [TL, D], F32).ap()
    h2T_s = dscr("h2T_s", [128, 16, TL], BF16).ap()
    S_s = dscr("S_s", [TL, 8, 2, 128], F32)
    G_s = dscr("G_s", [128, 128, TL], BF16).ap()
    dbg_s = dscr("dbg_s", [128, 4096], F32).ap()

    with ExitStack() as es0:
        T = Tracker(nc, es0)

        def SB(es, name, shape, dt):
            return es.enter_context(nc.sbuf_tensor("sb_" + name, shape, dt))

        def PS(es, name, shape, dt=F32):
            return es.enter_context(nc.psum_tensor("ps_" + name, shape, dt))

        def dma(q, out, in_, reads=(), writes=(), final=False):
            return T.op(q, lambda e: e.dma_start(out=out, in_=in_), reads=reads, writes=writes, dma=True, final=final)

        cols = SB(es0, "cols", [128, NCOLS], F32); b_cols = Buf()
        identf = SB(es0, "identf", [128, 128], F32); b_id = Buf()
        onesf = SB(es0, "onesf", [128, 128], F32); b_ones = Buf()
        onesb = SB(es0, "onesb", [128, 128], BF16); b_onesb = Buf()
        onesm = SB(es0, "onesm", [128, 128], F32); b_onesm = Buf()
        epsc = SB(es0, "epsc", [128, 1], F32); b_eps = Buf()
        modT = SB(es0, "modT", [128, 96, 2], F32); b_mod = Buf()
        dcol = SB(es0, "dcol", [128, 50], F32); b_dcol = Buf()
        Grow = SB(es0, "Grow", [128, 2, D], F32); b_grow = Buf()

        dma("sp", cols[:], cols_d, writes=[b_cols])
        dma("sp", identf[:], identf_d, writes=[b_id])
        T.op("pool", lambda e: e.memset(onesf[:], 1.0), writes=[b_ones])
        T.op("pool", lambda e: e.memset(onesb[:], 1.0), writes=[b_onesb])
        T.op("pool", lambda e: e.memset(onesm[:], 1.0 / 128.0), writes=[b_onesm])
        T.op("pool", lambda e: e.memset(epsc[:], EPS), writes=[b_eps])

        if phases <= 0:
            dma("sp", y_d[0:128, 0:128], identf[:], reads=[b_id], final=True)
            T.emit()
            return nc
        with ExitStack() as es:
            silu_t = SB(es, "silu_t", [128, 32], F32); b_silu = Buf()
            wa = [SB(es, "wa%d" % i, [128, 16, 512], F32) for i in range(2)]
            b_wa = [Buf(), Buf()]
            modrow = SB(es, "modrow", [2, 6 * D], F32); b_mrow = Buf()
            badar = SB(es, "badar", [2, 6 * D], F32); b_badar = Buf()
            PMr = [PS(es, "PMr%d" % i, [2, 512]) for i in range(2)]; b_pmr = [Buf(), Buf()]
            PM = PS(es, "PM", [128, 96, 2]); b_pm = Buf()
            PG = [PS(es, "PGa%d" % i, [128, 512]) for i in range(2)]
            b_pg = [Buf(), Buf()]
            T.op("act", lambda e: e.activation(out=silu_t[:], in_=cols[:, C_CC:C_CC + 32], func=AF.Silu),
                 reads=[b_cols], writes=[b_silu])
            dma("act", badar[:], badar_d, writes=[b_badar])
            wav = wada_d.rearrange("(kc p) n -> p kc n", p=128)
            for nb in range(24):
                w_t = wa[nb % 2]; bw = b_wa[nb % 2]
                pm = PMr[nb % 2]; bpm = b_pmr[nb % 2]
                for hf in range(2):
                    dma("sp" if hf == 0 else "act", w_t[:, hf * 8:(hf + 1) * 8, :],
                        wav[:, hf * 8:(hf + 1) * 8, nb * 512:(nb + 1) * 512], writes=[bw])
                for kc in range(16):
                    T.op("pe", lambda e, w_t=w_t, pm=pm, kc=kc: e.matmul(
                        pm[:], lhsT=silu_t[:, 2 * kc:2 * kc + 2], rhs=w_t[:, kc, :],
                        start=(kc == 0), stop=(kc == 15)), reads=[bw, b_silu], writes=[bpm])
                T.op("dve", lambda e, pm=pm, nb=nb: e.tensor_tensor(
                    out=modrow[:, nb * 512:(nb + 1) * 512], in0=pm[:], in1=badar[:, nb * 512:(nb + 1) * 512], op=ALU.add),
                    reads=[bpm, b_badar], writes=[b_mrow])
            for jg in range(96):
                T.op("pe", lambda e, jg=jg: e.transpose(out=PM[:, jg, :], in_=modrow[0:2, jg * 128:(jg + 1) * 128],
                     identity=identf[0:2, 0:2]), reads=[b_mrow, b_id], writes=[b_pm])
            T.op("dve", lambda e: e.tensor_copy(out=modT[:], in_=PM[:]), reads=[b_pm], writes=[b_mod])
            T.op("dve", lambda e: e.scalar_tensor_tensor(out=dcol[:, 0:16], in0=modT[:, 16:32, 0], scalar=1.0,
                 in1=cols[:, C_G1:C_G1 + 16], op0=ALU.add, op1=ALU.mult), reads=[b_mod, b_cols], writes=[b_dcol])
            T.op("dve", lambda e: e.scalar_tensor_tensor(out=dcol[:, 16:32], in0=modT[:, 16:32, 1], scalar=1.0,
                 in1=cols[:, C_G1:C_G1 + 16], op0=ALU.add, op1=ALU.mult), reads=[b_mod, b_cols], writes=[b_dcol])
            T.op("dve", lambda e: e.scalar_tensor_tensor(out=dcol[:, 32:48], in0=modT[:, 64:80, 0], scalar=1.0,
                 in1=cols[:, C_G2:C_G2 + 16], op0=ALU.add, op1=ALU.mult), reads=[b_mod, b_cols], writes=[b_dcol])
            T.op("dve", lambda e: e.tensor_scalar(out=dcol[:, 48:49], in0=cols[:, C_GQ:C_GQ + 1],
                 scalar1=float(128 ** -0.5), scalar2=None, op0=ALU.mult), reads=[b_cols], writes=[b_dcol])
            T.op("dve", lambda e: e.tensor_copy(out=dcol[:, 49:50], in_=cols[:, C_GK:C_GK + 1]),
                 reads=[b_cols], writes=[b_dcol])
            n = 0
            for gi, base in enumerate((32, 80)):
                for nb in range(4):
                    pg = PG[n % 2]; bpg = b_pg[n % 2]; n += 1
                    c0 = base * 128 + nb * 512
                    T.op("pe", lambda e, pg=pg, c0=c0: e.matmul(
                        pg[:], lhsT=onesf[0:1, :], rhs=modrow[0:1, c0:c0 + 512], start=True, stop=True),
                        reads=[b_ones, b_mrow], writes=[bpg])
                    T.op("act", lambda e, pg=pg, gi=gi, nb=nb: e.activation(
                        out=Grow[:, gi, nb * 512:(nb + 1) * 512], in_=pg[:], func=AF.Copy),
                        reads=[bpg], writes=[b_grow])
        T.barrier()
        if phases <= 1:
            dma("sp", dbg_s[:, 0:192], modT[:].rearrange("p a b -> p (a b)"), reads=[b_mod], final=True)
            dma("sp", dbg_s[:, 192:242], dcol[:], reads=[b_dcol], final=True)
            dma("sp", y_d[0:128, :], Grow[:, 0, :], reads=[b_grow], final=True)
            dma("sp", y_d[128:256, :], Grow[:, 1, :], reads=[b_grow], final=True)
            T.emit()
            return nc

        class WStream:
            def __init__(self, es, pref, wview, cast_eng="pool"):
                self.cast_eng = cast_eng
                self.st = [SB(es, pref + "wst%d" % i, [128, 16, 128], F32) for i in range(2)]
                self.b_st = [Buf(), Buf()]
                self.wb = [SB(es, pref + "wcb%d" % i, [128, 16, 128], BF16) for i in range(2)]
                self.b_wb = [Buf(), Buf()]
                self.n = 0
                self.wview = wview

            def get(self, col0):
                j = self.n % 2; self.n += 1
                st = self.st[j]; bst = self.b_st[j]; wb = self.wb[j]; bwb = self.b_wb[j]
                dma("sp", st[:], self.wview[:, :, col0:col0 + 128], writes=[bst])
                if self.cast_eng == "act":
                    T.op("act", lambda e: e.activation(out=wb[:], in_=st[:], func=AF.Copy), reads=[bst], writes=[bwb])
                else:
                    T.op("pool", lambda e: e.tensor_copy(out=wb[:], in_=st[:]), reads=[bst], writes=[bwb])
                return wb, bwb

        def norm_mod_T(es, pref, ntiles, load_fn, scol_fn, shcol_fn, hT_fn, tok0_fn, after_fn=None):
            xt = [SB(es, pref + "xt%d" % i, [128, D], F32) for i in range(2)]; b_xt = [Buf(), Buf()]
            xn = [SB(es, pref + "xn%d" % i, [128, D], F32) for i in range(2)]; b_xn = [Buf(), Buf()]
            junk = SB(es, pref + "junk", [128, D], BF16); b_junk = Buf()
            st = [SB(es, pref + "st%d" % i, [128, 4], F32) for i in range(2)]; b_st = [Buf(), Buf()]
            PT = [PS(es, pref + "PT%d" % i, [128, 512]) for i in range(2)]; b_pt = [Buf(), Buf()]
            k = 0
            load_fn(0, xt[0], b_xt[0], xn[0], b_xn[0])
            for i in range(ntiles):
                x_t = xt[i % 2]; bx = b_xt[i % 2]; xn_t = xn[i % 2]; bxn = b_xn[i % 2]
                s_t = st[i % 2]; bs = b_st[i % 2]
                T.op("act", lambda e, x_t=x_t, s_t=s_t: e.activation(out=junk[:], in_=x_t[:], func=AF.Square,
                     accum_out=s_t[:, 0:1]), reads=[bx], writes=[b_junk, bs])
                T.op("act", lambda e, s_t=s_t: e.activation(out=s_t[:, 1:2], in_=s_t[:, 0:1], func=AF.Sqrt,
                     scale=1.0 / D, bias=epsc[:, 0:1]), reads=[bs, b_eps], writes=[bs])
                T.op("dve", lambda e, s_t=s_t: e.reciprocal(out=s_t[:, 2:3], in_=s_t[:, 1:2]), reads=[bs], writes=[bs])
                T.op("act", lambda e, x_t=x_t, xn_t=xn_t, s_t=s_t: e.activation(out=xn_t[:], in_=x_t[:], func=AF.Copy,
                     scale=s_t[:, 2:3]), reads=[bx, bs], writes=[bxn])
                if i + 1 < ntiles:
                    load_fn(i + 1, xt[(i + 1) % 2], b_xt[(i + 1) % 2], xn[(i + 1) % 2], b_xn[(i + 1) % 2])
                t0 = tok0_fn(i)
                hT, b_hT = hT_fn(i)
                for g4 in range(4):
                    p_t = PT[k % 2]; bp = b_pt[k % 2]; k += 1
                    for q in range(4):
                        kc = g4 * 4 + q
                        T.op("pe", lambda e, p_t=p_t, q=q, kc=kc, xn_t=xn_t: e.transpose(
                            out=p_t[:, q * 128:(q + 1) * 128], in_=xn_t[:, kc * 128:(kc + 1) * 128], identity=identf[:]),
                            reads=[bxn, b_id], writes=[bp])
                    for q in range(4):
                        kc = g4 * 4 + q
                        if pref != "b" or q % 2 == 0:
                            T.op("dve", lambda e, p_t=p_t, q=q, kc=kc, t0=t0, i=i, hT=hT: e.tensor_scalar(
                                out=hT[:, kc, t0:t0 + 128], in0=p_t[:, q * 128:(q + 1) * 128],
                                scalar1=scol_fn(i, kc), scalar2=shcol_fn(i, kc), op0=ALU.mult, op1=ALU.add),
                                reads=[bp, b_dcol, b_mod], writes=[b_hT])
                        else:
                            T.op("act", lambda e, p_t=p_t, q=q, kc=kc, t0=t0, i=i, hT=hT: e.activation(
                                out=hT[:, kc, t0:t0 + 128], in_=p_t[:, q * 128:(q + 1) * 128], func=AF.Identity,
                                scale=scol_fn(i, kc), bias=shcol_fn(i, kc)),
                                reads=[bp, b_dcol, b_mod], writes=[b_hT])
                if after_fn is not None:
                    after_fn(i, x_t, bx, xn_t, bxn)

        with ExitStack() as es:
            hT = SB(es, "hT", [128, 16, TA], BF16); b_hT = Buf()
            with ExitStack() as es2:
                def load1(i, x_t, bx, xn_t, bxn):
                    if i < 16:
                        dma("sp", x_t[:], x_d[i * 128:(i + 1) * 128, :], writes=[bx])
                    else:
                        dma("sp", x_t[:], ctx_d[(i - 16) * 128:(i - 15) * 128, :], writes=[bx])
                norm_mod_T(es2, "b", 18, load1,
                           lambda i, kc: dcol[:, kc:kc + 1] if i < 16 else dcol[:, 16 + kc:17 + kc],
                           lambda i, kc: modT[:, kc, 0:1] if i < 16 else modT[:, kc, 1:2],
                           lambda i: (hT, b_hT), lambda i: i * 128)
            T.barrier()
            winv = win_d.rearrange("(kc p) n -> p kc n", p=128)
            WS = WStream(es, "c", winv, cast_eng="act")
            cosT = SB(es, "cosT", [128, TL], F32); b_cos = Buf()
            sinT = SB(es, "sinT", [128, TL], F32); b_sin = Buf()
            rotT = SB(es, "rotT", [128, 128], F32); b_rot = Buf()
            dma("sp", cosT[:], cosT_d, writes=[b_cos])
            dma("sp", sinT[:], sinT_d, writes=[b_sin])
            dma("sp", rotT[:], rotT_d, writes=[b_rot])
            NS = 3
            PQ = [PS(es, "PQ%d" % i, [128, 512]) for i in range(NS)]; b_pq = [Buf() for _ in range(NS)]
            PSSl = [PS(es, "PSS%d" % i, [128, 512]) for i in range(2)]; b_pssl = [Buf(), Buf()]
            PRl = [PS(es, "PR%d" % i, [128, 512]) for i in range(2)]; b_prl = [Buf(), Buf()]
            qg = [SB(es, "qg%d" % i, [128, 512], F32) for i in range(NS)]; b_qg = [Buf() for _ in range(NS)]
            sq = [SB(es, "sq%d" % i, [128, 512], F32) for i in range(NS)]; b_sq = [Buf() for _ in range(NS)]
            rs = [SB(es, "rs%d" % i, [128, 512], F32) for i in range(NS)]; b_rs = [Buf() for _ in range(NS)]
            t1 = [SB(es, "t1%d" % i, [128, 512], F32) for i in range(NS)]; b_t1 = [Buf() for _ in range(NS)]
            t2 = [SB(es, "t2%d" % i, [128, 512], F32) for i in range(NS)]; b_t2 = [Buf() for _ in range(NS)]
            stage = [SB(es, "stage%d" % i, [128, TA], BF16) for i in range(2)]; b_stage = [Buf(), Buf()]
            vsb = SB(es, "vsb", [128, 18, 256], BF16); b_vsb = Buf()
            nblk = 0
            pending = [None]

            def make_chain(j, n, t0, rope, stg, bstg, fc, last, nb_):
                PSS = PSSl[nb_ % 2]; b_pss = b_pssl[nb_ % 2]; PR = PRl[nb_ % 2]; b_pr = b_prl[nb_ % 2]

                def chain():
                    T.op("pe", lambda e: e.matmul(PSS[:, 0:n], lhsT=onesm[:], rhs=sq[j][:, 0:n],
                         start=True, stop=True), reads=[b_onesm, b_sq[j]], writes=[b_pss])
                    if rope:
                        T.op("pe", lambda e: e.matmul(PR[:, 0:n], lhsT=rotT[:], rhs=qg[j][:, 0:n],
                             start=True, stop=True), reads=[b_rot, b_qg[j]], writes=[b_pr])
                    T.op("act", lambda e: e.activation(out=rs[j][:, 0:n], in_=PSS[:, 0:n], func=AF.Ln,
                         bias=epsc[:, 0:1]), reads=[b_pss, b_eps], writes=[b_rs[j]])
                    T.op("act", lambda e: e.activation(out=rs[j][:, 0:n], in_=rs[j][:, 0:n], func=AF.Exp, scale=-0.5),
                         reads=[b_rs[j]], writes=[b_rs[j]])
                    if rope:
                        T.op("pool", lambda e: e.tensor_tensor(out=t1[j][:, 0:n], in0=qg[j][:, 0:n],
                             in1=cosT[:, t0:t0 + n], op=ALU.mult), reads=[b_qg[j], b_cos], writes=[b_t1[j]])
                        T.op("dve", lambda e: e.tensor_tensor(out=t2[j][:, 0:n], in0=PR[:, 0:n],
                             in1=sinT[:, t0:t0 + n], op=ALU.mult), reads=[b_pr, b_sin], writes=[b_t2[j]])
                        T.op("pool", lambda e: e.tensor_tensor(out=t1[j][:, 0:n], in0=t1[j][:, 0:n],
                             in1=t2[j][:, 0:n], op=ALU.add), reads=[b_t1[j], b_t2[j]], writes=[b_t1[j]])
                        T.op("dve", lambda e: e.tensor_tensor(out=stg[:, t0:t0 + n],
                             in0=t1[j][:, 0:n], in1=rs[j][:, 0:n], op=ALU.mult),
                             reads=[b_t1[j], b_rs[j]], writes=[bstg])
                    else:
                        T.op("dve", lambda e: e.tensor_tensor(out=stg[:, t0:t0 + n],
                             in0=qg[j][:, 0:n], in1=rs[j][:, 0:n], op=ALU.mult),
                             reads=[b_qg[j], b_rs[j]], writes=[bstg])
                    if last:
                        if fc < 8:
                            dma("sp", qT_s[fc], stg[:, 0:TL], reads=[bstg])
                        else:
                            dma("sp", kT_s[fc - 8], stg[:, :], reads=[bstg])
                return chain

            for fc in range(10):
                stg = stage[fc % 2]; bstg = b_stage[fc % 2]
                gcol = dcol[:, 48:49] if fc < 8 else dcol[:, 49:50]
                blocks = [(i * 512, 512, True) for i in range(4)]
                if fc >= 8:
                    blocks.append((TL, 256, False))
                win, b_win = WS.get(fc * 128)
                for bi_, (t0, n, rope) in enumerate(blocks):
                    j = nblk % NS; nb_ = nblk; nblk += 1
                    pq = PQ[j]; bpq = b_pq[j]
                    for kc in range(16):
                        T.op("pe", lambda e, pq=pq, kc=kc, win=win, t0=t0, n=n: e.matmul(
                            pq[:, 0:n], lhsT=win[:, kc, :], rhs=hT[:, kc, t0:t0 + n],
                            start=(kc == 0), stop=(kc == 15)), reads=[b_win, b_hT], writes=[bpq])
                    if pending[0] is not None:
                        pending[0]()
                    T.op("act", lambda e, j=j, pq=pq, n=n, gcol=gcol: e.activation(
                        out=qg[j][:, 0:n], in_=pq[:, 0:n], func=AF.Copy, scale=gcol),
                        reads=[bpq, b_dcol], writes=[b_qg[j]])
                    T.op("act", lambda e, j=j, pq=pq, n=n: e.activation(
                        out=sq[j][:, 0:n], in_=pq[:, 0:n], func=AF.Square), reads=[bpq], writes=[b_sq[j]])
                    pending[0] = make_chain(j, n, t0, rope, stg, bstg, fc, bi_ == len(blocks) - 1, nb_)
            pending[0]()
            nv = 0
            for vc in range(2):
                win, b_win = WS.get(1280 + vc * 128)
                for tile in range(18):
                    pv = PQ[nv % NS]; bpv = b_pq[nv % NS]; nv += 1
                    for kc in range(16):
                        T.op("pe", lambda e, pv=pv, kc=kc, tile=tile, win=win: e.matmul(
                            pv[:, 0:128], lhsT=hT[:, kc, tile * 128:(tile + 1) * 128], rhs=win[:, kc, :],
                            start=(kc == 0), stop=(kc == 15)), reads=[b_hT, b_win], writes=[bpv])
                    T.op("act", lambda e, pv=pv, tile=tile, vc=vc: e.activation(
                        out=vsb[:, tile, vc * 128:(vc + 1) * 128], in_=pv[:, 0:128], func=AF.Copy),
                        reads=[bpv], writes=[b_vsb])
            dma("sp", v_s, vsb[:], reads=[b_vsb])
            for fc in range(8):
                stg = stage[fc % 2]; bstg = b_stage[fc % 2]
                win, b_win = WS.get(1536 + fc * 128)
                for blk in range(4):
                    j = nblk % NS; nblk += 1
                    pq = PQ[j]; bpq = b_pq[j]
                    for kc in range(16):
                        T.op("pe", lambda e, pq=pq, kc=kc, win=win, blk=blk: e.matmul(
                            pq[:], lhsT=win[:, kc, :],
                            rhs=hT[:, kc, blk * 512:(blk + 1) * 512], start=(kc == 0), stop=(kc == 15)),
                            reads=[b_win, b_hT], writes=[bpq])
                    T.op("act", lambda e, pq=pq, blk=blk, stg=stg: e.activation(
                        out=stg[:, blk * 512:(blk + 1) * 512], in_=pq[:], func=AF.Copy), reads=[bpq], writes=[bstg])
                dma("sp", fT_s[fc], stg[:, 0:TL], reads=[bstg])
        T.barrier()
        if phases <= 2:
            dma("sp", y_d[0:128, :], Grow[:, 0, :], reads=[b_grow], final=True)
            T.emit()
            return nc

        b_qTs = Buf(); b_kTs = Buf(); b_vs = Buf(); b_fTs = Buf(); b_mixs = Buf()
        with ExitStack() as es:
            kT = SB(es, "kT", [128, TA], BF16); b_kT = Buf()
            vg = SB(es, "vg", [128, 18, 128], BF16); b_vg = Buf()
            qTh = [SB(es, "qTh%d" % i, [128, TL], BF16) for i in range(2)]; b_qTh = [Buf(), Buf()]
            NR = 4
            LA = 2
            pT = [SB(es, "pT%d" % i, [128, 512], BF16) for i in range(NR)]; b_pT = [Buf() for _ in range(NR)]
            ost = [SB(es, "ost%d" % i, [128, TL], BF16) for i in range(2)]; b_ost = [Buf(), Buf()]
            rsum = SB(es, "rsum", [128, 512], F32); b_rsum = Buf()
            PSt = [PS(es, "PSt%d" % i, [128, 512]) for i in range(NR)]; b_pst = [Buf() for _ in range(NR)]
            PO = [PS(es, "PO%d" % i, [128, 512]) for i in range(2)]; b_po = [Buf(), Buf()]
            PZ = [PS(es, "PZ%d" % i, [128, 512]) for i in range(2)]; b_pz = [Buf(), Buf()]
            steps = [(h // 4, h, qb, kt) for h in range(8) for qb in range(4) for kt in range(18)]
            nst = len(steps)
            for idx in range(nst + LA):
                if idx < nst:
                    g, h, qb, kt = steps[idx]
                    q_t = qTh[h % 2]; bq = b_qTh[h % 2]
                    if qb == 0 and kt == 0:
                        if h % 4 == 0:
                            dma("sp", kT[:], kT_s[g], writes=[b_kT])
                        dma("sp", q_t[:], qT_s[h], writes=[bq])
                    s_p = PSt[idx % NR]; bsp = b_pst[idx % NR]; p_t = pT[idx % NR]; bpt = b_pT[idx % NR]
                    T.op("pe", lambda e, s_p=s_p, kt=kt, q_t=q_t, qb=qb: e.matmul(
                        s_p[:], lhsT=kT[:, kt * 128:(kt + 1) * 128], rhs=q_t[:, qb * 512:(qb + 1) * 512],
                        start=True, stop=True), reads=[b_kT, bq], writes=[bsp])
                    T.op("act", lambda e, s_p=s_p, p_t=p_t: e.activation(out=p_t[:], in_=s_p[:], func=AF.Exp),
                         reads=[bsp], writes=[bpt])
                j = idx - LA
                if j >= 0:
                    g, h, qb, kt = steps[j]
                    u = h * 4 + qb
                    po = PO[u % 2]; bpo = b_po[u % 2]; pz = PZ[u % 2]; bpz = b_pz[u % 2]
                    o_t = ost[h % 2]; bo = b_ost[h % 2]
                    pp_t = pT[j % NR]; pbpt = b_pT[j % NR]
                    if h % 4 == 0 and qb == 0 and kt == 0:
                        dma("sp", vg[:], v_s[:, :, g * 128:(g + 1) * 128], writes=[b_vg])
                    T.op("pe", lambda e, po=po, kt=kt, pp_t=pp_t: e.matmul(
                        po[:], lhsT=vg[:, kt, :], rhs=pp_t[:], start=(kt == 0), stop=(kt == 17)),
                        reads=[b_vg, pbpt], writes=[bpo])
                    T.op("pe", lambda e, pz=pz, kt=kt, pp_t=pp_t: e.matmul(
                        pz[:], lhsT=onesb[:], rhs=pp_t[:], start=(kt == 0), stop=(kt == 17)),
                        reads=[b_onesb, pbpt], writes=[bpz])
                    if kt == 17:
                        T.op("dve", lambda e, pz=pz: e.reciprocal(out=rsum[:], in_=pz[:]), reads=[bpz], writes=[b_rsum])
                        T.op("dve", lambda e, po=po, o_t=o_t, qb=qb: e.tensor_tensor(
                            out=o_t[:, qb * 512:(qb + 1) * 512], in0=po[:], in1=rsum[:], op=ALU.mult),
                            reads=[bpo, b_rsum], writes=[bo])
                        if qb == 3:
                            dma("sp", mixT_s[h], o_t[:], reads=[bo])
        T.barrier()

        with ExitStack() as es:
            ccsc = SB(es, "ccsc", [128, 2, 512], F32); b_ccsc = Buf()
            wf = SB(es, "wf", [128, 4, 2, 256], F32); b_wf = Buf()
            AB = SB(es, "AB", [128, 4, 2, 512], BF16); b_AB = Buf()
            UV = SB(es, "UV", [128, 4, 16, 512], BF16); b_UV = Buf()
            fTt = [SB(es, "fTt%d" % i, [128, TL], BF16) for i in range(4)]; b_fTt = [Buf() for _ in range(4)]
            tabC = [SB(es, "tabC%d" % i, [128, 16, 512], BF16) for i in range(1)]; b_tabC = [Buf()]
            tabS = [SB(es, "tabS%d" % i, [128, 16, 512], BF16) for i in range(1)]; b_tabS = [Buf()]
            yst = [SB(es, "yst%d" % i, [128, 512], BF16) for i in range(2)]; b_yst = [Buf() for _ in range(2)]
            PA = [PS(es, "PA%d" % i, [128, 512]) for i in range(2)]; b_pa = [Buf(), Buf()]
            PU = [PS(es, "PU%d" % i, [128, 512]) for i in range(2)]; b_pu = [Buf(), Buf()]
            PY = [PS(es, "PY%d" % i, [128, 512]) for i in range(2)]; b_py = [Buf(), Buf()]
            dma("sp", ccsc[:], ccsc_d.rearrange("(c p) n -> p c n", p=128), writes=[b_ccsc])
            dma("sp", wf[:], wf_d.rearrange("g (c p) n -> p g c n", p=128), writes=[b_wf])
            na = 0
            for g in range(4):
                for mc in range(2):
                    pa = PA[na % 2]; bpa = b_pa[na % 2]; na += 1
                    for cs in range(2):
                        for kc in range(2):
                            T.op("pe", lambda e, pa=pa, cs=cs, kc=kc, mc=mc, g=g: e.matmul(
                                pa[:, cs * 256:(cs + 1) * 256],
                                lhsT=ccsc[:, kc, cs * 256 + mc * 128:cs * 256 + (mc + 1) * 128],
                                rhs=wf[:, g, kc, :], start=(kc == 0), stop=(kc == 1)),
                                reads=[b_ccsc, b_wf], writes=[bpa])
                    T.op("act", lambda e, pa=pa, g=g, mc=mc: e.activation(out=AB[:, g, mc, :], in_=pa[:], func=AF.Copy),
                         reads=[bpa], writes=[b_AB])
            nu = 0
            for g in range(4):
                for cc in range(2):
                    dma("sp", fTt[(g % 2) * 2 + cc][:], fT_s[g * 2 + cc], writes=[b_fTt[(g % 2) * 2 + cc]])
                for tt in range(16):
                    pu = PU[nu % 2]; bpu = b_pu[nu % 2]; nu += 1
                    for cc in range(2):
                        f_t = fTt[(g % 2) * 2 + cc]; bf = b_fTt[(g % 2) * 2 + cc]
                        T.op("pe", lambda e, pu=pu, f_t=f_t, tt=tt, g=g, cc=cc: e.matmul(
                            pu[:], lhsT=f_t[:, tt * 128:(tt + 1) * 128], rhs=AB[:, g, cc, :],
                            start=(cc == 0), stop=(cc == 1)), reads=[bf, b_AB], writes=[bpu])
                    T.op("act", lambda e, pu=pu, g=g, tt=tt: e.activation(out=UV[:, g, tt, :], in_=pu[:], func=AF.Copy),
                         reads=[bpu], writes=[b_UV])
            ny = 0
            CLv = CL_d.rearrange("(n p) k -> p n k", p=128)
            SLv = nSL_d.rearrange("(n p) k -> p n k", p=128)
            for kb in range(4):
                tc_t = tabC[0]; btc = b_tabC[0]; ts_t = tabS[0]; bts = b_tabS[0]
                dma("sp", tc_t[:], CLv[:, :, kb * 512:(kb + 1) * 512], writes=[btc])
                dma("sp", ts_t[:], SLv[:, :, kb * 512:(kb + 1) * 512], writes=[bts])
                for g in range(4):
                    for dc in range(2):
                        py = PY[ny % 2]; bpy = b_py[ny % 2]; ny += 1
                        for n_ in range(16):
                            T.op("pe", lambda e, py=py, g=g, n_=n_, dc=dc, tc_t=tc_t: e.matmul(
                                py[:], lhsT=UV[:, g, n_, dc * 128:(dc + 1) * 128], rhs=tc_t[:, n_, :],
                                start=(n_ == 0), stop=False), reads=[b_UV, btc], writes=[bpy])
                        for n_ in range(16):
                            T.op("pe", lambda e, py=py, g=g, n_=n_, dc=dc, ts_t=ts_t: e.matmul(
                                py[:], lhsT=UV[:, g, n_, 256 + dc * 128:256 + (dc + 1) * 128], rhs=ts_t[:, n_, :],
                                start=False, stop=(n_ == 15)), reads=[b_UV, bts], writes=[bpy])
                        ci = g * 2 + dc
                        ys = yst[ny % 2]; bys = b_yst[ny % 2]
                        T.op("act", lambda e, py=py, ci=ci, ys=ys: e.activation(
                            out=ys[:], in_=py[:], func=AF.Identity,
                            bias=cols[:, C_BF + ci:C_BF + ci + 1]), reads=[bpy, b_cols], writes=[bys])
                        dma("sp", mixT_s[8 + ci][:, kb * 512:(kb + 1) * 512], ys[:], reads=[bys])
        T.barrier()
        if phases <= 3:
            dma("sp", y_d[0:128, :], Grow[:, 0, :], reads=[b_grow], final=True)
            T.emit()
            return nc

        with ExitStack() as es:
            wout = SB(es, "wout", [128, 16, D], BF16); b_woutl = [Buf() for _ in range(4)]
            woutv = wout_d.rearrange("(kc p) n -> p kc n", p=128)
            for nb_ in range(4):
                dma("pool", wout[:, :, nb_ * 512:(nb_ + 1) * 512], woutv[:, :, nb_ * 512:(nb_ + 1) * 512], writes=[b_woutl[nb_]])
            h2blk = [SB(es, "h2blk%d" % i, [128, 16, 512], BF16) for i in range(2)]; b_h2blk = [Buf(), Buf()]
            mixb = [SB(es, "mixb%d" % i, [128, 16, 512], BF16) for i in range(2)]; b_mixb = [Buf(), Buf()]
            PO4 = PS(es, "PO4", [128, D]); b_po4 = [Buf(), Buf()]

            def loadF(i, x_t, bx, xn_t, bxn):
                blk = i // 4
                mb = mixb[blk % 2]; bmb = b_mixb[blk % 2]
                if i % 4 == 0:
                    dma("sp", mb[:], mixT_s[:, :, blk * 512:(blk + 1) * 512].rearrange("c p t -> p c t"), writes=[bmb])
                dma("sp", x_t[:], x_d[i * 128:(i + 1) * 128, :], writes=[bx])
                tl = (i % 4) * 128
                for nb in range(4):
                    for kc in range(16):
                        T.op("pe", lambda e, nb=nb, kc=kc, mb=mb, tl=tl: e.matmul(
                            PO4[:, nb * 512:(nb + 1) * 512], lhsT=mb[:, kc, tl:tl + 128],
                            rhs=wout[:, kc, nb * 512:(nb + 1) * 512], start=(kc == 0), stop=(kc == 15)),
                            reads=[bmb, b_woutl[nb]], writes=[b_po4[nb // 2]])
                for hf in range(2):
                    T.op("dve", lambda e, xn_t=xn_t, hf=hf: e.tensor_tensor(out=xn_t[:, hf * 1024:(hf + 1) * 1024],
                         in0=PO4[:, hf * 1024:(hf + 1) * 1024], in1=Grow[:, 0, hf * 1024:(hf + 1) * 1024], op=ALU.mult),
                         reads=[b_po4[hf], b_grow], writes=[bxn])
                T.op("pool", lambda e, x_t=x_t, xn_t=xn_t: e.tensor_tensor(out=x_t[:], in0=x_t[:], in1=xn_t[:], op=ALU.add),
                     reads=[bx, bxn], writes=[bx])
                dma("sp", x1_s[i * 128:(i + 1) * 128, :], x_t[:], reads=[bx])

            def afterF(i, x_t, bx, xn_t, bxn):
                if i % 4 == 3:
                    blk = i // 4
                    dma("sp", h2T_s[:, :, blk * 512:(blk + 1) * 512], h2blk[blk % 2][:], reads=[b_h2blk[blk % 2]])

            norm_mod_T(es, "f", 16, loadF,
                       lambda i, kc: dcol[:, 32 + kc:33 + kc],
                       lambda i, kc: modT[:, 48 + kc, 0:1],
                       lambda i: (h2blk[(i // 4) % 2], b_h2blk[(i // 4) % 2]), lambda i: (i % 4) * 128, afterF)
        T.barrier()
        if phases <= 4:
            dma("sp", y_d[0:128, :], Grow[:, 0, :], reads=[b_grow], final=True)
            T.emit()
            return nc

        es12 = ExitStack()
        acolT_all = SB(es12, "acolT", [128, TL], F32); b_acolT = Buf()
        thrT_all = SB(es12, "thrT", [128, TL], F32); b_thrT = Buf()
        wT_all = SB(es12, "wT", [128, TL], F32); b_wTc = Buf()
        with ExitStack() as es:
            WQ = WStream(es, "q", wq_d.rearrange("(kc p) n -> p kc n", p=128), cast_eng="act")
            skT = SB(es, "skT", [128, 16, 128], BF16); b_skT = Buf()
            dma("pool", skT[:], skT_d.rearrange("c d j -> d c j"), writes=[b_skT])
            h2b = [SB(es, "h2b%d" % i, [128, 16, 512], BF16) for i in range(1)]; b_h2b = [Buf()]
            qTb2 = [SB(es, "qTb%d" % i, [128, 16, 512], BF16) for i in range(2)]; b_qTb2 = [Buf(), Buf()]
            PQ2 = [PS(es, "PQ2%d" % i, [128, 512]) for i in range(2)]; b_pq2 = [Buf(), Buf()]
            PSc = PS(es, "PSc", [128, D]); b_psc = Buf()
            PTc = PS(es, "PTc", [128, 512]); b_ptc = Buf()
            Ssb = [SB(es, "Ssb%d" % i, [128, D], F32) for i in range(2)]; b_Ssb = [Buf(), Buf()]
            Swk = SB(es, "Swk", [128, D], F32); b_Swk = Buf()
            top = SB(es, "top", [128, 16, 16], F32); b_topc = [Buf() for _ in range(16)]
            idxu = SB(es, "idxu", [128, 8, 16], mybir.dt.uint32); b_idxu = [Buf() for _ in range(8)]
            b_Swkc = [Buf() for _ in range(16)]; b_ctoph = [Buf() for _ in range(8)]; b_cand2h = [Buf() for _ in range(8)]
            cand = SB(es, "cand", [128, 8, 256], F32); b_cand = Buf()
            cand2 = SB(es, "cand2", [128, 8, 256], F32); b_cand2 = Buf()
            ctop = SB(es, "ctop", [128, 8, 16], F32); b_ctop = Buf()
            e16 = SB(es, "e16", [128, 8, 16], F32); b_e16 = Buf()
            zz = SB(es, "zz", [128, 8, 4], F32); b_zz = Buf()
            tm2 = [SB(es, "tm%d" % i, [128, 3, 128], F32) for i in range(2)]; b_tm2 = [Buf(), Buf()]
            nq2 = [0]
            S_flat = S_s.ap().rearrange("t h p j -> t (h p j)")
            hb = h2b[0]; bhb = b_h2b[0]

            def qproj(blk):
                qT_t = qTb2[blk % 2]; bqT = b_qTb2[blk % 2]
                dma("sp", hb[:], h2T_s[:, :, blk * 512:(blk + 1) * 512], writes=[bhb])
                for c in range(16):
                    pq = PQ2[nq2[0] % 2]; bpq = b_pq2[nq2[0] % 2]; nq2[0] += 1
                    wq, b_wq = WQ.get(c * 128)
                    for kc in range(16):
                        T.op("pe", lambda e, pq=pq, kc=kc, wq=wq: e.matmul(
                            pq[:], lhsT=wq[:, kc, :], rhs=hb[:, kc, :],
                            start=(kc == 0), stop=(kc == 15)), reads=[b_wq, bhb], writes=[bpq])
                    T.op("act", lambda e, pq=pq, c=c, qT_t=qT_t: e.activation(out=qT_t[:, c, :], in_=pq[:], func=AF.Copy),
                         reads=[bpq], writes=[bqT])

            pend_tr = [None]
            qproj(0)
            for blk in range(4):
                qTb = qTb2[blk % 2]; b_qTb = b_qTb2[blk % 2]
                if blk + 1 < 4:
                    qproj(blk + 1)
                for tl in range(4):
                    ti = blk * 4 + tl
                    S_t = Ssb[ti % 2]; bS = b_Ssb[ti % 2]
                    tm = tm2[ti % 2]; b_tm = b_tm2[ti % 2]
                    for c in range(16):
                        T.op("pe", lambda e, c=c, tl=tl, qTb=qTb: e.matmul(
                            PSc[:, c * 128:(c + 1) * 128], lhsT=qTb[:, c, tl * 128:(tl + 1) * 128], rhs=skT[:, c, :],
                            start=True, stop=True), reads=[b_qTb, b_skT], writes=[b_psc])
                    if pend_tr[0] is not None:
                        pend_tr[0]()
                        pend_tr[0] = None
                    T.op("act", lambda e, S_t=S_t: e.activation(out=S_t[:], in_=PSc[:], func=AF.Copy),
                         reads=[b_psc], writes=[bS])
                    dma("sp", S_flat[ti * 128:(ti + 1) * 128, :], S_t[:], reads=[bS])
                    for c in range(16):
                        sl = slice(c * 128, (c + 1) * 128)
                        T.op("dve", lambda e, c=c, sl=sl, S_t=S_t: e.max(out=top[:, c, 0:8], in_=S_t[:, sl]),
                             reads=[bS], writes=[b_topc[c]])
                    for c in range(16):
                        sl = slice(c * 128, (c + 1) * 128)
                        T.op("dve", lambda e, c=c, sl=sl, S_t=S_t: e.match_replace(
                            out=Swk[:, sl], in_to_replace=top[:, c, 0:8], in_values=S_t[:, sl], imm_value=NEG),
                            reads=[bS, b_topc[c]], writes=[b_Swkc[c]])
                    for c in range(16):
                        sl = slice(c * 128, (c + 1) * 128)
                        T.op("dve", lambda e, c=c, sl=sl: e.max(out=top[:, c, 8:16], in_=Swk[:, sl]),
                             reads=[b_Swkc[c]], writes=[b_topc[c]])
                    for h in range(8):
                        c = 2 * h
                        sl = slice(c * 128, (c + 1) * 128)
                        T.op("dve", lambda e, c=c, h=h, sl=sl, S_t=S_t: e.max_index(out=idxu[:, h, 0:8], in_max=top[:, c, 0:8],
                             in_values=S_t[:, sl]), reads=[bS, b_topc[c]], writes=[b_idxu[h]])
                        T.op("dve", lambda e, c=c, h=h, sl=sl: e.max_index(out=idxu[:, h, 8:16], in_max=top[:, c, 8:16],
                             in_values=Swk[:, sl]), reads=[b_Swkc[c], b_topc[c]], writes=[b_idxu[h]])
                    tv = top[:].rearrange("t (h p) k -> t h p k", p=2)
                    a_bc = tv[:, :, 0, :].unsqueeze(3).broadcast_to([128, 8, 16, 16])
                    b_bc = tv[:, :, 1, :].unsqueeze(2).broadcast_to([128, 8, 16, 16])
                    T.op("dve", lambda e, a_bc=a_bc, b_bc=b_bc: e.tensor_tensor(
                        out=cand[:].rearrange("t h (k l) -> t h k l", l=16), in0=a_bc, in1=b_bc, op=ALU.add),
                        reads=b_topc, writes=[b_cand])
                    for h in range(8):
                        T.op("dve", lambda e, h=h: e.max(out=ctop[:, h, 0:8], in_=cand[:, h, :]),
                             reads=[b_cand], writes=[b_ctoph[h]])
                    for h in range(8):
                        T.op("dve", lambda e, h=h: e.match_replace(out=cand2[:, h, :], in_to_replace=ctop[:, h, 0:8],
                             in_values=cand[:, h, :], imm_value=NEG), reads=[b_cand, b_ctoph[h]], writes=[b_cand2h[h]])
                    for h in range(8):
                        T.op("dve", lambda e, h=h: e.max(out=ctop[:, h, 8:16], in_=cand2[:, h, :]),
                             reads=[b_cand2h[h]], writes=[b_ctoph[h]])
                    b_top = None
                    T.op("dve", lambda e: e.tensor_tensor(out=e16[:], in0=ctop[:], in1=ctop[:, :, 0:1].broadcast_to([128, 8, 16]),
                         op=ALU.subtract), reads=b_ctoph, writes=[b_e16])
                    T.op("act", lambda e: e.activation(out=e16[:], in_=e16[:], func=AF.Exp), reads=[b_e16], writes=[b_e16])
                    T.op("dve", lambda e: e.tensor_reduce(out=zz[:, :, 1], in_=e16[:], op=ALU.add, axis=AX.X),
                         reads=[b_e16], writes=[b_zz])
                    T.op("act", lambda e: e.activation(out=zz[:, :, 2], in_=zz[:, :, 1], func=AF.Ln), reads=[b_zz], writes=[b_zz])
                    T.op("dve", lambda e: e.tensor_tensor(out=zz[:, :, 0], in0=zz[:, :, 2], in1=ctop[:, :, 0], op=ALU.add),
                         reads=[b_zz] + b_ctoph, writes=[b_zz])
                    a_v = tv[:, :, 0, :]
                    T.op("dve", lambda e, tm=tm: e.tensor_copy(out=tm[:, 0, :].rearrange("t (h k) -> t h k", k=16), in_=idxu[:]),
                         reads=b_idxu, writes=[b_tm])
                    T.op("dve", lambda e: e.tensor_tensor(out=cand2[:], in0=cand[:],
                         in1=ctop[:, :, 15:16].broadcast_to([128, 8, 256]), op=ALU.is_ge),
                         reads=[b_cand] + b_ctoph, writes=b_cand2h)
                    T.op("dve", lambda e: e.tensor_scalar(out=cand2[:], in0=cand2[:], scalar1=-1.0, scalar2=NEG,
                         op0=ALU.add, op1=ALU.mult), reads=b_cand2h, writes=b_cand2h)
                    T.op("dve", lambda e, b_bc=b_bc: e.tensor_tensor(out=cand2[:].rearrange("t h (k l) -> t h k l", l=16),
                         in0=cand2[:].rearrange("t h (k l) -> t h k l", l=16), in1=b_bc, op=ALU.add),
                         reads=b_cand2h + b_topc, writes=b_cand2h)
                    T.op("dve", lambda e, tm=tm: e.tensor_reduce(out=tm[:, 1, :].rearrange("t (h k) -> t h k", k=16),
                         in_=cand2[:].rearrange("t h (k l) -> t h k l", l=16), op=ALU.min, axis=AX.X),
                         reads=b_cand2h, writes=[b_tm])
                    T.op("dve", lambda e, a_v=a_v, tm=tm: e.tensor_tensor(out=tm[:, 2, :].rearrange("t (h k) -> t h k", k=16),
                         in0=a_v, in1=zz[:, :, 0:1].broadcast_to([128, 8, 16]), op=ALU.subtract),
                         reads=b_topc + [b_zz], writes=[b_tm])
                    T.op("act", lambda e, tm=tm: e.activation(out=tm[:, 2, :], in_=tm[:, 2, :], func=AF.Exp), reads=[b_tm], writes=[b_tm])
                    def mk_tr(ti, tm, b_tm):
                        def tr():
                            for k3 in range(3):
                                T.op("pe", lambda e, k3=k3: e.transpose(out=PTc[:, k3 * 128:(k3 + 1) * 128], in_=tm[:, k3, :],
                                     identity=identf[:]), reads=[b_tm, b_id], writes=[b_ptc])
                            T.op("act", lambda e: e.activation(out=acolT_all[:, ti * 128:(ti + 1) * 128], in_=PTc[:, 0:128],
                                 func=AF.Copy), reads=[b_ptc], writes=[b_acolT])
                            T.op("act", lambda e: e.activation(out=thrT_all[:, ti * 128:(ti + 1) * 128], in_=PTc[:, 128:256],
                                 func=AF.Copy), reads=[b_ptc], writes=[b_thrT])
                            T.op("act", lambda e: e.activation(out=wT_all[:, ti * 128:(ti + 1) * 128], in_=PTc[:, 256:384],
                                 func=AF.Copy), reads=[b_ptc], writes=[b_wTc])
                        return tr
                    pend_tr[0] = mk_tr(ti, tm, b_tm)
            pend_tr[0]()
        T.barrier()
        if phases <= 5:
            dma("sp", dbg_s[:, 0:2048], acolT_all[:], reads=[b_acolT], final=True)
            dma("sp", dbg_s[:, 2048:4096], wT_all[:], reads=[b_wTc], final=True)
            dma("sp", y_d[0:128, :], thrT_all[:], reads=[b_thrT], final=True)
            T.emit()
            es12.close()
            return nc

        TB = 16
        with ExitStack() as es:
            iot = SB(es, "iot", [128, 128], F32); b_iot = Buf()
            dma("sp", iot[:], iota_d, writes=[b_iot])
            iota3 = SB(es, "iota3", [128, 128, TB], BF16); b_iota3 = Buf()
            idxb = SB(es, "idxb", [128, TL], BF16); b_idxb = Buf()
            T.op("dve", lambda e: e.tensor_copy(out=iota3[:], in_=iot[:].unsqueeze(2).broadcast_to([128, 128, TB])),
                 reads=[b_iot], writes=[b_iota3])
            T.op("act", lambda e: e.activation(out=idxb[:], in_=acolT_all[:], func=AF.Copy), reads=[b_acolT], writes=[b_idxb])
            rep = [SB(es, "rep%d" % i, [128, TB, 128], F32) for i in range(3)]; b_rep = [Buf() for _ in range(3)]
            for i_ in range(3):
                T.op("pool", lambda e, i_=i_: e.memset(rep[i_][:], 0.0), writes=[b_rep[i_]])
            Eb = [SB(es, "Eb%d" % i, [128, TB, 128], BF16) for i in range(2)]
            b_Eb = [[Buf() for _ in range(TB)] for _ in range(2)]
            Mk = [SB(es, "Mk%d" % i, [128, TB, 128], BF16) for i in range(2)]
            b_Mk = [[Buf() for _ in range(TB)] for _ in range(2)]
            X0 = [SB(es, "X0%d" % i, [128, 128, TB], BF16) for i in range(3)]
            b_X0 = [[Buf() for _ in range(TB)] for _ in range(3)]
            Rt = [SB(es, "Rt%d" % i, [128, TB, 128], BF16) for i in range(3)]
            b_Rt = [[Buf() for _ in range(TB)] for _ in range(3)]
            Gbuf = [SB(es, "Gbuf%d" % i, [128, 128, 128], BF16) for i in range(2)]
            b_Gbuf = [[Buf() for _ in range(32)] for _ in range(2)]
            PGt = [PS(es, "PGt%d" % i, [128, 4, 128]) for i in range(8)]; b_pgt = [Buf() for _ in range(8)]
            npg = [0]
            iota_bc = iot[:].unsqueeze(1).broadcast_to([128, TB, 128])
            pend = []

            def make_back(ti, sb_, R_t, bR, x0, bx0, gb, bgb):
                def back():
                    for q4 in range(TB // 4):
                        pg = PGt[npg[0] % 8]; bpg = b_pgt[npg[0] % 8]; npg[0] += 1
                        for u in range(4):
                            tt = q4 * 4 + u
                            T.op("pe", lambda e, pg=pg, u=u, tt=tt: e.matmul(
                                pg[:, u, :], lhsT=R_t[:, tt, :], rhs=x0[:, :, tt], start=True, stop=True),
                                reads=[bR[tt], bx0[tt]], writes=[bpg])
                        tl0 = sb_ * TB + q4 * 4
                        T.op("act", lambda e, pg=pg, tl0=tl0: e.activation(
                            out=gb[:, :, tl0:tl0 + 4], in_=pg[:].rearrange("j u i -> j i u"), func=AF.Copy),
                            reads=[bpg], writes=[bgb[tl0 // 4]])
                    if sb_ == 128 // TB - 1:
                        dma("sp", G_s[:, :, ti * 128:(ti + 1) * 128].rearrange("i j t -> j i t"), gb[:], reads=bgb)
                return back

            for ti in range(16):
                gb = Gbuf[ti % 2]; bgb = b_Gbuf[ti % 2]
                for sb_ in range(128 // TB):
                    bi = ti * (128 // TB) + sb_
                    tok0 = ti * 128 + sb_ * TB
                    r_t = rep[bi % 3]; br = b_rep[bi % 3]
                    E_t = Eb[bi % 2]; bE = b_Eb[bi % 2]; x0 = X0[bi % 3]; bx0 = b_X0[bi % 3]
                    R_t = Rt[bi % 3]; bR = b_Rt[bi % 3]; m_t = Mk[bi % 2]; bM = b_Mk[bi % 2]
                    src = bass.AP(S_s, tok0 * 2048 + 128, [[256, 8], [2048, TB], [1, 128]])
                    dma("sp", r_t[0:128:16, :, :], src, writes=[br])
                    T.op("dve", lambda e, r_t=r_t: e.stream_shuffle(out=r_t[:], in_=r_t[:], mask=[0] * 16 + [16] * 16),
                         reads=[br], writes=[br])
                    T.op("act", lambda e, E_t=E_t, r_t=r_t: e.activation(out=E_t[:], in_=r_t[:], func=AF.Exp),
                         reads=[br], writes=bE)
                    idx_bc = idxb[:, tok0:tok0 + TB].unsqueeze(1).broadcast_to([128, 128, TB])
                    thr_bc = thrT_all[:, tok0:tok0 + TB].unsqueeze(2).broadcast_to([128, TB, 128])
                    T.op("dve", lambda e, x0=x0, idx_bc=idx_bc: e.tensor_tensor(out=x0[:], in0=iota3[:], in1=idx_bc,
                         op=ALU.is_equal), reads=[b_iota3, b_idxb], writes=bx0)
                    T.op("dve", lambda e, m_t=m_t, r_t=r_t, thr_bc=thr_bc: e.tensor_tensor(out=m_t[:], in0=r_t[:], in1=thr_bc,
                         op=ALU.is_ge), reads=[br, b_thrT], writes=bM)
                    w_bc = wT_all[:, tok0:tok0 + TB].unsqueeze(2).broadcast_to([128, TB, 128])
                    T.op("pool", lambda e, m_t=m_t, E_t=E_t: e.tensor_tensor(out=m_t[:], in0=m_t[:], in1=E_t[:],
                         op=ALU.mult), reads=bM + bE, writes=bM)
                    T.op("pool", lambda e, R_t=R_t, m_t=m_t, w_bc=w_bc: e.tensor_tensor(out=R_t[:], in0=m_t[:], in1=w_bc,
                         op=ALU.mult), reads=bM + [b_wTc], writes=bR)
                    pend.append(make_back(ti, sb_, R_t, bR, x0, bx0, gb, bgb))
                    if len(pend) > 2:
                        pend.pop(0)()
            while pend:
                pend.pop(0)()
        T.barrier()
        es12.close()
        if phases <= 6:
            dma("sp", y_d[0:128, :], Grow[:, 0, :], reads=[b_grow], final=True)
            T.emit()
            return nc

        NH = 2
        TH = TL // NH
        NT = TH // 128
        GE = 4
        with ExitStack() as es:
            gfr = SB(es, "gfr", [128, D], F32); b_gfr = Buf()
            dma("sp", gfr[:], gfin_d.ap().broadcast_to([128, D]), writes=[b_gfr])
            acc = SB(es, "acc", [128, NT, D], F32); b_acc = [Buf() for _ in range(NT)]
            h2h = SB(es, "h2h", [128, 16, TH], BF16); b_h2h = Buf()
            Wg = [SB(es, "Wg%d" % i, [128, GE, TH], BF16) for i in range(2)]; b_Wg = [Buf(), Buf()]
            UT = [SB(es, "UT%d" % i, [128, 16, 128], BF16) for i in range(2)]; b_UT = [Buf() for _ in range(2)]
            Vb = [SB(es, "Vb%d" % i, [128, D], BF16) for i in range(GE)]; b_Vb = [Buf() for _ in range(GE)]
            ga = [SB(es, "ga%d" % i, [128, TH], BF16) for i in range(2)]; b_ga = [Buf(), Buf()]
            x1t = [SB(es, "x1t%d" % i, [128, D], F32) for i in range(1)]; b_x1t = [Buf()]
            sth = [SB(es, "sth%d" % i, [128, 4], F32) for i in range(2)]; b_sth = [Buf(), Buf()]
            PAe = [PS(es, "PAe%d" % i, [128, TH]) for i in range(2)]; b_pae = [Buf(), Buf()]
            POe = [PS(es, "POe%d" % i, [128, 1024]) for i in range(2)]; b_poe = [Buf(), Buf()]
            nch = 0; nun = 0; neg = 0
            for half in range(NH):
                tb0 = half * TH
                dma("sp", h2h[:], h2T_s[:, :, tb0:tb0 + TH], writes=[b_h2h])
                for eg in range(128 // GE):
                    w_t = Wg[neg % 2]; bw = b_Wg[neg % 2]; neg += 1
                    dma("sp", w_t[:], G_s[eg * GE:(eg + 1) * GE, :, tb0:tb0 + TH].rearrange("i j t -> j i t"), writes=[bw])
                    for ii in range(GE):
                        i = eg * GE + ii
                        u_t = UT[nch % 2]; bu = b_UT[nch % 2]
                        v_t = Vb[ii]; bv = b_Vb[ii]
                        g_t = ga[nch % 2]; bg = b_ga[nch % 2]
                        pa = PAe[nch % 2]; bpa = b_pae[nch % 2]
                        nch += 1
                        dma("pool", u_t[:], uT_d[i].rearrange("p (kc e) -> p kc e", e=128), writes=[bu])
                        dma("pool", v_t[:], v_d[i * 128:(i + 1) * 128, :], writes=[bv])
                        for tb in range(TH // 512):
                            for kc in range(16):
                                T.op("pe", lambda e, pa=pa, tb=tb, kc=kc, u_t=u_t: e.matmul(
                                    pa[:, tb * 512:(tb + 1) * 512], lhsT=u_t[:, kc, :], rhs=h2h[:, kc, tb * 512:(tb + 1) * 512],
                                    start=(kc == 0), stop=(kc == 15)), reads=[bu, b_h2h], writes=[bpa])
                        T.op("act", lambda e, pa=pa, g_t=g_t: e.activation(out=g_t[:], in_=pa[:], func=AF.Gelu),
                             reads=[bpa], writes=[bg])
                        T.op("dve", lambda e, w_t=w_t, ii=ii, g_t=g_t: e.tensor_tensor(
                            out=w_t[:, ii, :], in0=w_t[:, ii, :], in1=g_t[:], op=ALU.mult), reads=[bw, bg], writes=[bw])
                    for tl in range(NT):
                        for dh in range(2):
                            po = POe[nun % 2]; bpo = b_poe[nun % 2]; nun += 1
                            for ii in range(GE):
                                v_t = Vb[ii]; bv = b_Vb[ii]
                                for nb in range(2):
                                    T.op("pe", lambda e, po=po, nb=nb, ii=ii, tl=tl, dh=dh, w_t=w_t, v_t=v_t: e.matmul(
                                        po[:, nb * 512:(nb + 1) * 512], lhsT=w_t[:, ii, tl * 128:(tl + 1) * 128],
                                        rhs=v_t[:, dh * 1024 + nb * 512:dh * 1024 + (nb + 1) * 512],
                                        start=(ii == 0), stop=(ii == GE - 1)), reads=[bw, bv], writes=[bpo])
                            if eg == 0:
                                T.op("act", lambda e, po=po, tl=tl, dh=dh: e.activation(
                                    out=acc[:, tl, dh * 1024:(dh + 1) * 1024], in_=po[:], func=AF.Copy),
                                    reads=[bpo], writes=[b_acc[tl]])
                            else:
                                T.op("dve", lambda e, po=po, tl=tl, dh=dh: e.tensor_tensor(
                                    out=acc[:, tl, dh * 1024:(dh + 1) * 1024], in0=po[:],
                                    in1=acc[:, tl, dh * 1024:(dh + 1) * 1024], op=ALU.add),
                                    reads=[bpo, b_acc[tl]], writes=[b_acc[tl]])
                junk = Wg[0][:, 0:2, :]; b_junk = b_Wg[0]
                for tl in range(NT):
                    ti = half * NT + tl
                    x1 = x1t[0]; bx1 = b_x1t[0]
                    s_t = sth[tl % 2]; bs = b_sth[tl % 2]
                    o_t = acc[:, tl, :]; bo = b_acc[tl]
                    dma("sp", x1[:], x1_s[ti * 128:(ti + 1) * 128, :], writes=[bx1])
                    T.op("dve", lambda e, o_t=o_t: e.tensor_tensor(out=o_t, in0=o_t, in1=Grow[:, 1, :],
                         op=ALU.mult), reads=[bo, b_grow], writes=[bo])
                    T.op("pool", lambda e, o_t=o_t, x1=x1: e.tensor_tensor(out=o_t, in0=o_t, in1=x1[:], op=ALU.add),
                         reads=[bo, bx1], writes=[bo])
                    T.op("act", lambda e, o_t=o_t, s_t=s_t: e.activation(out=junk, in_=o_t, func=AF.Square,
                         accum_out=s_t[:, 0:1]), reads=[bo], writes=[b_junk, bs])
                    T.op("act", lambda e, s_t=s_t: e.activation(out=s_t[:, 1:2], in_=s_t[:, 0:1], func=AF.Sqrt,
                         scale=1.0 / D, bias=epsc[:, 0:1]), reads=[bs, b_eps], writes=[bs])
                    T.op("dve", lambda e, s_t=s_t: e.reciprocal(out=s_t[:, 2:3], in_=s_t[:, 1:2]), reads=[bs], writes=[bs])
                    T.op("dve", lambda e, o_t=o_t, s_t=s_t: e.scalar_tensor_tensor(out=o_t, in0=o_t, scalar=s_t[:, 2:3],
                         in1=gfr[:], op0=ALU.mult, op1=ALU.mult), reads=[bo, bs, b_gfr], writes=[bo])
                    dma("sp", y_d[ti * 128:(ti + 1) * 128, :], o_t, reads=[bo], final=True)
        T.emit()
        print("tracker: ops=%d waits=%d" % (T.n_ops, T.n_waits))
    return nc


_CONSTS = None


def _consts():
    global _CONSTS
    if _CONSTS is not None:
        return _CONSTS
    identf = np.eye(128, dtype=np.float32)
    R = np.zeros((128, 128), np.float32)
    for blk in (0, 64):
        for m in range(32):
            R[blk + m, blk + m + 32] = -1.0
            R[blk + 32 + m, blk + m] = 1.0
    rotT = np.ascontiguousarray(R.T)
    inv_freq = (10000.0 ** (-np.arange(32, dtype=np.float32) / 32)).astype(np.float32)
    tpos = np.arange(TL)
    row = (tpos // 64).astype(np.float32)
    col = (tpos % 64).astype(np.float32)
    ang_row = row[:, None] * inv_freq[None, :]
    ang_col = col[:, None] * inv_freq[None, :]
    ang = np.concatenate([ang_row, ang_row, ang_col, ang_col], axis=1).astype(np.float32)
    cosT = np.ascontiguousarray(np.cos(ang).T.astype(np.float32))
    sinT = np.ascontiguousarray(np.sin(ang).T.astype(np.float32))
    cc = np.arange(256)
    ph = 2.0 * np.pi * ((cc[:, None] * cc[None, :]) % 256) / 256.0
    sc = 1.0 / math.sqrt(256.0 * TL)
    ccsc = np.concatenate([np.cos(ph) * sc, np.sin(ph) * sc], axis=1).astype(np.float32)
    n = np.arange(TL)
    phl = 2.0 * np.pi * ((n[:, None] * n[None, :]) % TL) / TL
    CL = np.cos(phl).astype(ml_dtypes.bfloat16)
    nSL = (-np.sin(phl)).astype(ml_dtypes.bfloat16)
    iotaf = np.ascontiguousarray(np.broadcast_to(np.arange(128, dtype=np.float32)[None, :], (128, 128)))
    _CONSTS = dict(iotaf=iotaf, identf=identf, rotT=rotT, cosT=cosT, sinT=sinT, ccsc=ccsc, CL=CL, nSL=nSL)
    return _CONSTS


def make_in_maps(x, c, ctx, c_ctx, w_ada, b_ada, g_norm1, w_in, g_q, g_k, w_fourier, b_fourier,
                 w_out, g_norm2, w_query, sub_keys, u_experts, v_experts, g_final):
    f = lambda a: np.ascontiguousarray(np.asarray(a, dtype=np.float32))
    cs = _consts()
    w_ada0 = f(w_ada[0]); w_in0 = f(w_in[0]); w_out0 = f(w_out[0]); wq0 = f(w_query[0]); wf0 = f(w_fourier[0])
    skT = f(np.asarray(sub_keys[0]).reshape(16, 128, 128).transpose(0, 2, 1))
    u = np.asarray(u_experts[0], dtype=np.float32)
    uT = f(u.reshape(128, 128, 16, 128).transpose(0, 3, 2, 1).reshape(128, 128, 2048))
    v0 = f(v_experts[0])
    gfin = f(np.asarray(g_final).reshape(1, D))
    bada_row = f(np.stack([np.asarray(b_ada[0]), np.asarray(b_ada[0])], axis=0))

    def colz(vec):
        return np.asarray(vec, np.float32).reshape(-1, 128).T

    in_maps = []
    for b in range(8):
        cols = np.zeros((128, NCOLS), np.float32)
        cb = colz(c[b]); cx = colz(c_ctx)
        cols[:, C_CC:C_CC + 32:2] = cb
        cols[:, C_CC + 1:C_CC + 32:2] = cx
        cols[:, C_G1:C_G1 + 16] = colz(g_norm1[0])
        cols[:, C_G2:C_G2 + 16] = colz(g_norm2[0])
        cols[:, C_BADA:C_BADA + 96] = colz(b_ada[0])
        cols[:, C_BF:C_BF + 8] = colz(b_fourier[0])
        cols[:, C_GQ] = np.asarray(g_q[0], np.float32)
        cols[:, C_GK] = np.asarray(g_k[0], np.float32)
        m = dict(x=f(x[b]), ctx=f(ctx[b]), cols=cols, w_ada=w_ada0, w_in=w_in0, w_fourier=wf0, w_out=w_out0,
                 w_query=wq0, skT=skT, uT=uT, v_experts=v0, g_final=gfin, bada_row=bada_row)
        m.update(cs)
        in_maps.append(m)
    return in_maps


def kernel(**inputs):
    in_maps = make_in_maps(**inputs)
    nc = build()
    res = run_bass_kernel_spmd(nc, in_maps, core_ids=list(range(8)))
    out = np.stack([np.asarray(r["y"], dtype=np.float32) for r in res.results], axis=0)
    return out
```
